# Optimizing a Trainium2 kernel written in Bass

```python
import math
import jax, jax.numpy as jnp
from jax import lax
import numpy as np

D_MODEL = 2048
BATCH = 16
SEQ = 256
DEPTH = 2
DEC_BATCH = 2
DEC_SEQ = 1024
PAST_LEN = 512

GRID_W = 64
HEAD_DIM = 128
N_Q_HEADS = D_MODEL // 2 // HEAD_DIM
N_KV_HEADS = 2
Q_PER_KV = N_Q_HEADS // N_KV_HEADS
ATTN_W = N_Q_HEADS * HEAD_DIM
KV_W = N_KV_HEADS * HEAD_DIM
SSM_W = D_MODEL // 4
SSM_GROUP = 16
SSM_GROUPS = SSM_W // SSM_GROUP
SSM_STATE = 64
RET_HEADS = D_MODEL // 4 // HEAD_DIM
RET_W = RET_HEADS * HEAD_DIM
MIX_W = ATTN_W + SSM_W + RET_W
IN_SIZES = (ATTN_W, KV_W, KV_W, SSM_W, RET_W, RET_W, RET_W, RET_W)
IN_W = ATTN_W + 2 * KV_W + SSM_W + 4 * RET_W
D_FF = 256 * ((8 * D_MODEL // 3 + 255) // 256)
N_SUB = 3
Q_BLOCK = 128
RET_CHUNK = 128
ROPE_BASE = 10000.0
ROPE_FREQS = HEAD_DIM // 4
EPS = 1e-6

kernel_name = "hybrid_diffusion_trunk_step"


def rmsnorm(x, g):
    x32 = x.astype(jnp.float32)
    y = x32 * lax.rsqrt(jnp.mean(x32 * x32, axis=-1, keepdims=True) + EPS)
    return (y * g.astype(jnp.float32)).astype(x.dtype)


def grid_rope(length):
    rows = length // GRID_W
    r = jnp.repeat(jnp.arange(rows), GRID_W).astype(jnp.float32)
    col = jnp.tile(jnp.arange(GRID_W), rows).astype(jnp.float32)
    inv = ROPE_BASE ** (-jnp.arange(ROPE_FREQS, dtype=jnp.float32) / ROPE_FREQS)
    ang = jnp.stack([r, col], axis=-1)[:, :, None] * inv
    return jnp.cos(ang), jnp.sin(ang)


def apply_rope(x, rope):
    cos, sin = rope
    b, l, h, d = x.shape
    x4 = x.astype(jnp.float32).reshape(b, l, h, 2, 2, ROPE_FREQS)
    rot = jnp.stack([-x4[..., 1, :], x4[..., 0, :]], axis=-2)
    c4 = cos[None, :, None, :, None, :]
    s4 = sin[None, :, None, :, None, :]
    return (x4 * c4 + rot * s4).reshape(b, l, h, d).astype(x.dtype)


def adaln(cond, w, b):
    m = (jax.nn.silu(cond) @ w + b).reshape(cond.shape[0], N_SUB, 3, D_MODEL)
    return m[:, :, 0], m[:, :, 1], m[:, :, 2]


def swiglu(x, w_in, w_out):
    a, u = jnp.split(x @ w_in, 2, axis=-1)
    return (jax.nn.silu(a) * u) @ w_out


def block_attention(q, k, v):
    b, lq = q.shape[:2]
    nb = lq // Q_BLOCK
    qb = jnp.moveaxis(q.reshape(b, nb, Q_BLOCK, N_KV_HEADS, Q_PER_KV, HEAD_DIM), 1, 0)
    scale = HEAD_DIM ** -0.5

    def one(qc):
        s = jnp.einsum("bqkgd,bskd->bkgqs", qc, k, preferred_element_type=jnp.float32) * scale
        p = jax.nn.softmax(s, axis=-1).astype(v.dtype)
        return jnp.einsum("bkgqs,bskd->bqkgd", p, v)

    out = lax.map(one, qb)
    return jnp.moveaxis(out, 0, 1).reshape(b, lq, ATTN_W)


def _lin_rec(e1, e2):
    a1, b1 = e1
    a2, b2 = e2
    return a1 * a2, a2 * b1 + b2


def s5_scan(u, a_re, a_im, log_dt, b_re, b_im, c_re, c_im, h0):
    lam = lax.complex(a_re.astype(jnp.float32), a_im.astype(jnp.float32))
    lam_bar = jnp.exp(lam * jnp.exp(log_dt.astype(jnp.float32))[:, None])
    bmat = lax.complex(b_re.astype(jnp.float32), b_im.astype(jnp.float32))
    b_bar = ((lam_bar - 1.0) / lam)[..., None] * bmat
    bu = jnp.einsum("gph,blgh->blgp", b_bar, u.astype(jnp.complex64))
    bu = bu.at[:, 0].add(lam_bar[None] * h0)
    a = jnp.broadcast_to(lam_bar, bu.shape)
    _, xs = lax.associative_scan(_lin_rec, (a, bu), axis=1)
    cmat = lax.complex(c_re.astype(jnp.float32), c_im.astype(jnp.float32))
    y = jnp.real(jnp.einsum("ghp,blgp->blgh", cmat, xs))
    return y, xs[:, -1]


def s5_mixer(u, p, h0):
    b, l, _ = u.shape
    u32 = u.astype(jnp.float32)
    ug = u32.reshape(b, l, SSM_GROUPS, SSM_GROUP)

    def run(d, seq):
        return s5_scan(seq, p["ssm_a_re"][d], p["ssm_a_im"][d], p["ssm_log_dt"][d],
                       p["ssm_b_re"][d], p["ssm_b_im"][d], p["ssm_c_re"][d], p["ssm_c_im"][d], h0[:, d])

    y_f, h_f = run(0, ug)
    y_b, h_b = run(1, ug[:, ::-1])
    y = (y_f + y_b[:, ::-1]).reshape(b, l, SSM_W) + p["ssm_d"].astype(jnp.float32) * u32
    z = jax.nn.gelu(y).astype(u.dtype) @ p["w_ssm_glu"]
    return z[..., :SSM_W] * jax.nn.sigmoid(z[..., SSM_W:]), jnp.stack([h_f, h_b], axis=1)


def retention_scan(q, k, v, log_g, s0):
    b, l = q.shape[:2]
    n = l // RET_CHUNK

    def chunks(t):
        return jnp.moveaxis(t.reshape(b, n, RET_CHUNK, RET_HEADS, HEAD_DIM), 1, 0)

    idx = jnp.arange(RET_CHUNK, dtype=jnp.float32)
    diff = idx[:, None] - idx[None, :]
    inner = jnp.where(diff >= 0, jnp.exp(jnp.maximum(diff, 0.0)[None] * log_g[:, None, None]), 0.0)
    q_dec = jnp.exp((idx[:, None] + 1.0) * log_g[None])
    k_dec = jnp.exp((RET_CHUNK - 1.0 - idx)[:, None] * log_g[None])
    c_dec = jnp.exp(RET_CHUNK * log_g)

    def step(s, qkv):
        qc, kc, vc = qkv
        att = jnp.einsum("bihd,bjhd->bhij", qc, kc) * inner
        o = (jnp.einsum("bhij,bjhe->bihe", att, vc)
             + jnp.einsum("bihd,bhde->bihe", qc, s) * q_dec[None, :, :, None])
        s = s * c_dec[None, :, None, None] + jnp.einsum("bjhd,bjhe->bhde", kc * k_dec[None, :, :, None], vc)
        return s, o

    s_fin, o = lax.scan(step, s0, (chunks(q), chunks(k), chunks(v)))
    return jnp.moveaxis(o, 0, 1).reshape(b, l, RET_HEADS, HEAD_DIM), s_fin


def retention(q, k, v, g, decay_logit, s0):
    b, l = q.shape[:2]
    log_g = jax.nn.log_sigmoid(decay_logit.astype(jnp.float32))
    q32, k32, v32 = (t.astype(jnp.float32) for t in (q, k, v))
    o_f, s_f = retention_scan(q32, k32, v32, log_g[0], s0[:, 0])
    o_b, s_b = retention_scan(q32[:, ::-1], k32[:, ::-1], v32[:, ::-1], log_g[1], s0[:, 1])
    o = o_f + o_b[:, ::-1]
    o = o - jnp.mean(o, axis=-1, keepdims=True)
    o = o * lax.rsqrt(jnp.mean(o * o, axis=-1, keepdims=True) + EPS)
    out = o.reshape(b, l, RET_W) * jax.nn.silu(g.astype(jnp.float32))
    return out.astype(g.dtype), jnp.stack([s_f, s_b], axis=1)


def mixer(h, p, rope, ctx):
    b, l, _ = h.shape
    cuts = [int(i) for i in np.cumsum(IN_SIZES)[:-1]]
    qa, ka, va, us, qr, kr, vr, gr = jnp.split(h @ p["w_in"], cuts, axis=-1)
    qa = rmsnorm(qa.reshape(b, l, N_Q_HEADS, HEAD_DIM), p["q_norm_g"])
    ka = rmsnorm(ka.reshape(b, l, N_KV_HEADS, HEAD_DIM), p["k_norm_g"])
    va = va.reshape(b, l, N_KV_HEADS, HEAD_DIM)
    qr = qr.reshape(b, l, RET_HEADS, HEAD_DIM)
    kr = kr.reshape(b, l, RET_HEADS, HEAD_DIM) * HEAD_DIM ** -0.5
    vr = vr.reshape(b, l, RET_HEADS, HEAD_DIM)
    if ctx is None:
        k_all, v_all = ka, va
        h0 = jnp.zeros((b, 2, SSM_GROUPS, SSM_STATE), jnp.complex64)
        s0 = jnp.zeros((b, 2, RET_HEADS, HEAD_DIM, HEAD_DIM), jnp.float32)
    else:
        k_ctx, v_ctx, h0, s0 = ctx
        qa, ka, qr, kr = (apply_rope(t, rope) for t in (qa, ka, qr, kr))
        k_all = jnp.concatenate([k_ctx.astype(ka.dtype), ka], axis=1)
        v_all = jnp.concatenate([v_ctx.astype(va.dtype), va], axis=1)
    attn = block_attention(qa.reshape(b, l, N_KV_HEADS, Q_PER_KV, HEAD_DIM), k_all, v_all)
    ssm, h_fin = s5_mixer(us, p, h0)
    ret, s_fin = retention(qr, kr, vr, gr, p["ret_decay_logit"], s0)
    out = jnp.concatenate([attn, ssm.astype(attn.dtype), ret.astype(attn.dtype)], axis=-1) @ p["w_out"]
    return out, ((ka, va, h_fin, s_fin) if ctx is None else None)


def layer(x, mod, p, rope, ctx):
    shift, scale, gate = mod

    def pre(x, i):
        return rmsnorm(x, p["norm_g"][i]) * (1.0 + scale[:, i, None]) + shift[:, i, None]

    x = x + 0.5 * gate[:, 0, None] * swiglu(pre(x, 0), p["w_ffn_in"][0], p["w_ffn_out"][0])
    y, ctx_out = mixer(pre(x, 1), p, rope, ctx)
    x = x + gate[:, 1, None] * y
    x = x + 0.5 * gate[:, 2, None] * swiglu(pre(x, 2), p["w_ffn_in"][1], p["w_ffn_out"][1])
    return x, ctx_out


def setup_inputs(seed: int = 0) -> dict:
    key = jax.random.key(seed)
    ks = jax.random.split(key, 32)
    f32 = jnp.float32

    def nrm(k, shape, s):
        return jax.random.normal(k, shape, f32) * s

    a_im0 = jnp.pi * jnp.arange(SSM_STATE, dtype=f32)
    ret_logit0 = jnp.log(2.0 ** (5.0 + jnp.arange(RET_HEADS, dtype=f32)) - 1.0)
    return {
        "x_prompt": nrm(ks[0], (BATCH, SEQ, D_MODEL), 1.0),
        "x_sample": nrm(ks[1], (DEC_BATCH, DEC_SEQ, D_MODEL), 1.0),
        "cache_k": nrm(ks[2], (DEC_BATCH, DEPTH, PAST_LEN, N_KV_HEADS, HEAD_DIM), 1.0),
        "cache_v": nrm(ks[3], (DEC_BATCH, DEPTH, PAST_LEN, N_KV_HEADS, HEAD_DIM), 1.0),
        "state_ssm": nrm(ks[4], (DEC_BATCH, DEPTH, 2, SSM_GROUPS, SSM_STATE, 2), 0.1),
        "state_ret": nrm(ks[5], (DEC_BATCH, DEPTH, 2, RET_HEADS, HEAD_DIM, HEAD_DIM), 0.5),
        "c": nrm(ks[6], (DEC_BATCH, D_MODEL), 1.0),
        "c_ctx": nrm(ks[7], (D_MODEL,), 1.0),
        "w_mod": nrm(ks[8], (DEPTH, D_MODEL, N_SUB * 3 * D_MODEL), 0.5 * D_MODEL ** -0.5),
        "b_mod": nrm(ks[9], (DEPTH, N_SUB * 3 * D_MODEL), 0.02),
        "norm_g": 1.0 + nrm(ks[10], (DEPTH, N_SUB, D_MODEL), 0.02),
        "w_ffn_in": nrm(ks[11], (DEPTH, 2, D_MODEL, 2 * D_FF), D_MODEL ** -0.5),
        "w_ffn_out": nrm(ks[12], (DEPTH, 2, D_FF, D_MODEL), D_FF ** -0.5),
        "w_in": nrm(ks[13], (DEPTH, D_MODEL, IN_W), D_MODEL ** -0.5),
        "w_out": nrm(ks[14], (DEPTH, MIX_W, D_MODEL), MIX_W ** -0.5),
        "q_norm_g": 1.0 + nrm(ks[15], (DEPTH, HEAD_DIM), 0.02),
        "k_norm_g": 1.0 + nrm(ks[16], (DEPTH, HEAD_DIM), 0.02),
        "ssm_a_re": -0.5 + nrm(ks[17], (DEPTH, 2, SSM_GROUPS, SSM_STATE), 0.01),
        "ssm_a_im": a_im0 + nrm(ks[18], (DEPTH, 2, SSM_GROUPS, SSM_STATE), 0.01),
        "ssm_log_dt": jax.random.uniform(ks[19], (DEPTH, 2, SSM_GROUPS), f32, math.log(1e-3), math.log(1e-1)),
        "ssm_b_re": nrm(ks[20], (DEPTH, 2, SSM_GROUPS, SSM_STATE, SSM_GROUP), (2 * SSM_GROUP) ** -0.5),
        "ssm_b_im": nrm(ks[21], (DEPTH, 2, SSM_GROUPS, SSM_STATE, SSM_GROUP), (2 * SSM_GROUP) ** -0.5),
        "ssm_c_re": nrm(ks[22], (DEPTH, 2, SSM_GROUPS, SSM_GROUP, SSM_STATE), SSM_STATE ** -0.5),
        "ssm_c_im": nrm(ks[23], (DEPTH, 2, SSM_GROUPS, SSM_GROUP, SSM_STATE), SSM_STATE ** -0.5),
        "ssm_d": nrm(ks[24], (DEPTH, SSM_W), 1.0),
        "w_ssm_glu": nrm(ks[25], (DEPTH, SSM_W, 2 * SSM_W), SSM_W ** -0.5),
        "ret_decay_logit": ret_logit0 + nrm(ks[26], (DEPTH, 2, RET_HEADS), 0.01),
        "final_norm_g": 1.0 + nrm(ks[27], (D_MODEL,), 0.02),
    }


def reference(x_prompt, x_sample, cache_k, cache_v, state_ssm, state_ret, c, c_ctx,
              w_mod, b_mod, norm_g, w_ffn_in, w_ffn_out, w_in, w_out, q_norm_g, k_norm_g,
              ssm_a_re, ssm_a_im, ssm_log_dt, ssm_b_re, ssm_b_im, ssm_c_re, ssm_c_im, ssm_d,
              w_ssm_glu, ret_decay_logit, final_norm_g):
    rope = grid_rope(x_sample.shape[1])
    xp, xs = x_prompt, x_sample
    ks_, vs_, hs_, ss_ = [], [], [], []
    for l in range(DEPTH):
        p = {
            "norm_g": norm_g[l], "w_ffn_in": w_ffn_in[l], "w_ffn_out": w_ffn_out[l],
            "w_in": w_in[l], "w_out": w_out[l], "q_norm_g": q_norm_g[l], "k_norm_g": k_norm_g[l],
            "ssm_a_re": ssm_a_re[l], "ssm_a_im": ssm_a_im[l], "ssm_log_dt": ssm_log_dt[l],
            "ssm_b_re": ssm_b_re[l], "ssm_b_im": ssm_b_im[l], "ssm_c_re": ssm_c_re[l],
            "ssm_c_im": ssm_c_im[l], "ssm_d": ssm_d[l], "w_ssm_glu": w_ssm_glu[l],
            "ret_decay_logit": ret_decay_logit[l],
        }
        xp, (k_l, v_l, h_l, s_l) = layer(xp, adaln(c_ctx[None], w_mod[l], b_mod[l]), p, None, None)
        ks_.append(k_l)
        vs_.append(v_l)
        hs_.append(jnp.stack([jnp.real(h_l), jnp.imag(h_l)], axis=-1))
        ss_.append(s_l)
        st = state_ssm[:, l]
        ctx_l = (cache_k[:, l], cache_v[:, l],
                 lax.complex(st[..., 0].astype(jnp.float32), st[..., 1].astype(jnp.float32)),
                 state_ret[:, l].astype(jnp.float32))
        xs, _ = layer(xs, adaln(c, w_mod[l], b_mod[l]), p, rope, ctx_l)
    y_prompt = rmsnorm(xp, final_norm_g)
    y_sample = rmsnorm(xs, final_norm_g)
    new_cache_k = jnp.stack(ks_, axis=1)
    new_cache_v = jnp.stack(vs_, axis=1)
    new_state_ssm = jnp.stack(hs_, axis=1)
    new_state_ret = jnp.stack(ss_, axis=1)
    return (y_prompt, y_sample, new_cache_k, new_cache_v, new_state_ssm, new_state_ret)
```

```python
import numpy as np
import concourse.bass as bass
import concourse.mybir as mybir
from concourse.bass_utils import run_bass_kernel_spmd
from contextlib import ExitStack

F32 = mybir.dt.float32
BF16 = mybir.dt.bfloat16
AF = mybir.ActivationFunctionType
ALU = mybir.AluOpType
AX = mybir.AxisListType

D = 2048
KC = 16
DFF = 5632
HC = 44
NCORE = 8
DEPTH = 2
EPS = 1e-6
PAST = 512
PI_SAFE = 3.1415925


class Buf:
    __slots__ = ("w", "r", "name")

    def __init__(self, name=""):
        self.w = None
        self.r = {}
        self.name = name


class Eng:
    EPOCH = 12000

    def __init__(self, nc, name, h, ndma=0):
        self.nc, self.name, self.h = nc, name, h
        self.sems = [nc.alloc_semaphore(f"s_{name}_0")]
        self.ep, self.cnt = 0, 0
        self.waited = {}
        self.dsem = [nc.alloc_semaphore(f"d_{name}_{i}") for i in range(ndma)]
        self.dval = [0] * ndma
        self.di = 0

    def bump(self, inst):
        if self.cnt >= self.EPOCH:
            self.ep += 1
            self.cnt = 0
            self.sems.append(self.nc.alloc_semaphore(f"s_{self.name}_{self.ep}"))
        self.cnt += 1
        inst.then_inc(self.sems[self.ep], 1)
        return ("c", self.name, self.ep, self.cnt, self.sems[self.ep])


class TR:
    def __init__(self, nc):
        self.nc = nc
        self.E = {
            "pe": Eng(nc, "pe", nc.tensor),
            "act": Eng(nc, "act", nc.scalar),
            "dve": Eng(nc, "dve", nc.vector),
            "sp": Eng(nc, "sp", nc.sync, ndma=12),
            "pool": Eng(nc, "pool", nc.gpsimd, ndma=12),
        }
        self.out_toks = []

    def _wait(self, E, deps):
        for t in deps:
            if t is None:
                continue
            if t[0] == "c":
                _, en, ep, cnt, sem = t
                if en == "pe" and E.name == "pe":
                    continue
                key = ("c", en)
                cur = E.waited.get(key, (-1, -1))
                if (ep, cnt) <= cur:
                    continue
                E.h.wait_ge(sem, cnt)
                E.waited[key] = (ep, cnt)
            else:
                _, sid, val, sem = t
                key = ("d", sid)
                if E.waited.get(key, 0) >= val:
                    continue
                E.h.wait_ge(sem, val)
                E.waited[key] = val

    def _deps(self, R, W):
        deps = []
        for b in R:
            deps.append(b.w)
        for b in W:
            deps.append(b.w)
            deps.extend(b.r.values())
        return deps

    def _mark(self, tok, R, W):
        key = (tok[0], tok[1])
        for b in R:
            b.r[key] = tok
        for b in W:
            b.w = tok
            b.r = {}

    def op(self, en, emit, R=(), W=()):
        E = self.E[en]
        self._wait(E, self._deps(R, W))
        inst = emit(E.h)
        tok = E.bump(inst)
        self._mark(tok, R, W)
        return tok

    def grp(self, en, emits, R=(), W=()):
        E = self.E[en]
        self._wait(E, self._deps(R, W))
        inst = None
        for f in emits:
            inst = f(E.h)
        tok = E.bump(inst)
        self._mark(tok, R, W)
        return tok

    def dma(self, q, out, in_, R=(), W=(), is_out=False):
        E = self.E[q]
        self._wait(E, self._deps(R, W))
        i = E.di
        E.di = (E.di + 1) % len(E.dsem)
        sem, prev = E.dsem[i], E.dval[i]
        key = ("d", id(sem))
        if prev > 0 and E.waited.get(key, 0) < prev:
            E.h.wait_ge(sem, prev)
            E.waited[key] = prev
        E.h.dma_start(out=out, in_=in_).then_inc(sem, 16)
        E.dval[i] = prev + 16
        tok = ("d", id(sem), prev + 16, sem)
        self._mark(tok, R, W)
        if is_out:
            self.out_toks.append(tok)
        return tok

    def barrier(self):
        toks = []
        for en in ("pe", "act", "dve"):
            E = self.E[en]
            if E.cnt > 0 or E.ep > 0:
                toks.append(("c", en, E.ep, E.cnt, E.sems[E.ep]))
        E = self.E["sp"]
        for i, s in enumerate(E.dsem):
            if E.dval[i] > 0:
                toks.append(("d", id(s), E.dval[i], s))
        for en in ("pe", "act", "dve", "sp"):
            X = self.E[en]
            for t in toks:
                if t[0] == "c" and t[1] == en:
                    continue
                self._wait(X, [t])
        self.fence = toks

    def wait_fence(self, en):
        self._wait(self.E[en], getattr(self, "fence", []))

    def finish(self):
        E = self.E["sp"]
        self._wait(E, self.out_toks)
        for q in ("sp", "pool"):
            Q = self.E[q]
            for i, s in enumerate(Q.dsem):
                if Q.dval[i] > 0:
                    self._wait(E, [("d", id(s), Q.dval[i], s)])


class Prog:
    def __init__(self, nseq_p=2, L_p=256, with_sample=True, layers=(0, 1), do_ffn=True):
        self.nc = nc = bass.Bass("TRN2", target_bir_lowering=False)
        self.tr = TR(nc)
        self.with_sample = with_sample
        self.layers, self.do_ffn = list(layers), do_ffn
        self.TP = nseq_p * L_p
        self.TT = self.TP + (1024 if with_sample else 0)
        self.nseq_p, self.L_p = nseq_p, L_p
        di = self.din = {}

        def inp(name, shape, dt=F32):
            di[name] = nc.dram_tensor(name, list(shape), dt, kind="ExternalInput").ap()
            return di[name]

        inp("xin", [D, self.TT])
        inp("cvec", [128, KC, 2])
        inp("w_mod", [DEPTH, D, 9 * D])
        inp("b_modT", [DEPTH, 128, 144])
        inp("norm_gT", [DEPTH, 128, 48])
        inp("final_gT", [128, KC])
        if do_ffn:
            inp("w_ffn_in", [DEPTH, 2, D, 2 * DFF])
            inp("w_ffn_out", [DEPTH, 2, DFF, D])
        inp("w_in", [DEPTH, D, 4096])
        inp("w_out", [DEPTH, D, D])
        inp("qg_rep", [DEPTH, 128, 128])
        inp("kg_rep", [DEPTH, 128, 128])
        inp("ssm_abd", [DEPTH, 2, 128, 3, 16])
        inp("ssm_BT", [DEPTH, 2, 2, 128, 16 * 128])
        inp("ssm_CT", [DEPTH, 2, 2, 128, 16 * 128])
        inp("ssm_d_rep", [DEPTH, 128, 512])
        inp("w_glu", [DEPTH, 512, 1024])
        inp("dlog_rep", [DEPTH, 128, 8])
        for nm in ("c_ident", "c_J", "c_tpos", "c_diff", "c_lmask", "c_umask"):
            inp(nm, [128, 128])
        inp("c_ipos", [128, 1])
        if with_sample:
            inp("cache_k", [DEPTH, PAST, 256])
            inp("cache_v", [DEPTH, PAST, 256])
            inp("h0s", [DEPTH, 2, 128, 2, 16])
            inp("s0s", [DEPTH, 2, 4, 128, 128])
            inp("rope_c", [128, 8, 128])
            inp("rope_s", [128, 8, 128])
        do = self.dout = {}

        def outp(name, shape):
            do[name] = nc.dram_tensor(name, list(shape), F32, kind="ExternalOutput").ap()

        outp("y", [D, self.TT])
        outp("ck", [DEPTH, self.TP, 256])
        outp("cv", [DEPTH, self.TP, 256])
        outp("hs", [DEPTH, nseq_p, 2, 128, 32])
        outp("ss", [DEPTH, nseq_p, 2, 4, 128, 128])
        self.xs = nc.dram_tensor("xs_scr", [D, self.TT], F32, kind="Internal").ap()
        self.xsb = [Buf(f"xs{i}") for i in range(self.TT // 512)]

        A = nc.alloc_sbuf_tensor
        self.ident = A("ident", [128, 128], BF16)
        self.Jm = A("Jm", [128, 128], BF16)
        self.ones = A("ones", [128, 128], BF16)
        self.tpos = A("tpos", [128, 128], F32)
        self.ipos = A("ipos", [128, 1], F32)
        self.diff = A("diff", [128, 128], F32)
        self.lmask = A("lmask", [128, 128], F32)
        self.umask = A("umask", [128, 128], F32)
        self.cst = Buf("cst")
        self.NSLOT = 6
        self.ring = A("ring", [128, self.NSLOT, 4096], BF16)
        self.rbuf = [Buf(f"ring{i}") for i in range(self.NSLOT)]
        self.ri = 0
        self.ps = nc.alloc_psum_tensor("ps", [128, 8, 512], F32)
        self.pb = [Buf(f"psb{i}") for i in range(8)]
        self.modT = A("modT", [128, 144, 2], F32)
        self.bmod = A("bmod", [128, 144], F32)
        self.ngT = A("ngT", [128, 48], F32)
        self.fgT = A("fgT", [128, KC], F32)
        self.Amod = A("Amod", [128, 3, 2, KC], F32)
        self.gate = A("gate", [128, 3, 2, KC], F32)
        self.scb = A("scb", [128, KC, 2], BF16)
        self.cv32 = A("cv32", [128, KC, 2], F32)
        self.modb = Buf("mod")
        self.qg = A("qg", [128, 128], F32)
        if with_sample:
            self.ropec = A("ropec", [128, 8, 128], F32)
            self.ropes = A("ropes", [128, 8, 128], F32)
        self.kg = A("kg", [128, 128], F32)
        self.build()

    def load_f32_via_sp(self, dst_ap, src_ap, buf):
        return self.tr.dma("sp", dst_ap, src_ap, W=[buf])

    def wslot(self, src_ap, nk, ncols):
        s = self.ri
        self.ri = (self.ri + 1) % self.NSLOT
        dst = self.ring[:, s, 0:nk * ncols].rearrange("p (k n) -> p k n", n=ncols)
        self.tr.dma("pool", dst, src_ap, W=[self.rbuf[s]])
        return s

    def wap(self, s, k, ncols, c0, c1):
        return self.ring[:, s, k * ncols + c0:k * ncols + c1]

    def setup_consts(self):
        tr, nc, di = self.tr, self.nc, self.din
        with nc.sbuf_tensor("ctmp", [128, 2, 128], F32) as ctmp:
            tb = Buf("ctmp")
            tr.dma("sp", ctmp[:, 0, :], di["c_ident"], W=[tb])
            tr.dma("sp", ctmp[:, 1, :], di["c_J"], W=[tb])
            tr.op("dve", lambda e: e.tensor_copy(out=self.ident[:], in_=ctmp[:, 0, :]), R=[tb], W=[self.cst])
            tr.op("dve", lambda e: e.tensor_copy(out=self.Jm[:], in_=ctmp[:, 1, :]), R=[tb], W=[self.cst])
            tr.op("dve", lambda e: e.memset(self.ones[:], 1.0), W=[self.cst])
            tr.dma("sp", self.tpos[:], di["c_tpos"], W=[self.cst])
            tr.dma("sp", self.ipos[:], di["c_ipos"], W=[self.cst])
            tr.dma("sp", self.diff[:], di["c_diff"], W=[self.cst])
            tr.dma("sp", self.lmask[:], di["c_lmask"], W=[self.cst])
            tr.dma("sp", self.umask[:], di["c_umask"], W=[self.cst])
            tr.dma("sp", self.fgT[:], di["final_gT"], W=[self.cst])
            tr.dma("sp", self.cv32[:], di["cvec"], W=[self.cst])
            if self.with_sample:
                tr.dma("sp", self.ropec[:], di["rope_c"], W=[self.cst])
                tr.dma("sp", self.ropes[:], di["rope_s"], W=[self.cst])
            tr.op("act", lambda e: e.activation(out=self.scb[:], in_=self.cv32[:], func=AF.Silu),
                  R=[self.cst], W=[self.cst])
            tr.barrier()

    def adaln(self, l):
        tr, nc, di = self.tr, self.nc, self.din
        wm = di["w_mod"][l].rearrange("(kc p) n -> p kc n", p=128)
        tr.dma("sp", self.bmod[:], di["b_modT"][l], W=[self.modb])
        tr.dma("sp", self.ngT[:], di["norm_gT"][l], W=[self.modb])
        tr.dma("sp", self.qg[:], di["qg_rep"][l], W=[self.modb])
        tr.dma("sp", self.kg[:], di["kg_rep"][l], W=[self.modb])
        bank = 7
        pv = self.ps[:, bank, 0:288].rearrange("p (j c) -> p j c", c=2)
        for u in range(72):
            s = self.wslot(wm[:, :, u * 256:(u + 1) * 256], 16, 256)
            for jj in range(2):
                j = u * 2 + jj
                ems = []
                for kc in range(KC):
                    ems.append(lambda e, kc=kc, jj=jj, j=j, s=s: e.matmul(
                        pv[:, j, :], lhsT=self.wap(s, kc, 256, jj * 128, jj * 128 + 128),
                        rhs=self.scb[:, kc, :], start=(kc == 0), stop=(kc == KC - 1)))
                tr.grp("pe", ems, R=[self.rbuf[s], self.cst], W=[self.pb[bank]])
        for c in range(2):
            tr.op("dve", lambda e, c=c: e.tensor_tensor(out=self.modT[:, :, c], in0=pv[:, :, c],
                                                        in1=self.bmod[:], op=ALU.add),
                  R=[self.pb[bank], self.modb], W=[self.modb])
        for sub in range(3):
            for c in range(2):
                sc = self.modT[:, (sub * 3 + 1) * 16:(sub * 3 + 2) * 16, c]
                gt = self.modT[:, (sub * 3 + 2) * 16:(sub * 3 + 3) * 16, c]
                tr.op("dve", lambda e, sub=sub, c=c, sc=sc: e.scalar_tensor_tensor(
                    out=self.Amod[:, sub, c, :], in0=sc, scalar=1.0, in1=self.ngT[:, sub * 16:(sub + 1) * 16],
                    op0=ALU.add, op1=ALU.mult), R=[self.modb], W=[self.modb])
                tr.op("dve", lambda e, sub=sub, c=c, gt=gt: e.tensor_scalar(
                    out=self.gate[:, sub, c, :], in0=gt, scalar1=(1.0 if sub == 1 else 0.5), scalar2=None,
                    op0=ALU.mult), R=[self.modb], W=[self.modb])

    def shift_ap(self, sub, c, kc):
        j = (sub * 3 + 0) * 16 + kc
        return self.modT[:, j, c:c + 1]

    def norm_tile(self, xt, xb, n, out_t, ob, A_ap, shift_fn, work, wb):
        tr = self.tr
        bank = 6
        sq = work[:, 2, :].bitcast(BF16)
        for kc in range(KC):
            h = kc % 2
            tr.op("act", lambda e, kc=kc, h=h: e.activation(out=sq[:, h * 512:h * 512 + n], in_=xt[:, kc, 0:n],
                                                            func=AF.Square), R=[xb], W=[wb[2 + h]])
            tr.op("pe", lambda e, kc=kc, h=h: e.matmul(self.ps[:, bank, 0:n], lhsT=self.ones[:],
                                                       rhs=sq[:, h * 512:h * 512 + n], start=(kc == 0),
                                                       stop=(kc == KC - 1)), R=[wb[2 + h], self.cst], W=[self.pb[bank]])
        rstd = work[:, 0, 0:n]
        tr.op("act", lambda e: e.activation(out=rstd, in_=self.ps[:, bank, 0:n], func=AF.Sqrt,
                                            scale=1.0 / D, bias=self.epsb[:, 0:1]), R=[self.pb[bank], self.cst], W=[wb[0]])
        tr.op("dve", lambda e: e.reciprocal(out=rstd, in_=rstd), R=[wb[0]], W=[wb[0]])
        tmp = work[:, 1, 0:n]
        for kc in range(KC):
            tr.op("dve", lambda e, kc=kc: e.tensor_tensor(out=tmp, in0=xt[:, kc, 0:n], in1=rstd, op=ALU.mult),
                  R=[xb, wb[0]], W=[wb[1]])
            sh = shift_fn(kc) if shift_fn is not None else 0.0
            tr.op("act", lambda e, kc=kc, sh=sh: e.activation(out=out_t[:, kc, 0:n], in_=tmp, func=AF.Identity,
                                                              scale=A_ap(kc), bias=sh),
                  R=[wb[1], self.modb, self.cst], W=[ob])

    def ffn_tile(self, l, f, xt, xb, n, hT, hb, hid, hidb, gate_ap, work, wb):
        tr, di = self.tr, self.din
        win = di["w_ffn_in"][l, f].rearrange("(kc p) n -> p kc n", p=128)
        wout = di["w_ffn_out"][l, f].rearrange("(kc p) n -> p kc n", p=128)
        bi = 0
        for h2 in range(HC // 2):
            sa = self.wslot(win[:, :, h2 * 256:(h2 + 1) * 256], 16, 256)
            su = self.wslot(win[:, :, DFF + h2 * 256:DFF + (h2 + 1) * 256], 16, 256)
            for jj in range(2):
                hc = h2 * 2 + jj
                ba, bu = (bi % 2) * 2, (bi % 2) * 2 + 1
                bi += 1
                for (s, bk) in ((sa, ba), (su, bu)):
                    ems = [lambda e, kc=kc, s=s, bk=bk, jj=jj: e.matmul(
                        self.ps[:, bk, 0:n], lhsT=self.wap(s, kc, 256, jj * 128, jj * 128 + 128),
                        rhs=hT[:, kc, 0:n], start=(kc == 0), stop=(kc == KC - 1)) for kc in range(KC)]
                    tr.grp("pe", ems, R=[self.rbuf[s], hb], W=[self.pb[bk]])
                sl = work[:, bi % 2, 0:n]
                tr.op("act", lambda e, ba=ba, sl=sl: e.activation(out=sl, in_=self.ps[:, ba, 0:n], func=AF.Silu),
                      R=[self.pb[ba]], W=[wb[bi % 2]])
                tr.op("dve", lambda e, bu=bu, sl=sl, hc=hc: e.tensor_tensor(out=hid[:, hc, 0:n], in0=self.ps[:, bu, 0:n],
                                                                            in1=sl, op=ALU.mult),
                      R=[self.pb[bu], wb[bi % 2]], W=[hidb])
        for o2 in range(8):
            slots = [self.wslot(wout[:, q * 11:(q + 1) * 11, o2 * 256:(o2 + 1) * 256], 11, 256) for q in range(4)]
            for jj in range(2):
                oc = o2 * 2 + jj
                bk = 4 + (oc % 2)
                ems = []
                for q in range(4):
                    for k in range(11):
                        kk = q * 11 + k
                        ems.append(lambda e, q=q, k=k, kk=kk, jj=jj, bk=bk: e.matmul(
                            self.ps[:, bk, 0:n], lhsT=self.wap(slots[q], k, 256, jj * 128, jj * 128 + 128),
                            rhs=hid[:, kk, 0:n], start=(kk == 0), stop=(kk == HC - 1)))
                tr.grp("pe", ems, R=[self.rbuf[s] for s in slots] + [hidb], W=[self.pb[bk]])
                tr.op("dve", lambda e, oc=oc, bk=bk: e.scalar_tensor_tensor(
                    out=xt[:, oc, 0:n], in0=self.ps[:, bk, 0:n], scalar=gate_ap(oc), in1=xt[:, oc, 0:n],
                    op0=ALU.mult, op1=ALU.add), R=[self.pb[bk], self.modb, xb], W=[xb])

    def proj_tok(self, slots, hT, hb, t0, bank):
        tr = self.tr
        ems = [lambda e, kc=kc: e.matmul(self.ps[:, bank, :], lhsT=hT[:, kc, t0:t0 + 128],
                                         rhs=self.wap(slots[kc // 8], kc % 8, 512, 0, 512),
                                         start=(kc == 0), stop=(kc == KC - 1)) for kc in range(KC)]
        tr.grp("pe", ems, R=[self.rbuf[s] for s in slots] + [hb], W=[self.pb[bank]])

    def w_in_slots(self, l, c0):
        wv = self.din["w_in"][l].rearrange("(kc p) n -> p kc n", p=128)
        return [self.wslot(wv[:, h * 8:(h + 1) * 8, c0:c0 + 512], 8, 512) for h in range(2)]

    def transpose_to(self, src_tok, sb, ncol_blocks, dst_fn, db, bank, rev=False):
        tr = self.tr
        X = self.Jm if rev else self.ident
        ems = [lambda e, i=i: e.matmul(self.ps[:, bank, i * 128:(i + 1) * 128], lhsT=src_tok[:, i * 128:(i + 1) * 128],
                                       rhs=X[:], start=True, stop=True) for i in range(ncol_blocks)]
        tr.grp("pe", ems, R=[sb, self.cst], W=[self.pb[bank]])
        for i in range(ncol_blocks):
            tr.op("act", lambda e, i=i: e.activation(out=dst_fn(i), in_=self.ps[:, bank, i * 128:(i + 1) * 128],
                                                     func=AF.Copy), R=[self.pb[bank]], W=[db])

    def load_norm(self, xsrc, t0, n, xt, xb, hT, hb, sub, c, work, wb):
        self.tr.dma("sp", xt[:, :, 0:n], xsrc.rearrange("(kc p) t -> p kc t", p=128)[:, :, t0:t0 + n], R=[self.xsb[t0 // 512]], W=[xb])
        self.norm_tile(xt, xb, n, hT, hb, lambda kc: self.Amod[:, sub, c, kc:kc + 1],
                       lambda kc: self.shift_ap(sub, c, kc), work, wb)

    def s5_prep(self, l, d, S):
        tr, di = self.tr, self.din
        pb = S["pbuf"]
        abd = S["abd"]
        tr.dma("sp", abd[:], di["ssm_abd"][l, d], W=[pb])
        tr.wait_fence("pool")
        tr.dma("pool", S["BT"][:].rearrange("p (c n) -> p c n", c=2), di["ssm_BT"][l, d].rearrange("c p n -> p c n"), W=[pb])
        sm = S["sm"]
        V = lambda i: sm[:, i, :]
        a_re, a_im, ldt = abd[:, 0, :], abd[:, 1, :], abd[:, 2, :]
        dt, lr, th, r, cth, sth, kre, kim = V(0), V(1), V(2), V(3), V(4), V(5), V(6), V(7)
        t1, t2, t3, den = V(8), V(9), V(10), V(11)
        c128, s128, kcl, ksl = V(12), V(13), V(14), V(15)
        thr = V(16)
        O = lambda f, **kw: tr.op("dve", f, R=[pb], W=[pb], **kw)
        Aop = lambda f: tr.op("act", f, R=[pb, self.cst], W=[pb])
        TWO_PI = 2.0 * np.pi
        MAGIC = 12582912.0

        def sincos(th_ap, s_out, c_out, shape_tmp1, shape_tmp2):
            O(lambda e: e.tensor_scalar(out=shape_tmp1, in0=th_ap, scalar1=1.0 / TWO_PI, scalar2=MAGIC, op0=ALU.mult, op1=ALU.add))
            O(lambda e: e.tensor_scalar(out=shape_tmp1, in0=shape_tmp1, scalar1=MAGIC, scalar2=-TWO_PI, op0=ALU.subtract, op1=ALU.mult))
            O(lambda e: e.tensor_tensor(out=shape_tmp1, in0=shape_tmp1, in1=th_ap, op=ALU.add))
            O(lambda e: e.tensor_scalar(out=shape_tmp1, in0=shape_tmp1, scalar1=PI_SAFE, scalar2=-PI_SAFE, op0=ALU.min, op1=ALU.max))
            Aop(lambda e: e.activation(out=s_out, in_=shape_tmp1, func=AF.Sin))
            O(lambda e: e.tensor_scalar(out=shape_tmp2, in0=shape_tmp1, scalar1=np.pi / 2, scalar2=None, op0=ALU.add))
            O(lambda e: e.tensor_scalar(out=shape_tmp1, in0=shape_tmp2, scalar1=np.pi, scalar2=-TWO_PI, op0=ALU.is_gt, op1=ALU.mult))
            O(lambda e: e.tensor_tensor(out=shape_tmp2, in0=shape_tmp2, in1=shape_tmp1, op=ALU.add))
            O(lambda e: e.tensor_scalar(out=shape_tmp2, in0=shape_tmp2, scalar1=PI_SAFE, scalar2=-PI_SAFE, op0=ALU.min, op1=ALU.max))
            Aop(lambda e: e.activation(out=c_out, in_=shape_tmp2, func=AF.Sin))

        Aop(lambda e: e.activation(out=dt, in_=ldt, func=AF.Exp))
        O(lambda e: e.tensor_tensor(out=lr, in0=a_re, in1=dt, op=ALU.mult))
        O(lambda e: e.tensor_tensor(out=th, in0=a_im, in1=dt, op=ALU.mult))
        Aop(lambda e: e.activation(out=r, in_=lr, func=AF.Exp))
        sincos(th, sth, cth, t1, t2)
        O(lambda e: e.tensor_tensor(out=t1, in0=r, in1=cth, op=ALU.mult))
        O(lambda e: e.tensor_scalar(out=t1, in0=t1, scalar1=-1.0, scalar2=None, op0=ALU.add))
        O(lambda e: e.tensor_tensor(out=t2, in0=r, in1=sth, op=ALU.mult))
        O(lambda e: e.tensor_tensor(out=den, in0=a_re, in1=a_re, op=ALU.mult))
        O(lambda e: e.tensor_tensor(out=t3, in0=a_im, in1=a_im, op=ALU.mult))
        O(lambda e: e.tensor_tensor(out=den, in0=den, in1=t3, op=ALU.add))
        O(lambda e: e.reciprocal(out=den, in_=den))
        O(lambda e: e.tensor_tensor(out=kre, in0=t1, in1=a_re, op=ALU.mult))
        O(lambda e: e.tensor_tensor(out=t3, in0=t2, in1=a_im, op=ALU.mult))
        O(lambda e: e.tensor_tensor(out=kre, in0=kre, in1=t3, op=ALU.add))
        O(lambda e: e.tensor_tensor(out=kre, in0=kre, in1=den, op=ALU.mult))
        O(lambda e: e.tensor_tensor(out=kim, in0=t2, in1=a_re, op=ALU.mult))
        O(lambda e: e.tensor_tensor(out=t3, in0=t1, in1=a_im, op=ALU.mult))
        O(lambda e: e.tensor_tensor(out=kim, in0=kim, in1=t3, op=ALU.subtract))
        O(lambda e: e.tensor_tensor(out=kim, in0=kim, in1=den, op=ALU.mult))
        cosT, sinT, rtab = S["cosT"], S["sinT"], S["rtab"]
        tA, tB = S["tA"], S["tB"]
        for j in range(16):
            O(lambda e, j=j: e.tensor_scalar(out=tB[:, j * 128:(j + 1) * 128], in0=self.tpos[:], scalar1=th[:, j:j + 1],
                                             scalar2=None, op0=ALU.mult))
            O(lambda e, j=j: e.tensor_scalar(out=rtab[:, j * 128:(j + 1) * 128], in0=self.lmask[:, 0:128], scalar1=0.0,
                                             scalar2=r[:, j:j + 1], op0=ALU.mult, op1=ALU.add))
            O(lambda e, j=j: e.memset(rtab[:, j * 128:j * 128 + 1], 0.0))
        sincos(tB[:], sinT[:], cosT[:], tA[:], tB[:])
        O(lambda e: e.tensor_scalar(out=thr, in0=th, scalar1=128.0, scalar2=None, op0=ALU.mult))
        sincos(thr, s128, c128, t1, t2)
        O(lambda e: e.tensor_copy(out=kcl, in_=cosT[:].rearrange("p (j t) -> p j t", t=128)[:, :, 127]))
        O(lambda e: e.tensor_copy(out=ksl, in_=sinT[:].rearrange("p (j t) -> p j t", t=128)[:, :, 127]))
        O(lambda e: e.tensor_tensor(out=t1, in0=kre, in1=kcl, op=ALU.mult))
        O(lambda e: e.tensor_tensor(out=t2, in0=kim, in1=ksl, op=ALU.mult))
        O(lambda e: e.tensor_tensor(out=t3, in0=kre, in1=ksl, op=ALU.mult))
        O(lambda e: e.tensor_tensor(out=ksl, in0=kim, in1=kcl, op=ALU.mult))
        O(lambda e: e.tensor_tensor(out=kcl, in0=t1, in1=t2, op=ALU.subtract))
        O(lambda e: e.tensor_tensor(out=ksl, in0=ksl, in1=t3, op=ALU.add))
        nkim = V(17)
        O(lambda e: e.tensor_scalar(out=nkim, in0=kim, scalar1=-1.0, scalar2=None, op0=ALU.mult))
        CT32, CTb = S["CT32"], S["CTb"]
        tr.dma("sp", CT32[:].rearrange("p (c n) -> p c n", c=2), di["ssm_CT"][l, d].rearrange("c p n -> p c n"), R=[pb], W=[pb])
        for j in range(16):
            cre = CT32[:, j * 128:(j + 1) * 128]
            cim = CT32[:, 2048 + j * 128:2048 + (j + 1) * 128]
            u1, u2 = S["u12"][:, 0:128], S["u12"][:, 128:256]
            O(lambda e, j=j, cim=cim: e.tensor_scalar(out=u1, in0=cim, scalar1=kim[:, j:j + 1], scalar2=None, op0=ALU.mult))
            O(lambda e, j=j, cre=cre: e.scalar_tensor_tensor(out=CTb[:, j * 128:(j + 1) * 128], in0=cre, scalar=kre[:, j:j + 1],
                                                             in1=u1, op0=ALU.mult, op1=ALU.subtract))
            O(lambda e, j=j, cim=cim: e.tensor_scalar(out=u2, in0=cim, scalar1=kre[:, j:j + 1], scalar2=-1.0, op0=ALU.mult, op1=ALU.mult))
            O(lambda e, j=j, cre=cre: e.scalar_tensor_tensor(out=CTb[:, 2048 + j * 128:2048 + (j + 1) * 128], in0=cre,
                                                             scalar=nkim[:, j:j + 1], in1=u2, op0=ALU.mult, op1=ALU.add))
            O(lambda e, j=j: e.tensor_scalar(out=CTb[:, 4096 + j * 128:4096 + (j + 1) * 128], in0=CTb[:, j * 128:(j + 1) * 128],
                                             scalar1=-1.0, scalar2=None, op0=ALU.mult))

    def uniq(self):
        self._uid = getattr(self, "_uid", 0) + 1
        return self._uid

    def alloc(self, es, name, shape, dt):
        return es.enter_context(self.nc.sbuf_tensor(f"{name}_{self.uniq()}", list(shape), dt))

    def proj_phase(self, l, G, cols, consume, pre=None):
        nc, tr = self.nc, self.tr
        T = G["nseq"] * G["L"]
        with ExitStack() as es:
            xt = self.alloc(es, "xtm", [128, KC, 512], F32)
            hT = self.alloc(es, "hTm", [128, KC, 512], BF16)
            work = self.alloc(es, "wkm", [128, 3, 512], F32)
            xb, hb = Buf(), Buf()
            wb = [Buf() for _ in range(4)]
            if pre is not None:
                pre()
            bi = 0
            for ti in range(T // 512):
                self.load_norm(self.xs, G["t0"] + ti * 512, 512, xt, xb, hT, hb, 1, G["c"], work, wb)
                for cb, c0 in enumerate(cols):
                    slots = self.w_in_slots(l, c0)
                    for tt in range(4):
                        bank = bi % 4
                        bi += 1
                        self.proj_tok(slots, hT, hb, tt * 128, bank)
                        consume(cb, ti * 4 + tt, bank)
            tr.barrier()

    def rope_tok(self, x_ap, xbuf, nh, ttg, tmp_ap, tbuf):
        tr = self.tr
        cs = self.ropec[:, ttg, :]
        sn = self.ropes[:, ttg, :]
        for h in range(nh):
            xh = x_ap[:, h * 128:(h + 1) * 128]
            xv = xh.rearrange("p (a b f) -> p a b f", a=2, b=2)
            tv = tmp_ap[:, 0:128].rearrange("p (a b f) -> p a b f", a=2, b=2)
            sv = sn.rearrange("p (a b f) -> p a b f", a=2, b=2)
            tr.op("dve", lambda e, xv=xv, tv=tv, sv=sv: e.tensor_tensor(out=tv[:, :, 0, :], in0=xv[:, :, 1, :], in1=sv[:, :, 0, :], op=ALU.mult),
                  R=[xbuf, self.cst], W=[tbuf])
            tr.op("dve", lambda e, xv=xv, tv=tv, sv=sv: e.tensor_tensor(out=tv[:, :, 1, :], in0=xv[:, :, 0, :], in1=sv[:, :, 1, :], op=ALU.mult),
                  R=[xbuf, self.cst], W=[tbuf])
            tr.op("dve", lambda e, xh=xh: e.tensor_tensor(out=xh, in0=xh, in1=cs, op=ALU.mult), R=[self.cst, tbuf], W=[xbuf])
            tr.op("dve", lambda e, xh=xh: e.tensor_tensor(out=xh, in0=xh, in1=tmp_ap[:, 0:128], op=ALU.add), R=[tbuf], W=[xbuf])

    def mix_s5(self, l, G, mixT, mb):
        nc, tr, di = self.nc, self.tr, self.din
        nseq, L = G["nseq"], G["L"]
        T = nseq * L
        NTT, nch = T // 128, L // 128
        with ExitStack() as es0:
            u_tok = self.alloc(es0, "utok", [128, NTT, 512], BF16)
            ub = Buf()
            self.proj_phase(l, G, [1536], lambda cb, ttg, bank: tr.op(
                "act", lambda e: e.activation(out=u_tok[:, ttg, :], in_=self.ps[:, bank, :], func=AF.Copy),
                R=[self.pb[bank]], W=[ub]))
            with ExitStack() as es:
                A = lambda n, sh, dt: self.alloc(es, n, sh, dt)
                cosT, sinT, rtab = A("cosT", [128, 2048], F32), A("sinT", [128, 2048], F32), A("rtab", [128, 2048], F32)
                W5 = A("W5", [128, 5120], F32)
                BT, CTb = A("BT", [128, 4096], BF16), A("CTb", [128, 6144], BF16)
                _xa, _xb, _xc, _xd = A("xpa", [128, 1024], BF16), A("xpb", [128, 1024], BF16), A("xpc", [128, 1024], BF16), A("xpd", [128, 1024], BF16)
                xre2, xim2, xre_b, xim_b = [_xa, _xa], [_xb, _xb], [_xc, _xc], [_xd, _xd]
                _xbuf = Buf()
                xbf2 = [_xbuf, _xbuf]
                uT2 = [A("uT", [128, 512], BF16), A("uT", [128, 512], BF16)]
                uTb2 = [Buf(), Buf()]
                ybf = Buf()
                ytokf = A("ytokf", [128, NTT, 512], BF16)
                ytb = A("ytb", [128, 512], BF16)
                gT = A("gT", [128, 4, T], BF16)
                gtok = A("gtok", [128, 512], BF16)
                abd, sm = A("abd", [128, 3, 16], F32), A("sm", [128, 24, 16], F32)
                car = A("car", [128, 4, 16], F32)
                hfin = A("hfin", [128, 16, 2], F32)
                drep = A("drep", [128, 512], F32)
                h0t = A("h0t", [128, 2, 16], F32)
                pbuf, wbf, xbf, uTb, yfb, ytbb, gTb, gtb, carb, hfb, drb = [Buf() for _ in range(11)]
                tr.dma("sp", drep[:], di["ssm_d_rep"][l], W=[drb])
                S = dict(pbuf=pbuf, abd=abd, sm=sm, BT=BT, CTb=CTb, cosT=cosT, sinT=sinT, rtab=rtab,
                         tA=W5[:, 0:2048], tB=W5[:, 2048:4096], CT32=W5[:, 0:4096], u12=W5[:, 4096:4352])
                V = lambda i: sm[:, i, :]
                r_, cth, sth, kre, kim = V(3), V(4), V(5), V(6), V(7)
                c128, s128, krl_re, krl_im = V(12), V(13), V(14), V(15)
                D_re, D_im, W_re, W_im, T1 = [W5[:, i * 1024:(i + 1) * 1024] for i in range(5)]
                rcr, rci = car[:, 0, :], car[:, 1, :]
                YB = T1
                O = lambda f, R, W: tr.op("dve", f, R=R, W=W)
                TT = lambda e, o, a, b, op: e.tensor_tensor(out=o, in0=a, in1=b, op=op)
                for d in range(2):
                    tr.barrier()
                    self.s5_prep(l, d, S)
                    for s in range(nseq):
                        if G["sample"]:
                            tr.dma("sp", h0t[:], di["h0s"][l, d], W=[carb])
                            q1, q2, q3, q4 = V(18), V(19), V(20), V(21)
                            h_re, h_im = h0t[:, 0, :], h0t[:, 1, :]
                            R_, W_ = [pbuf, carb], [pbuf, carb]
                            O(lambda e: TT(e, q1, kre, kre, ALU.mult), R_, W_)
                            O(lambda e: TT(e, q2, kim, kim, ALU.mult), R_, W_)
                            O(lambda e: TT(e, q1, q1, q2, ALU.add), R_, W_)
                            O(lambda e: e.reciprocal(out=q1, in_=q1), R_, W_)
                            O(lambda e: TT(e, q2, h_re, kre, ALU.mult), R_, W_)
                            O(lambda e: TT(e, q3, h_im, kim, ALU.mult), R_, W_)
                            O(lambda e: TT(e, q2, q2, q3, ALU.add), R_, W_)
                            O(lambda e: TT(e, q2, q2, q1, ALU.mult), R_, W_)
                            O(lambda e: TT(e, q3, h_im, kre, ALU.mult), R_, W_)
                            O(lambda e: TT(e, q4, h_re, kim, ALU.mult), R_, W_)
                            O(lambda e: TT(e, q3, q3, q4, ALU.subtract), R_, W_)
                            O(lambda e: TT(e, q3, q3, q1, ALU.mult), R_, W_)
                            O(lambda e: TT(e, q1, r_, cth, ALU.mult), R_, W_)
                            O(lambda e: TT(e, q4, r_, sth, ALU.mult), R_, W_)
                            O(lambda e: TT(e, rcr, q1, q2, ALU.mult), R_, W_)
                            O(lambda e: TT(e, rci, q4, q3, ALU.mult), R_, W_)
                            O(lambda e: TT(e, rcr, rcr, rci, ALU.subtract), R_, W_)
                            O(lambda e: TT(e, rci, q1, q3, ALU.mult), R_, W_)
                            O(lambda e: TT(e, q1, q4, q2, ALU.mult), R_, W_)
                            O(lambda e: TT(e, rci, rci, q1, ALU.add), R_, W_)
                        else:
                            O(lambda e: e.memset(car[:, 0:2, :], 0.0), [], [carb])
                        order = list(range(nch)) if d == 0 else list(range(nch - 1, -1, -1))
                        items = [(oi, ci, hh) for oi, ci in enumerate(order) for hh in range(2)]
                        v3 = lambda ap: ap.rearrange("p (a b) -> p a b", b=512)
                        j3 = lambda ap: ap.rearrange("p (j t) -> p j t", t=128)

                        def stage_bu(item):
                            oi, ci, hh = item
                            ttg = s * nch + ci
                            uTc = uT2[oi % 2]
                            if hh == 0:
                                self.transpose_to(u_tok[:, ttg, :], ub, 4, lambda i: uTc[:, i * 128:(i + 1) * 128], uTb2[oi % 2], 0, rev=(d == 1))
                            ems = []
                            for c2 in range(2):
                                for jj in range(8):
                                    j = 8 * hh + jj
                                    ems.append(lambda e, c2=c2, jj=jj, j=j: e.matmul(
                                        self.ps[:, 1 + 2 * c2 + jj // 4, (jj % 4) * 128:(jj % 4 + 1) * 128],
                                        lhsT=BT[:, c2 * 2048 + j * 128:c2 * 2048 + (j + 1) * 128],
                                        rhs=uTc[:, (j // 4) * 128:(j // 4 + 1) * 128], start=True, stop=True))
                            tr.grp("pe", ems, R=[pbuf, uTb2[oi % 2]], W=[self.pb[1], self.pb[2], self.pb[3], self.pb[4]])

                        def stage_derot(item):
                            oi, ci, hh = item
                            bre, bim = self.ps[:, 1:3, :], self.ps[:, 3:5, :]
                            cs = v3(cosT[:, hh * 1024:(hh + 1) * 1024])
                            sn = v3(sinT[:, hh * 1024:(hh + 1) * 1024])
                            Rp = [self.pb[1], self.pb[2], self.pb[3], self.pb[4], pbuf, wbf]
                            O(lambda e: TT(e, v3(T1), bre, cs, ALU.mult), Rp, [wbf])
                            O(lambda e: TT(e, v3(D_re), bim, sn, ALU.mult), Rp, [wbf])
                            O(lambda e: TT(e, v3(D_im), bre, sn, ALU.mult), Rp, [wbf])
                            O(lambda e: TT(e, v3(W_im), bim, cs, ALU.mult), Rp, [wbf])
                            O(lambda e: TT(e, D_re, D_re, T1, ALU.add), [wbf], [wbf])
                            O(lambda e: TT(e, D_im, W_im, D_im, ALU.subtract), [wbf], [wbf])
                            O(lambda e: TT(e, j3(D_re)[:, :, 0], j3(D_re)[:, :, 0], rcr[:, 8 * hh:8 * hh + 8], ALU.add), [wbf, carb], [wbf])
                            O(lambda e: TT(e, j3(D_im)[:, :, 0], j3(D_im)[:, :, 0], rci[:, 8 * hh:8 * hh + 8], ALU.add), [wbf, carb], [wbf])

                        def stage_rest(item):
                            oi, ci, hh = item
                            xre, xim, xbf = xre2[hh], xim2[hh], xbf2[hh]
                            rt = rtab[:, hh * 1024:(hh + 1) * 1024]
                            O(lambda e: e.tensor_tensor_scan(out=W_re, data0=rt, data1=D_re, initial=0.0, op0=ALU.mult, op1=ALU.add), [wbf, pbuf], [wbf])
                            O(lambda e: e.tensor_tensor_scan(out=W_im, data0=rt, data1=D_im, initial=0.0, op0=ALU.mult, op1=ALU.add), [wbf, pbuf], [wbf])
                            lr_, li_ = j3(W_re)[:, :, 127], j3(W_im)[:, :, 127]
                            hs_ = slice(8 * hh, 8 * hh + 8)
                            a1, a2 = car[:, 2, hs_], car[:, 3, hs_]
                            Rc, Wc = [wbf, carb, pbuf], [carb]
                            if oi == nch - 1 and not G["sample"]:
                                hv = hfin[:, hs_, :]
                                O(lambda e: TT(e, a1, lr_, krl_re[:, hs_], ALU.mult), Rc, Wc)
                                O(lambda e: TT(e, a2, li_, krl_im[:, hs_], ALU.mult), Rc, Wc)
                                O(lambda e: TT(e, hv[:, :, 0], a1, a2, ALU.subtract), Rc, [hfb])
                                O(lambda e: TT(e, a1, li_, krl_re[:, hs_], ALU.mult), Rc, Wc)
                                O(lambda e: TT(e, a2, lr_, krl_im[:, hs_], ALU.mult), Rc, Wc)
                                O(lambda e: TT(e, hv[:, :, 1], a1, a2, ALU.add), Rc, [hfb])
                            if oi < nch - 1:
                                O(lambda e: TT(e, a1, lr_, c128[:, hs_], ALU.mult), Rc, Wc)
                                O(lambda e: TT(e, a2, li_, s128[:, hs_], ALU.mult), Rc, Wc)
                                O(lambda e: TT(e, rcr[:, hs_], a1, a2, ALU.subtract), Rc, Wc)
                                O(lambda e: TT(e, a1, lr_, s128[:, hs_], ALU.mult), Rc, Wc)
                                O(lambda e: TT(e, a2, li_, c128[:, hs_], ALU.mult), Rc, Wc)
                                O(lambda e: TT(e, rci[:, hs_], a1, a2, ALU.add), Rc, Wc)
                                O(lambda e: TT(e, rcr[:, hs_], rcr[:, hs_], r_[:, hs_], ALU.mult), Rc, Wc)
                                O(lambda e: TT(e, rci[:, hs_], rci[:, hs_], r_[:, hs_], ALU.mult), Rc, Wc)
                            csf, snf = cosT[:, hh * 1024:(hh + 1) * 1024], sinT[:, hh * 1024:(hh + 1) * 1024]
                            pA, pB = xre[:], xim[:]
                            pC, pD = xre_b[hh][:], xim_b[hh][:]
                            O(lambda e: TT(e, pA, W_re, csf, ALU.mult), [wbf, pbuf], [xbf])
                            O(lambda e: TT(e, pB, W_im, snf, ALU.mult), [wbf, pbuf], [xbf])
                            O(lambda e: TT(e, pC, W_re, snf, ALU.mult), [wbf, pbuf], [xbf])
                            O(lambda e: TT(e, pD, W_im, csf, ALU.mult), [wbf, pbuf], [xbf])
                            ems = []
                            for cc in range(2):
                                c = 2 * hh + cc
                                for q in range(4):
                                    jj = 4 * cc + q
                                    j = 8 * hh + jj
                                    for n_, (tb, xs_) in enumerate(((0, xre), (2, xim), (1, xre_b[hh]), (1, xim_b[hh]))):
                                        ems.append(lambda e, c=c, q=q, jj=jj, j=j, tb=tb, xs_=xs_, n_=n_: e.matmul(
                                            self.ps[:, 5, c * 128:(c + 1) * 128], lhsT=xs_[:, jj * 128:(jj + 1) * 128],
                                            rhs=CTb[:, tb * 2048 + j * 128:tb * 2048 + (j + 1) * 128],
                                            start=(q == 0 and n_ == 0), stop=(q == 3 and n_ == 3)))
                            tr.grp("pe", ems, R=[xbf, pbuf], W=[self.pb[5]])
                            if hh == 1:
                                ttg = s * nch + ci
                                if d == 0:
                                    tr.op("act", lambda e: e.activation(out=ytokf[:, ttg, :], in_=self.ps[:, 5, :], func=AF.Copy),
                                          R=[self.pb[5]], W=[yfb])
                                else:
                                    tr.op("act", lambda e: e.activation(out=ytb[:], in_=self.ps[:, 5, :], func=AF.Copy), R=[self.pb[5]], W=[ytbb])
                                    tr.grp("pe", [lambda e: e.matmul(self.ps[:, 6, :], lhsT=self.ident[:], rhs=ytokf[:, ttg, :], start=True, stop=False),
                                                  lambda e: e.matmul(self.ps[:, 6, :], lhsT=self.Jm[:], rhs=ytb[:], start=False, stop=True)],
                                           R=[yfb, ytbb, self.cst], W=[self.pb[6]])

                        def stage_final(item):
                            oi, ci, hh = item
                            if hh != 1 or d == 0:
                                return
                            ttg = s * nch + ci
                            Y, Y2 = YB[:, 0:512], YB[:, 512:1024]
                            O(lambda e: TT(e, Y, u_tok[:, ttg, :], drep[:], ALU.mult), [ub, drb, ybf], [ybf])
                            O(lambda e: TT(e, Y, Y, self.ps[:, 6, :], ALU.add), [ybf, self.pb[6]], [ybf])
                            O(lambda e: TT(e, Y2, Y, Y, ALU.mult), [ybf], [ybf])
                            O(lambda e: e.tensor_scalar(out=Y2, in0=Y2, scalar1=0.044715, scalar2=1.0, op0=ALU.mult, op1=ALU.add), [ybf], [ybf])
                            O(lambda e: TT(e, Y2, Y2, Y, ALU.mult), [ybf], [ybf])
                            tr.op("act", lambda e: e.activation(out=Y2, in_=Y2, func=AF.Sigmoid, scale=1.5957691216057308), R=[ybf], W=[ybf])
                            O(lambda e: TT(e, gtok[:], Y, Y2, ALU.mult), [ybf], [gtb])
                            self.transpose_to(gtok[:], gtb, 4, lambda i: gT[:, i, ttg * 128:(ttg + 1) * 128], gTb, 7)

                        stage_bu(items[0])
                        pend = None
                        for k, item in enumerate(items):
                            stage_derot(item)
                            if k + 1 < len(items):
                                stage_bu(items[k + 1])
                            if pend is not None:
                                stage_final(pend)
                                pend = None
                            stage_rest(item)
                            pend = item
                        stage_final(pend)
                        if not G["sample"]:
                            tr.dma("sp", self.dout["hs"][l, s, d], hfin[:].rearrange("p j c -> p (j c)"), R=[hfb], is_out=True)
                wg = di["w_glu"][l].rearrange("(kc p) n -> p kc n", p=128)
                gs = [self.wslot(wg[:, :, h * 512:(h + 1) * 512], 4, 512) for h in range(2)]
                for t0 in range(0, T, 512):
                    for oc in range(4):
                        for half, bank in ((0, oc % 2), (1, 2 + oc % 2)):
                            ems = [lambda e, kc=kc, half=half, bank=bank, oc=oc: e.matmul(
                                self.ps[:, bank, :], lhsT=self.wap(gs[half], kc, 512, oc * 128, (oc + 1) * 128),
                                rhs=gT[:, kc, t0:t0 + 512], start=(kc == 0), stop=(kc == 3)) for kc in range(4)]
                            tr.grp("pe", ems, R=[self.rbuf[gs[half]], gTb], W=[self.pb[bank]])
                        sg = W5[:, 0:512]
                        tr.op("act", lambda e, oc=oc: e.activation(out=sg, in_=self.ps[:, 2 + oc % 2, :], func=AF.Sigmoid), R=[self.pb[2 + oc % 2], wbf], W=[wbf])
                        O(lambda e, oc=oc: TT(e, mixT[:, 8 + oc, t0:t0 + 512], self.ps[:, oc % 2, :], sg, ALU.mult), [self.pb[oc % 2], wbf], [mb])
                tr.barrier()

    def headnorm(self, bank, nh, gtab, out_ap, obuf, sqt, ssm_, sbuf_):
        tr = self.tr
        tr.op("act", lambda e: e.activation(out=sqt[:, 0:nh * 128], in_=self.ps[:, bank, 0:nh * 128], func=AF.Square),
              R=[self.pb[bank]], W=[sbuf_])
        tr.op("dve", lambda e: e.tensor_reduce(out=ssm_[:, 0:nh], in_=sqt[:, 0:nh * 128].rearrange("p (h d) -> p h d", d=128),
                                               axis=AX.X, op=ALU.add), R=[sbuf_], W=[sbuf_])
        tr.op("act", lambda e: e.activation(out=ssm_[:, 0:nh], in_=ssm_[:, 0:nh], func=AF.Sqrt, scale=1.0 / 128, bias=self.epsb[:, 0:1]),
              R=[sbuf_, self.cst], W=[sbuf_])
        tr.op("dve", lambda e: e.reciprocal(out=ssm_[:, 0:nh], in_=ssm_[:, 0:nh]), R=[sbuf_], W=[sbuf_])
        for h in range(nh):
            tr.op("dve", lambda e, h=h: e.scalar_tensor_tensor(out=out_ap[:, h * 128:(h + 1) * 128], in0=self.ps[:, bank, h * 128:(h + 1) * 128],
                                                               scalar=ssm_[:, h:h + 1], in1=gtab[:], op0=ALU.mult, op1=ALU.mult),
                  R=[self.pb[bank], sbuf_, self.modb], W=[obuf])

    def mix_attn(self, l, G, mixT, mb):
        nc, tr, di = self.nc, self.tr, self.din
        nseq, L, smp = G["nseq"], G["L"], G["sample"]
        T = nseq * L
        NTT = T // 128
        Lk = L + (PAST if smp else 0)
        koff = PAST if smp else 0
        with ExitStack() as es0:
            A0 = lambda n, sh, dt: self.alloc(es0, n, sh, dt)
            qT = A0("qT", [128, 8, T], BF16)
            kT = A0("kT", [128, 2, nseq * Lk], BF16)
            v_tok = A0("vtok", [128, nseq * Lk // 128, 256], BF16)
            qTb, kTb, vb = Buf(), Buf(), Buf()
            with ExitStack() as es:
                A = lambda n, sh, dt: self.alloc(es, n, sh, dt)
                SETS = []
                for _ in range(2):
                    SETS.append(dict(q_tok=A("qtok", [128, 512], BF16), kst=A("kst", [128, 512], F32), kbf=A("kbf", [128, 256], BF16),
                                     sqt=A("sqt", [128, 512], F32), ssm_=A("ssms", [128, 8], F32), rtmp=A("rtmp", [128, 128], F32),
                                     qtb=Buf(), kstb=Buf(), kbb=Buf(), sqb=Buf(), rtb=Buf()))
                kst, kbf, kstb, kbb = SETS[0]["kst"], SETS[0]["kbf"], SETS[0]["kstb"], SETS[0]["kbb"]
                cnt = [0]

                def pre():
                    if not smp:
                        return
                    for i in range(PAST // 128):
                        tr.dma("sp", kst[:, 0:256], di["cache_k"][l, i * 128:(i + 1) * 128, :], W=[kstb])
                        tr.dma("sp", kst[:, 256:512], di["cache_v"][l, i * 128:(i + 1) * 128, :], W=[kstb])
                        tr.op("act", lambda e: e.activation(out=kbf[:], in_=kst[:, 0:256], func=AF.Copy), R=[kstb], W=[kbb])
                        tr.op("dve", lambda e, i=i: e.tensor_copy(out=v_tok[:, i, :], in_=kst[:, 256:512]), R=[kstb], W=[vb])
                        self.transpose_to(kbf[:], kbb, 2, lambda h, i=i: kT[:, h, i * 128:(i + 1) * 128], kTb, 5)

                def consume(cb, ttg, bank):
                    s_, tl = divmod(ttg * 128, L)
                    par = cnt[0] % 2
                    cnt[0] += 1
                    Z = SETS[par]
                    q_tok, kst, kbf, sqt, ssm_, rtmp = Z["q_tok"], Z["kst"], Z["kbf"], Z["sqt"], Z["ssm_"], Z["rtmp"]
                    qtb, kstb, kbb, sqb, rtb = Z["qtb"], Z["kstb"], Z["kbb"], Z["sqb"], Z["rtb"]
                    if cb < 2:
                        self.headnorm(bank, 4, self.qg, q_tok[:], qtb, sqt, ssm_, sqb)
                        if smp:
                            self.rope_tok(q_tok[:], qtb, 4, ttg, rtmp[:], rtb)
                        self.transpose_to(q_tok[:], qtb, 4, lambda h: qT[:, cb * 4 + h, ttg * 128:(ttg + 1) * 128], qTb, 4 + par)
                    else:
                        self.headnorm(bank, 2, self.kg, kst[:, 0:256], kstb, sqt, ssm_, sqb)
                        tr.op("act", lambda e: e.activation(out=kst[:, 256:512], in_=self.ps[:, bank, 256:512], func=AF.Copy),
                              R=[self.pb[bank]], W=[kstb])
                        if smp:
                            self.rope_tok(kst[:, 0:256], kstb, 2, ttg, rtmp[:], rtb)
                        else:
                            tr.dma("sp", self.dout["ck"][l, ttg * 128:(ttg + 1) * 128, :], kst[:, 0:256], R=[kstb], is_out=True)
                            tr.dma("sp", self.dout["cv"][l, ttg * 128:(ttg + 1) * 128, :], kst[:, 256:512], R=[kstb], is_out=True)
                        tr.op("act", lambda e: e.activation(out=kbf[:], in_=kst[:, 0:256], func=AF.Copy), R=[kstb], W=[kbb])
                        kc0 = s_ * Lk + koff + tl
                        tr.op("dve", lambda e: e.tensor_copy(out=v_tok[:, kc0 // 128, :], in_=kst[:, 256:512]), R=[kstb], W=[vb])
                        self.transpose_to(kbf[:], kbb, 2, lambda h: kT[:, h, kc0:kc0 + 128], kTb, 4 + par)

                self.proj_phase(l, G, [0, 512, 1024], consume, pre=pre)
            with ExitStack() as es:
                A = lambda n, sh, dt: self.alloc(es, n, sh, dt)
                PT = A("PT", [128, 2, 512], BF16)
                rs = A("rs", [128, 512], F32)
                ptb, rsb = [Buf(), Buf()], Buf()
                NQ = min(512, L)
                it = 0
                for s in range(nseq):
                    for h in range(8):
                        kvh = h // 4
                        for q0 in range(0, L, NQ):
                            qa = qT[:, h, s * L + q0:s * L + q0 + NQ]
                            bo, bs = 2 + 2 * (it % 2), 3 + 2 * (it % 2)
                            it += 1
                            nsc = Lk // 128
                            for sc in range(nsc):
                                k0 = s * Lk + sc * 128
                                bS = sc % 2
                                tr.op("pe", lambda e, k0=k0, bS=bS: e.matmul(self.ps[:, bS, 0:NQ], lhsT=kT[:, kvh, k0:k0 + 128], rhs=qa,
                                                                             start=True, stop=True), R=[kTb, qTb], W=[self.pb[bS]])
                                tr.op("act", lambda e, bS=bS: e.activation(out=PT[:, bS, 0:NQ], in_=self.ps[:, bS, 0:NQ], func=AF.Exp,
                                                                           scale=128.0 ** -0.5), R=[self.pb[bS]], W=[ptb[bS]])
                                tr.op("pe", lambda e, k0=k0, bS=bS, sc=sc: e.matmul(self.ps[:, bo, 0:NQ], lhsT=v_tok[:, k0 // 128, kvh * 128:(kvh + 1) * 128],
                                                                                    rhs=PT[:, bS, 0:NQ], start=(sc == 0), stop=(sc == nsc - 1)),
                                      R=[vb, ptb[bS]], W=[self.pb[bo]])
                                tr.op("pe", lambda e, bS=bS, sc=sc: e.matmul(self.ps[:, bs, 0:NQ], lhsT=self.ones[:], rhs=PT[:, bS, 0:NQ],
                                                                             start=(sc == 0), stop=(sc == nsc - 1)),
                                      R=[self.cst, ptb[bS]], W=[self.pb[bs]])
                            tr.op("dve", lambda e, bs=bs: e.reciprocal(out=rs[:, 0:NQ], in_=self.ps[:, bs, 0:NQ]), R=[self.pb[bs]], W=[rsb])
                            tr.op("dve", lambda e, bo=bo, s=s, q0=q0, h=h: e.tensor_tensor(
                                out=mixT[:, h, s * L + q0:s * L + q0 + NQ], in0=self.ps[:, bo, 0:NQ], in1=rs[:, 0:NQ], op=ALU.mult),
                                  R=[self.pb[bo], rsb], W=[mb])
                tr.barrier()

    def mix_ret(self, l, G, mixT, mb):
        nc, tr, di = self.nc, self.tr, self.din
        nseq, L, smp = G["nseq"], G["L"], G["sample"]
        T = nseq * L
        NTT, nch = T // 128, L // 128
        with ExitStack() as es0:
            A0 = lambda n, sh, dt: self.alloc(es0, n, sh, dt)
            qrT, krT = A0("qrT", [128, 4, T], BF16), A0("krT", [128, 4, T], BF16)
            kr_tok, vr_tok, gs_tok = A0("krtok", [128, NTT, 512], BF16), A0("vrtok", [128, NTT, 512], BF16), A0("gstok", [128, NTT, 512], BF16)
            qrb, krb, ktb, vtb, gsb = [Buf() for _ in range(5)]
            with ExitStack() as es:
                A = lambda n, sh, dt: self.alloc(es, n, sh, dt)
                RS = []
                for _ in range(2):
                    RS.append(dict(st=A("rst", [128, 512], F32), stb_=A("rstb", [128, 512], BF16), rtmp=A("rrtmp", [128, 128], F32),
                                   s1=Buf(), s2=Buf(), rtb=Buf()))
                cnt = [0]

                def consume(cb, ttg, bank):
                    par = cnt[0] % 2
                    cnt[0] += 1
                    Z = RS[par]
                    st, stb_, rtmp, s1, s2, rtb = Z["st"], Z["stb_"], Z["rtmp"], Z["s1"], Z["s2"], Z["rtb"]
                    if cb in (0, 1):
                        sc = 1.0 if cb == 0 else 128.0 ** -0.5
                        tr.op("act", lambda e: e.activation(out=st[:], in_=self.ps[:, bank, :], func=AF.Copy, scale=sc), R=[self.pb[bank]], W=[s1])
                        if smp:
                            self.rope_tok(st[:], s1, 4, ttg, rtmp[:], rtb)
                        if cb == 0:
                            tr.op("dve", lambda e: e.tensor_copy(out=stb_[:], in_=st[:]), R=[s1], W=[s2])
                            self.transpose_to(stb_[:], s2, 4, lambda h: qrT[:, h, ttg * 128:(ttg + 1) * 128], qrb, 4 + par)
                        else:
                            tr.op("dve", lambda e: e.tensor_copy(out=kr_tok[:, ttg, :], in_=st[:]), R=[s1], W=[ktb])
                            self.transpose_to(kr_tok[:, ttg, :], ktb, 4, lambda h: krT[:, h, ttg * 128:(ttg + 1) * 128], krb, 4 + par)
                    elif cb == 2:
                        tr.op("act", lambda e: e.activation(out=vr_tok[:, ttg, :], in_=self.ps[:, bank, :], func=AF.Copy), R=[self.pb[bank]], W=[vtb])
                    else:
                        tr.op("act", lambda e: e.activation(out=gs_tok[:, ttg, :], in_=self.ps[:, bank, :], func=AF.Silu), R=[self.pb[bank]], W=[gsb])

                self.proj_phase(l, G, [2048, 2560, 3072, 3584], consume)
            with ExitStack() as es:
                A = lambda n, sh, dt: self.alloc(es, n, sh, dt)
                dl = A("dl", [128, 8], F32)
                lg = A("lg", [128, 8], F32)
                lg128 = A("lg128", [128, 8], F32)
                Mt = A("Mt", [128, 4, 128], F32)
                qd = A("qd", [128, 8, 128], F32)
                kd = A("kd", [128, 8], F32)
                cd = A("cd", [128, 8], F32)
                tmpa, tmpb = A("tmpa", [128, 128], F32), A("tmpb", [128, 128], F32)
                KV = A("KV", [128, 2, nch, 128], F32)
                Sst = A("Sst", [128, 2, nch, 128], BF16)
                Srun = A("Srun", [128, 2, 128], F32)
                attM = A("attM", [128, 128], BF16)
                kdk = A("kdk", [128, 2, 128], BF16)
                qdq = A("qdq", [128, 2, 128], BF16)
                stat = A("stat", [128, 8], F32)
                rtk = A("rtk", [128, 128], BF16)
                on = A("on", [128, 128], F32)
                tb_, kvb, ssb, srb, amb, kdb_, qdb_, stb2, rtkb, onb = [Buf() for _ in range(10)]
                O = lambda f, R, W: tr.op("dve", f, R=R, W=W)
                Aop = lambda f, R, W: tr.op("act", f, R=R, W=W)
                TT = lambda e, o, a, b, op: e.tensor_tensor(out=o, in0=a, in1=b, op=op)
                tr.dma("sp", dl[:], di["dlog_rep"][l], W=[tb_])
                Aop(lambda e: e.activation(out=lg[:], in_=dl[:], func=AF.Exp, scale=-1.0), [tb_], [tb_])
                O(lambda e: e.tensor_scalar(out=lg[:], in0=lg[:], scalar1=1.0, scalar2=None, op0=ALU.add), [tb_], [tb_])
                Aop(lambda e: e.activation(out=lg[:], in_=lg[:], func=AF.Ln), [tb_], [tb_])
                O(lambda e: e.tensor_scalar(out=lg[:], in0=lg[:], scalar1=-1.0, scalar2=None, op0=ALU.mult), [tb_], [tb_])
                O(lambda e: e.tensor_scalar(out=lg128[:], in0=lg[:], scalar1=128.0, scalar2=None, op0=ALU.mult), [tb_], [tb_])
                Aop(lambda e: e.activation(out=cd[:], in_=lg128[:], func=AF.Exp), [tb_], [tb_])
                for h in range(4):
                    f_, b_ = h, 4 + h
                    O(lambda e: e.tensor_scalar(out=tmpa[:], in0=self.diff[:], scalar1=0.0, scalar2=None, op0=ALU.max), [self.cst, tb_], [tb_])
                    Aop(lambda e, f_=f_: e.activation(out=tmpa[:], in_=tmpa[:], func=AF.Exp, scale=lg[:, f_:f_ + 1]), [tb_], [tb_])
                    O(lambda e: TT(e, tmpa[:], tmpa[:], self.lmask[:], ALU.mult), [tb_, self.cst], [tb_])
                    O(lambda e: e.tensor_scalar(out=tmpb[:], in0=self.diff[:], scalar1=-1.0, scalar2=0.0, op0=ALU.mult, op1=ALU.max), [self.cst, tb_], [tb_])
                    Aop(lambda e, b_=b_: e.activation(out=tmpb[:], in_=tmpb[:], func=AF.Exp, scale=lg[:, b_:b_ + 1]), [tb_], [tb_])
                    O(lambda e: TT(e, tmpb[:], tmpb[:], self.umask[:], ALU.mult), [tb_, self.cst], [tb_])
                    O(lambda e, h=h: TT(e, Mt[:, h, :], tmpa[:], tmpb[:], ALU.add), [tb_], [tb_])
                    O(lambda e: e.tensor_scalar(out=tmpa[:], in0=self.tpos[:], scalar1=1.0, scalar2=None, op0=ALU.add), [self.cst, tb_], [tb_])
                    Aop(lambda e, f_=f_: e.activation(out=qd[:, f_, :], in_=tmpa[:], func=AF.Exp, scale=lg[:, f_:f_ + 1]), [tb_], [tb_])
                    O(lambda e: e.tensor_scalar(out=tmpb[:], in0=self.tpos[:], scalar1=-1.0, scalar2=128.0, op0=ALU.mult, op1=ALU.add), [self.cst, tb_], [tb_])
                    Aop(lambda e, b_=b_: e.activation(out=qd[:, b_, :], in_=tmpb[:], func=AF.Exp, scale=lg[:, b_:b_ + 1]), [tb_], [tb_])
                    O(lambda e: e.tensor_scalar(out=tmpa[:, 0:1], in0=self.ipos[:], scalar1=-1.0, scalar2=127.0, op0=ALU.mult, op1=ALU.add), [self.cst, tb_], [tb_])
                    Aop(lambda e, f_=f_: e.activation(out=kd[:, f_:f_ + 1], in_=tmpa[:, 0:1], func=AF.Exp, scale=lg[:, f_:f_ + 1]), [tb_], [tb_])
                    Aop(lambda e, b_=b_: e.activation(out=kd[:, b_:b_ + 1], in_=self.ipos[:], func=AF.Exp, scale=lg[:, b_:b_ + 1]), [tb_, self.cst], [tb_])
                for s in range(nseq):
                    for h in range(4):
                        hc = slice(h * 128, (h + 1) * 128)
                        for ci in range(nch):
                            ttg = s * nch + ci
                            for d_ in range(2):
                                O(lambda e, d_=d_, ttg=ttg: e.tensor_scalar(out=kdk[:, d_, :], in0=kr_tok[:, ttg, hc], scalar1=kd[:, d_ * 4 + h:d_ * 4 + h + 1],
                                                                            scalar2=None, op0=ALU.mult), [ktb, tb_], [kdb_])
                            tr.grp("pe", [lambda e, d_=d_, ttg=ttg: e.matmul(self.ps[:, 0, d_ * 128:(d_ + 1) * 128], lhsT=kdk[:, d_, :], rhs=vr_tok[:, ttg, hc],
                                                                             start=True, stop=True) for d_ in range(2)], R=[kdb_, vtb], W=[self.pb[0]])
                            Aop(lambda e, ci=ci: e.activation(out=KV[:, :, ci, :], in_=self.ps[:, 0, 0:256].rearrange("p (a b) -> p a b", b=128), func=AF.Copy),
                                [self.pb[0]], [kvb])
                        for d_ in range(2):
                            cdc = cd[:, d_ * 4 + h:d_ * 4 + h + 1]
                            if smp:
                                tr.dma("sp", Srun[:, d_, :], di["s0s"][l, d_, h], W=[srb])
                            else:
                                O(lambda e, d_=d_: e.memset(Srun[:, d_, :], 0.0), [], [srb])
                            order = list(range(nch)) if d_ == 0 else list(range(nch - 1, -1, -1))
                            for ci in order:
                                O(lambda e, d_=d_, ci=ci: e.tensor_copy(out=Sst[:, d_, ci, :], in_=Srun[:, d_, :]), [srb], [ssb])
                                O(lambda e, d_=d_, ci=ci, cdc=cdc: e.scalar_tensor_tensor(out=Srun[:, d_, :], in0=Srun[:, d_, :], scalar=cdc, in1=KV[:, d_, ci, :],
                                                                                          op0=ALU.mult, op1=ALU.add), [srb, kvb, tb_], [srb])
                            if not smp:
                                tr.dma("sp", self.dout["ss"][l, s, d_, h], Srun[:, d_, :], R=[srb], is_out=True)
                        for ci in range(nch):
                            ttg = s * nch + ci
                            tk = slice(ttg * 128, (ttg + 1) * 128)
                            tr.op("pe", lambda e, tk=tk: e.matmul(self.ps[:, 1, 0:128], lhsT=krT[:, h, tk], rhs=qrT[:, h, tk], start=True, stop=True),
                                  R=[krb, qrb], W=[self.pb[1]])
                            O(lambda e: TT(e, attM[:], self.ps[:, 1, 0:128], Mt[:, h, :], ALU.mult), [self.pb[1], tb_], [amb])
                            for d_ in range(2):
                                O(lambda e, d_=d_, tk=tk: TT(e, qdq[:, d_, :], qrT[:, h, tk], qd[:, d_ * 4 + h, :], ALU.mult), [qrb, tb_], [qdb_])
                            tr.grp("pe", [lambda e, ttg=ttg: e.matmul(self.ps[:, 2, 0:128], lhsT=attM[:], rhs=vr_tok[:, ttg, hc], start=True, stop=False),
                                          lambda e, ci=ci: e.matmul(self.ps[:, 2, 0:128], lhsT=qdq[:, 0, :], rhs=Sst[:, 0, ci, :], start=False, stop=False),
                                          lambda e, ci=ci: e.matmul(self.ps[:, 2, 0:128], lhsT=qdq[:, 1, :], rhs=Sst[:, 1, ci, :], start=False, stop=True)],
                                   R=[amb, vtb, qdb_, ssb], W=[self.pb[2]])
                            O(lambda e: e.bn_stats(out=stat[:, 0:6], in_=self.ps[:, 2, 0:128]), [self.pb[2]], [stb2])
                            O(lambda e: e.bn_aggr(out=stat[:, 6:8], in_=stat[:, 0:6]), [stb2], [stb2])
                            Aop(lambda e: e.activation(out=stat[:, 7:8], in_=stat[:, 7:8], func=AF.Sqrt, bias=self.epsb[:, 0:1]), [stb2, self.cst], [stb2])
                            O(lambda e: e.reciprocal(out=stat[:, 7:8], in_=stat[:, 7:8]), [stb2], [stb2])
                            O(lambda e: e.tensor_scalar(out=on[:], in0=self.ps[:, 2, 0:128], scalar1=stat[:, 6:7], scalar2=stat[:, 7:8],
                                                        op0=ALU.subtract, op1=ALU.mult), [self.pb[2], stb2], [onb])
                            O(lambda e, ttg=ttg: TT(e, rtk[:], on[:], gs_tok[:, ttg, hc], ALU.mult), [onb, gsb], [rtkb])
                            self.transpose_to(rtk[:], rtkb, 1, lambda i, tk=tk: mixT[:, 12 + h, tk], mb, 3)
                tr.barrier()

    def mix_out(self, l, G, mixT, mb):
        nc, tr, di = self.nc, self.tr, self.din
        T = G["nseq"] * G["L"]
        wv = di["w_out"][l].rearrange("(kc p) n -> p kc n", p=128)
        xsv = self.xs.rearrange("(kc p) t -> p kc t", p=128)
        with ExitStack() as es:
            xt = self.alloc(es, "xto", [128, KC, 512], F32)
            xb = Buf()
            for ti in range(T // 512):
                g0 = G["t0"] + ti * 512
                tr.dma("sp", xt[:], xsv[:, :, g0:g0 + 512], R=[self.xsb[g0 // 512]], W=[xb])
                for o2 in range(8):
                    s = self.wslot(wv[:, :, o2 * 256:(o2 + 1) * 256], 16, 256)
                    for jj in range(2):
                        oc = o2 * 2 + jj
                        bk = oc % 2
                        ems = [lambda e, kc=kc, jj=jj, bk=bk, s=s: e.matmul(
                            self.ps[:, bk, :], lhsT=self.wap(s, kc, 256, jj * 128, jj * 128 + 128),
                            rhs=mixT[:, kc, ti * 512:(ti + 1) * 512], start=(kc == 0), stop=(kc == KC - 1)) for kc in range(KC)]
                        tr.grp("pe", ems, R=[self.rbuf[s], mb], W=[self.pb[bk]])
                        tr.op("dve", lambda e, oc=oc, bk=bk: e.scalar_tensor_tensor(
                            out=xt[:, oc, :], in0=self.ps[:, bk, :], scalar=self.gate[:, 1, G["c"], oc:oc + 1], in1=xt[:, oc, :],
                            op0=ALU.mult, op1=ALU.add), R=[self.pb[bk], self.modb, xb], W=[xb])
                tr.dma("sp", xsv[:, :, g0:g0 + 512], xt[:], R=[xb], W=[self.xsb[g0 // 512]])
            tr.barrier()

    def mixer(self, l, G):
        nc, tr = self.nc, self.tr
        T = G["nseq"] * G["L"]
        with ExitStack() as es:
            mixT = self.alloc(es, "mixT", [128, 16, T], BF16)
            mb = Buf()
            self.mix_s5(l, G, mixT, mb)
            self.mix_attn(l, G, mixT, mb)
            self.mix_ret(l, G, mixT, mb)
            self.mix_out(l, G, mixT, mb)
            tr.barrier()

    def ffn_stage(self, l, f, supers, last):
        nc, tr, di = self.nc, self.tr, self.din
        sub = 0 if f == 0 else 2
        xsv = self.xs.rearrange("(kc p) t -> p kc t", p=128)
        win = di["w_ffn_in"][l, f].rearrange("(kc p) n -> p kc n", p=128)
        wout = di["w_ffn_out"][l, f].rearrange("(kc p) n -> p kc n", p=128)
        BLK = [(0, 12), (12, 12), (24, 12), (36, 8)]
        with ExitStack() as es:
            xt = self.alloc(es, "xtf", [128, 2, KC, 512], F32)
            hT = self.alloc(es, "hTf", [128, 2, KC, 512], BF16)
            work = self.alloc(es, "wkf", [128, 3, 512], F32)
            hid = self.alloc(es, "hid", [128, 2, 12, 512], BF16)
            xb, hb, hidb = [Buf(), Buf()], [Buf(), Buf()], [Buf(), Buf()]
            wb = [Buf() for _ in range(4)]
            for sup in supers:
                NTs = len(sup)
                for i, (t0, c) in enumerate(sup):
                    src = di["xin"] if (l == 0 and f == 0) else self.xs
                    self.load_norm(src, t0, 512, xt[:, i], xb[i], hT[:, i], hb[i], sub, c, work, wb)
                for (k0, HB) in BLK:
                    for h2 in range(HB // 2):
                        c0 = (k0 + 2 * h2) * 128
                        sa = self.wslot(win[:, :, c0:c0 + 256], 16, 256)
                        su = self.wslot(win[:, :, DFF + c0:DFF + c0 + 256], 16, 256)
                        for jj in range(2):
                            for i in range(NTs):
                                ba, bu = 2 * i, 2 * i + 1
                                for (s, bk) in ((sa, ba), (su, bu)):
                                    ems = [lambda e, kc=kc, s=s, bk=bk, jj=jj, i=i: e.matmul(
                                        self.ps[:, bk, :], lhsT=self.wap(s, kc, 256, jj * 128, jj * 128 + 128),
                                        rhs=hT[:, i, kc, :], start=(kc == 0), stop=(kc == KC - 1)) for kc in range(KC)]
                                    tr.grp("pe", ems, R=[self.rbuf[s], hb[i]], W=[self.pb[bk]])
                                sl = work[:, i, :]
                                tr.op("act", lambda e, ba=ba, sl=sl: e.activation(out=sl, in_=self.ps[:, ba, :], func=AF.Silu),
                                      R=[self.pb[ba]], W=[wb[i]])
                                tr.op("dve", lambda e, bu=bu, sl=sl, i=i, kk=2 * h2 + jj: e.tensor_tensor(
                                    out=hid[:, i, kk, :], in0=self.ps[:, bu, :], in1=sl, op=ALU.mult),
                                      R=[self.pb[bu], wb[i]], W=[hidb[i]])
                    for o2 in range(8):
                        s = self.wslot(wout[:, k0:k0 + HB, o2 * 256:(o2 + 1) * 256], HB, 256)
                        for jj in range(2):
                            oc = o2 * 2 + jj
                            for i in range(NTs):
                                bk = 4 + 2 * (oc % 2) + i
                                ems = [lambda e, k=k, jj=jj, bk=bk, i=i, s=s: e.matmul(
                                    self.ps[:, bk, :], lhsT=self.wap(s, k, 256, jj * 128, jj * 128 + 128),
                                    rhs=hid[:, i, k, :], start=(k == 0), stop=(k == HB - 1)) for k in range(HB)]
                                tr.grp("pe", ems, R=[self.rbuf[s], hidb[i]], W=[self.pb[bk]])
                                tr.op("dve", lambda e, oc=oc, bk=bk, i=i, c=sup[i][1]: e.scalar_tensor_tensor(
                                    out=xt[:, i, oc, :], in0=self.ps[:, bk, :], scalar=self.gate[:, sub, c, oc:oc + 1],
                                    in1=xt[:, i, oc, :], op0=ALU.mult, op1=ALU.add), R=[self.pb[bk], self.modb, xb[i]], W=[xb[i]])
                for i, (t0, c) in enumerate(sup):
                    if last:
                        self.final_norm_store(xt[:, i], xb[i], t0, work, wb)
                    else:
                        tr.dma("sp", xsv[:, :, t0:t0 + 512], xt[:, i], R=[xb[i]], W=[self.xsb[t0 // 512]])
            tr.barrier()

    def build(self):
        tr, nc, di, do = self.tr, self.nc, self.din, self.dout
        self.epsb = nc.alloc_sbuf_tensor("epsb", [128, 1], F32)
        tr.op("dve", lambda e: e.memset(self.epsb[:], EPS), W=[self.cst])
        self.setup_consts()
        groups = [dict(t0=0, nseq=self.nseq_p, L=self.L_p, c=0, sample=False)]
        tiles = [(0, 0)]
        supers = [[(0, 0)]]
        if self.with_sample:
            groups.append(dict(t0=512, nseq=1, L=1024, c=1, sample=True))
            tiles += [(512, 1), (1024, 1)]
            supers.append([(512, 1), (1024, 1)])
        if not self.do_ffn:
            xsv0 = self.xs.rearrange("(kc p) t -> p kc t", p=128)
            xiv0 = self.din["xin"].rearrange("(kc p) t -> p kc t", p=128)
            with ExitStack() as es:
                xt0 = self.alloc(es, "xt0", [128, KC, 512], F32)
                xb0 = Buf()
                for (t0, c) in tiles:
                    tr.dma("sp", xt0[:], xiv0[:, :, t0:t0 + 512], W=[xb0])
                    tr.dma("sp", xsv0[:, :, t0:t0 + 512], xt0[:], R=[xb0], W=[self.xsb[t0 // 512]])
                tr.barrier()
        for l in self.layers:
            self.adaln(l)
            if self.do_ffn:
                self.ffn_stage(l, 0, supers, False)
            for G in groups:
                self.mixer(l, G)
            if self.do_ffn:
                self.ffn_stage(l, 1, supers, l == self.layers[-1])
        if not self.do_ffn:
            xsv = self.xs.rearrange("(kc p) t -> p kc t", p=128)
            yv = self.dout["y"].rearrange("(kc p) t -> p kc t", p=128)
            with ExitStack() as es:
                xt = self.alloc(es, "xtd", [128, KC, 512], F32)
                xb = Buf()
                for (t0, c) in tiles:
                    tr.dma("sp", xt[:], xsv[:, :, t0:t0 + 512], R=[self.xsb[t0 // 512]], W=[xb])
                    tr.dma("sp", yv[:, :, t0:t0 + 512], xt[:], R=[xb], is_out=True)
        tr.finish()

    def final_norm_store(self, xt, xb, t0, work, wb):
        tr, nc = self.tr, self.nc
        yv = self.dout["y"].rearrange("(kc p) t -> p kc t", p=128)
        bank = 6
        sq = work[:, 2, :].bitcast(BF16)
        for kc in range(KC):
            h = kc % 2
            tr.op("act", lambda e, kc=kc, h=h: e.activation(out=sq[:, h * 512:(h + 1) * 512], in_=xt[:, kc, :], func=AF.Square),
                  R=[xb], W=[wb[2 + h]])
            tr.op("pe", lambda e, kc=kc, h=h: e.matmul(self.ps[:, bank, :], lhsT=self.ones[:], rhs=sq[:, h * 512:(h + 1) * 512],
                                                       start=(kc == 0), stop=(kc == KC - 1)), R=[wb[2 + h], self.cst], W=[self.pb[bank]])
        rstd = work[:, 0, :]
        tr.op("act", lambda e: e.activation(out=rstd, in_=self.ps[:, bank, :], func=AF.Sqrt, scale=1.0 / D, bias=self.epsb[:, 0:1]),
              R=[self.pb[bank], self.cst], W=[wb[0]])
        tr.op("dve", lambda e: e.reciprocal(out=rstd, in_=rstd), R=[wb[0]], W=[wb[0]])
        for kc in range(KC):
            tr.op("dve", lambda e, kc=kc: e.scalar_tensor_tensor(out=xt[:, kc, :], in0=xt[:, kc, :], scalar=self.fgT[:, kc:kc + 1],
                                                                 in1=rstd, op0=ALU.mult, op1=ALU.mult),
                  R=[xb, wb[0], self.cst], W=[xb])
        tr.dma("sp", yv[:, :, t0:t0 + 512], xt[:], R=[xb], is_out=True)


def _prep_shared(inp):
    f = lambda a: np.ascontiguousarray(np.asarray(a, dtype=np.float32))
    sh = {}
    for k in ("w_mod", "w_ffn_in", "w_ffn_out", "w_in", "w_out"):
        sh[k] = f(inp[k])
    sh["w_glu"] = f(inp["w_ssm_glu"])
    sh["b_modT"] = f(np.asarray(inp["b_mod"]).reshape(DEPTH, 144, 128).transpose(0, 2, 1))
    sh["norm_gT"] = f(np.asarray(inp["norm_g"]).reshape(DEPTH, 3, KC, 128).transpose(0, 3, 1, 2).reshape(DEPTH, 128, 48))
    sh["final_gT"] = f(np.asarray(inp["final_norm_g"]).reshape(KC, 128).T)
    sh["qg_rep"] = f(np.broadcast_to(np.asarray(inp["q_norm_g"])[:, None, :], (DEPTH, 128, 128)))
    sh["kg_rep"] = f(np.broadcast_to(np.asarray(inp["k_norm_g"])[:, None, :], (DEPTH, 128, 128)))

    def chmaj(a):
        a = np.asarray(a).reshape(DEPTH, 2, 16, 2, 64)
        return a.transpose(0, 1, 3, 4, 2).reshape(DEPTH, 2, 128, 16)
    ldt = np.broadcast_to(np.asarray(inp["ssm_log_dt"])[..., None], (DEPTH, 2, 32, 64))
    sh["ssm_abd"] = f(np.stack([chmaj(inp["ssm_a_re"]), chmaj(inp["ssm_a_im"]), chmaj(ldt)], axis=3))
    BT = np.zeros((DEPTH, 2, 2, 128, 16, 128), np.float32)
    CT = np.zeros((DEPTH, 2, 2, 128, 16, 128), np.float32)
    Bs = [np.asarray(inp["ssm_b_re"]), np.asarray(inp["ssm_b_im"])]
    Cs = [np.asarray(inp["ssm_c_re"]), np.asarray(inp["ssm_c_im"])]
    for j in range(16):
        for gl in range(2):
            g = 2 * j + gl
            r0 = 32 * (j % 4) + 16 * gl
            for c in range(2):
                BT[:, :, c, r0:r0 + 16, j, gl * 64:(gl + 1) * 64] = Bs[c][:, :, g].transpose(0, 1, 3, 2)
                CT[:, :, c, gl * 64:(gl + 1) * 64, j, r0:r0 + 16] = Cs[c][:, :, g].transpose(0, 1, 3, 2)
    sh["ssm_BT"] = BT.reshape(DEPTH, 2, 2, 128, 2048)
    sh["ssm_CT"] = CT.reshape(DEPTH, 2, 2, 128, 2048)
    sh["ssm_d_rep"] = f(np.broadcast_to(np.asarray(inp["ssm_d"])[:, None, :], (DEPTH, 128, 512)))
    sh["dlog_rep"] = f(np.broadcast_to(np.asarray(inp["ret_decay_logit"]).reshape(DEPTH, 1, 8), (DEPTH, 128, 8)))
    i = np.arange(128, dtype=np.float32)
    sh["c_ident"] = np.eye(128, dtype=np.float32)
    sh["c_J"] = np.ascontiguousarray(np.eye(128, dtype=np.float32)[::-1])
    sh["c_tpos"] = f(np.broadcast_to(i[None, :], (128, 128)))
    sh["c_ipos"] = f(i[:, None])
    sh["c_diff"] = f(i[None, :] - i[:, None])
    sh["c_lmask"] = f((i[None, :] >= i[:, None]))
    sh["c_umask"] = f((i[None, :] <= i[:, None]))
    t = np.arange(1024)
    pos = np.stack([t // 64, t % 64], axis=-1).astype(np.float32)
    inv = (np.float32(10000.0) ** (-np.arange(32, dtype=np.float32) / np.float32(32))).astype(np.float32)
    ang = (pos[:, :, None] * inv[None, None, :]).astype(np.float32)
    co, si = np.cos(ang).astype(np.float32), np.sin(ang).astype(np.float32)
    cfull = np.stack([co, co], axis=2).reshape(1024, 128)
    sfull = np.stack([-si, si], axis=2).reshape(1024, 128)
    sh["rope_c"] = f(cfull.reshape(8, 128, 128).transpose(1, 0, 2))
    sh["rope_s"] = f(sfull.reshape(8, 128, 128).transpose(1, 0, 2))
    return sh


def _in_maps(inp, P):
    sh = _prep_shared(inp)
    xp = np.asarray(inp["x_prompt"], np.float32)
    xsm = np.asarray(inp["x_sample"], np.float32)
    cs = np.asarray(inp["c"], np.float32)
    cctx = np.asarray(inp["c_ctx"], np.float32)
    ck = np.asarray(inp["cache_k"], np.float32)
    cv_ = np.asarray(inp["cache_v"], np.float32)
    sts = np.asarray(inp["state_ssm"], np.float32)
    str_ = np.asarray(inp["state_ret"], np.float32)
    maps = []
    for c in range(NCORE):
        b = c % 2
        m = dict(sh)
        xs_ = [xp[2 * c:2 * c + 2].reshape(512, D)]
        if P.with_sample:
            xs_.append(xsm[b])
        m["xin"] = np.ascontiguousarray(np.concatenate(xs_, axis=0).T)
        cvv = np.stack([cctx, cs[b]], axis=-1)
        m["cvec"] = np.ascontiguousarray(cvv.reshape(KC, 128, 2).transpose(1, 0, 2))
        m["cache_k"] = np.ascontiguousarray(ck[b].reshape(DEPTH, PAST, 256))
        m["cache_v"] = np.ascontiguousarray(cv_[b].reshape(DEPTH, PAST, 256))
        h0 = sts[b].reshape(DEPTH, 2, 16, 2, 64, 2).transpose(0, 1, 3, 4, 5, 2).reshape(DEPTH, 2, 128, 2, 16)
        m["h0s"] = np.ascontiguousarray(h0)
        m["s0s"] = np.ascontiguousarray(str_[b])
        maps.append({k: v for k, v in m.items() if k in P.din})
    return maps


_PROG = None


def kernel(**inp):
    global _PROG
    if _PROG is None:
        _PROG = Prog()
    P = _PROG
    res = run_bass_kernel_spmd(P.nc, _in_maps(inp, P), core_ids=list(range(NCORE)))
    R = res.results
    f32 = np.float32
    y_prompt = np.stack([R[c]["y"][:, 0:512].T.reshape(2, 256, D) for c in range(NCORE)]).reshape(16, 256, D).astype(f32)
    y_sample = np.stack([R[b]["y"][:, 512:1536].T for b in range(2)]).astype(f32)
    def cache(name):
        a = np.stack([R[c][name].reshape(DEPTH, 2, 256, 2, 128).transpose(1, 0, 2, 3, 4) for c in range(NCORE)])
        return np.ascontiguousarray(a.reshape(16, DEPTH, 256, 2, 128).astype(f32))
    def hstate():
        out = []
        for c in range(NCORE):
            a = R[c]["hs"].reshape(DEPTH, 2, 2, 2, 64, 16, 2)
            out.append(a.transpose(1, 0, 2, 5, 3, 4, 6).reshape(2, DEPTH, 2, 32, 64, 2))
        return np.ascontiguousarray(np.concatenate(out, axis=0).astype(f32))
    def sstate():
        out = [R[c]["ss"].transpose(1, 0, 2, 3, 4, 5) for c in range(NCORE)]
        return np.ascontiguousarray(np.concatenate(out, axis=0).astype(f32))
    return (y_prompt, y_sample, cache("ck"), cache("cv"), hstate(), sstate())
```

```python
import numpy as np
import concourse.bass as bass
import concourse.mybir as mybir
from concourse.bass_utils import run_bass_kernel_spmd
from contextlib import ExitStack

F32 = mybir.dt.float32
BF16 = mybir.dt.bfloat16
AF = mybir.ActivationFunctionType
ALU = mybir.AluOpType
AX = mybir.AxisListType

D = 2048
KC = 16
DFF = 5632
HC = 44
NCORE = 8
DEPTH = 2
EPS = 1e-6
PAST = 512
PI_SAFE = 3.1415925


class Buf:
    __slots__ = ("w", "r", "name")

    def __init__(self, name=""):
        self.w = None
        self.r = {}
        self.name = name


class Eng:
    EPOCH = 12000

    def __init__(self, nc, name, h, ndma=0):
        self.nc, self.name, self.h = nc, name, h
        self.sems = [nc.alloc_semaphore(f"s_{name}_0")]
        self.ep, self.cnt = 0, 0
        self.waited = {}
        self.dsem = [nc.alloc_semaphore(f"d_{name}_{i}") for i in range(ndma)]
        self.dval = [0] * ndma
        self.di = 0

    def bump(self, inst):
        if self.cnt >= self.EPOCH:
            self.ep += 1
            self.cnt = 0
            self.sems.append(self.nc.alloc_semaphore(f"s_{self.name}_{self.ep}"))
        self.cnt += 1
        inst.then_inc(self.sems[self.ep], 1)
        return ("c", self.name, self.ep, self.cnt, self.sems[self.ep])


class TR:
    def __init__(self, nc):
        self.nc = nc
        self.E = {
            "pe": Eng(nc, "pe", nc.tensor),
            "act": Eng(nc, "act", nc.scalar),
            "dve": Eng(nc, "dve", nc.vector),
            "sp": Eng(nc, "sp", nc.sync, ndma=12),
            "pool": Eng(nc, "pool", nc.gpsimd, ndma=12),
        }
        self.out_toks = []

    def _wait(self, E, deps):
        for t in deps:
            if t is None:
                continue
            if t[0] == "c":
                _, en, ep, cnt, sem = t
                if en == "pe" and E.name == "pe":
                    continue
                key = ("c", en)
                cur = E.waited.get(key, (-1, -1))
                if (ep, cnt) <= cur:
                    continue
                E.h.wait_ge(sem, cnt)
                E.waited[key] = (ep, cnt)
            else:
                _, sid, val, sem = t
                key = ("d", sid)
                if E.waited.get(key, 0) >= val:
                    continue
                E.h.wait_ge(sem, val)
                E.waited[key] = val

    def _deps(self, R, W):
        deps = []
        for b in R:
            deps.append(b.w)
        for b in W:
            deps.append(b.w)
            deps.extend(b.r.values())
        return deps

    def _mark(self, tok, R, W):
        key = (tok[0], tok[1])
        for b in R:
            b.r[key] = tok
        for b in W:
            b.w = tok
            b.r = {}

    def op(self, en, emit, R=(), W=()):
        E = self.E[en]
        self._wait(E, self._deps(R, W))
        inst = emit(E.h)
        tok = E.bump(inst)
        self._mark(tok, R, W)
        return tok

    def grp(self, en, emits, R=(), W=()):
        E = self.E[en]
        self._wait(E, self._deps(R, W))
        inst = None
        for f in emits:
            inst = f(E.h)
        tok = E.bump(inst)
        self._mark(tok, R, W)
        return tok

    def dma(self, q, out, in_, R=(), W=(), is_out=False):
        E = self.E[q]
        self._wait(E, self._deps(R, W))
        i = E.di
        E.di = (E.di + 1) % len(E.dsem)
        sem, prev = E.dsem[i], E.dval[i]
        key = ("d", id(sem))
        if prev > 0 and E.waited.get(key, 0) < prev:
            E.h.wait_ge(sem, prev)
            E.waited[key] = prev
        E.h.dma_start(out=out, in_=in_).then_inc(sem, 16)
        E.dval[i] = prev + 16
        tok = ("d", id(sem), prev + 16, sem)
        self._mark(tok, R, W)
        if is_out:
            self.out_toks.append(tok)
        return tok

    def barrier(self):
        toks = []
        for en in ("pe", "act", "dve"):
            E = self.E[en]
            if E.cnt > 0 or E.ep > 0:
                toks.append(("c", en, E.ep, E.cnt, E.sems[E.ep]))
        E = self.E["sp"]
        for i, s in enumerate(E.dsem):
            if E.dval[i] > 0:
                toks.append(("d", id(s), E.dval[i], s))
        for en in ("pe", "act", "dve", "sp"):
            X = self.E[en]
            for t in toks:
                if t[0] == "c" and t[1] == en:
                    continue
                self._wait(X, [t])
        self.fence = toks

    def wait_fence(self, en):
        self._wait(self.E[en], getattr(self, "fence", []))

    def finish(self):
        E = self.E["sp"]
        self._wait(E, self.out_toks)
        for q in ("sp", "pool"):
            Q = self.E[q]
            for i, s in enumerate(Q.dsem):
                if Q.dval[i] > 0:
                    self._wait(E, [("d", id(s), Q.dval[i], s)])


class Prog:
    def __init__(self, nseq_p=2, L_p=256, with_sample=True, layers=(0, 1), do_ffn=True):
        self.nc = nc = bass.Bass("TRN2", target_bir_lowering=False)
        self.tr = TR(nc)
        self.with_sample = with_sample
        self.layers, self.do_ffn = list(layers), do_ffn
        self.TP = nseq_p * L_p
        self.TT = self.TP + (1024 if with_sample else 0)
        self.nseq_p, self.L_p = nseq_p, L_p
        di = self.din = {}

        def inp(name, shape, dt=F32):
            di[name] = nc.dram_tensor(name, list(shape), dt, kind="ExternalInput").ap()
            return di[name]

        inp("xin", [D, self.TT])
        inp("cvec", [128, KC, 2])
        inp("w_mod", [DEPTH, D, 9 * D])
        inp("b_modT", [DEPTH, 128, 144])
        inp("norm_gT", [DEPTH, 128, 48])
        inp("final_gT", [128, KC])
        if do_ffn:
            inp("w_ffn_in", [DEPTH, 2, D, 2 * DFF])
            inp("w_ffn_out", [DEPTH, 2, DFF, D])
        inp("w_in", [DEPTH, D, 4096])
        inp("w_out", [DEPTH, D, D])
        inp("qg_rep", [DEPTH, 128, 128])
        inp("kg_rep", [DEPTH, 128, 128])
        inp("ssm_abd", [DEPTH, 2, 128, 3, 16])
        inp("ssm_BT", [DEPTH, 2, 2, 128, 16 * 128])
        inp("ssm_CT", [DEPTH, 2, 2, 128, 16 * 128])
        inp("ssm_d_rep", [DEPTH, 128, 512])
        inp("w_glu", [DEPTH, 512, 1024])
        inp("dlog_rep", [DEPTH, 128, 8])
        for nm in ("c_ident", "c_J", "c_tpos", "c_diff", "c_lmask", "c_umask"):
            inp(nm, [128, 128])
        inp("c_ipos", [128, 1])
        if with_sample:
            inp("cache_k", [DEPTH, PAST, 256])
            inp("cache_v", [DEPTH, PAST, 256])
            inp("h0s", [DEPTH, 2, 128, 2, 16])
            inp("s0s", [DEPTH, 2, 4, 128, 128])
            inp("rope_c", [128, 8, 128])
            inp("rope_s", [128, 8, 128])
        do = self.dout = {}

        def outp(name, shape):
            do[name] = nc.dram_tensor(name, list(shape), F32, kind="ExternalOutput").ap()

        outp("y", [D, self.TT])
        outp("ck", [DEPTH, self.TP, 256])
        outp("cv", [DEPTH, self.TP, 256])
        outp("hs", [DEPTH, nseq_p, 2, 128, 32])
        outp("ss", [DEPTH, nseq_p, 2, 4, 128, 128])
        self.xs = nc.dram_tensor("xs_scr", [D, self.TT], F32, kind="Internal").ap()
        self.xsb = [Buf(f"xs{i}") for i in range(self.TT // 512)]
        self.hscr = nc.dram_tensor("h_scr", [D, self.TT], BF16, kind="Internal").ap()
        self.hsb = [Buf(f"hs{i}") for i in range(self.TT // 512)]

        A = nc.alloc_sbuf_tensor
        self.ident = A("ident", [128, 128], BF16)
        self.Jm = A("Jm", [128, 128], BF16)
        self.ones = A("ones", [128, 128], BF16)
        self.tpos = A("tpos", [128, 128], F32)
        self.ipos = A("ipos", [128, 1], F32)
        self.diff = A("diff", [128, 128], F32)
        self.lmask = A("lmask", [128, 128], F32)
        self.umask = A("umask", [128, 128], F32)
        self.cst = Buf("cst")
        self.NSLOT = 6
        self.ring = A("ring", [128, self.NSLOT, 4096], BF16)
        self.rbuf = [Buf(f"ring{i}") for i in range(self.NSLOT)]
        self.ri = 0
        self.ps = nc.alloc_psum_tensor("ps", [128, 8, 512], F32)
        self.pb = [Buf(f"psb{i}") for i in range(8)]
        self.modT = A("modT", [128, 144, 2], F32)
        self.bmod = A("bmod", [128, 144], F32)
        self.ngT = A("ngT", [128, 48], F32)
        self.fgT = A("fgT", [128, KC], F32)
        self.Amod = A("Amod", [128, 3, 2, KC], F32)
        self.gate = A("gate", [128, 3, 2, KC], F32)
        self.scb = A("scb", [128, KC, 2], BF16)
        self.cv32 = A("cv32", [128, KC, 2], F32)
        self.modb = Buf("mod")
        self.qg = A("qg", [128, 128], F32)
        if with_sample:
            self.ropec = A("ropec", [128, 8, 128], F32)
            self.ropes = A("ropes", [128, 8, 128], F32)
        self.kg = A("kg", [128, 128], F32)
        self.build()

    def load_f32_via_sp(self, dst_ap, src_ap, buf):
        return self.tr.dma("sp", dst_ap, src_ap, W=[buf])

    def wslot(self, src_ap, nk, ncols):
        s = self.ri
        self.ri = (self.ri + 1) % self.NSLOT
        dst = self.ring[:, s, 0:nk * ncols].rearrange("p (k n) -> p k n", n=ncols)
        self.tr.dma("pool", dst, src_ap, W=[self.rbuf[s]])
        return s

    def wap(self, s, k, ncols, c0, c1):
        return self.ring[:, s, k * ncols + c0:k * ncols + c1]

    def setup_consts(self):
        tr, nc, di = self.tr, self.nc, self.din
        with nc.sbuf_tensor("ctmp", [128, 2, 128], F32) as ctmp:
            tb = Buf("ctmp")
            tr.dma("sp", ctmp[:, 0, :], di["c_ident"], W=[tb])
            tr.dma("sp", ctmp[:, 1, :], di["c_J"], W=[tb])
            tr.op("dve", lambda e: e.tensor_copy(out=self.ident[:], in_=ctmp[:, 0, :]), R=[tb], W=[self.cst])
            tr.op("dve", lambda e: e.tensor_copy(out=self.Jm[:], in_=ctmp[:, 1, :]), R=[tb], W=[self.cst])
            tr.op("dve", lambda e: e.memset(self.ones[:], 1.0), W=[self.cst])
            tr.dma("sp", self.tpos[:], di["c_tpos"], W=[self.cst])
            tr.dma("sp", self.ipos[:], di["c_ipos"], W=[self.cst])
            tr.dma("sp", self.diff[:], di["c_diff"], W=[self.cst])
            tr.dma("sp", self.lmask[:], di["c_lmask"], W=[self.cst])
            tr.dma("sp", self.umask[:], di["c_umask"], W=[self.cst])
            tr.dma("sp", self.fgT[:], di["final_gT"], W=[self.cst])
            tr.dma("sp", self.cv32[:], di["cvec"], W=[self.cst])
            if self.with_sample:
                tr.dma("sp", self.ropec[:], di["rope_c"], W=[self.cst])
                tr.dma("sp", self.ropes[:], di["rope_s"], W=[self.cst])
            tr.op("act", lambda e: e.activation(out=self.scb[:], in_=self.cv32[:], func=AF.Silu),
                  R=[self.cst], W=[self.cst])
            tr.barrier()

    def adaln(self, l):
        tr, nc, di = self.tr, self.nc, self.din
        wm = di["w_mod"][l].rearrange("(kc p) n -> p kc n", p=128)
        tr.dma("sp", self.bmod[:], di["b_modT"][l], W=[self.modb])
        tr.dma("sp", self.ngT[:], di["norm_gT"][l], W=[self.modb])
        tr.dma("sp", self.qg[:], di["qg_rep"][l], W=[self.modb])
        tr.dma("sp", self.kg[:], di["kg_rep"][l], W=[self.modb])
        bank = 7
        pv = self.ps[:, bank, 0:288].rearrange("p (j c) -> p j c", c=2)
        for u in range(72):
            s = self.wslot(wm[:, :, u * 256:(u + 1) * 256], 16, 256)
            for jj in range(2):
                j = u * 2 + jj
                ems = []
                for kc in range(KC):
                    ems.append(lambda e, kc=kc, jj=jj, j=j, s=s: e.matmul(
                        pv[:, j, :], lhsT=self.wap(s, kc, 256, jj * 128, jj * 128 + 128),
                        rhs=self.scb[:, kc, :], start=(kc == 0), stop=(kc == KC - 1)))
                tr.grp("pe", ems, R=[self.rbuf[s], self.cst], W=[self.pb[bank]])
        for c in range(2):
            tr.op("dve", lambda e, c=c: e.tensor_tensor(out=self.modT[:, :, c], in0=pv[:, :, c],
                                                        in1=self.bmod[:], op=ALU.add),
                  R=[self.pb[bank], self.modb], W=[self.modb])
        for sub in range(3):
            for c in range(2):
                sc = self.modT[:, (sub * 3 + 1) * 16:(sub * 3 + 2) * 16, c]
                gt = self.modT[:, (sub * 3 + 2) * 16:(sub * 3 + 3) * 16, c]
                tr.op("dve", lambda e, sub=sub, c=c, sc=sc: e.scalar_tensor_tensor(
                    out=self.Amod[:, sub, c, :], in0=sc, scalar=1.0, in1=self.ngT[:, sub * 16:(sub + 1) * 16],
                    op0=ALU.add, op1=ALU.mult), R=[self.modb], W=[self.modb])
                tr.op("dve", lambda e, sub=sub, c=c, gt=gt: e.tensor_scalar(
                    out=self.gate[:, sub, c, :], in0=gt, scalar1=(1.0 if sub == 1 else 0.5), scalar2=None,
                    op0=ALU.mult), R=[self.modb], W=[self.modb])

    def shift_ap(self, sub, c, kc):
        j = (sub * 3 + 0) * 16 + kc
        return self.modT[:, j, c:c + 1]

    def norm_tile(self, xt, xb, n, out_t, ob, A_ap, shift_fn, work, wb):
        tr = self.tr
        bank = 6
        sq = work[:, 2, :].bitcast(BF16)
        for kc in range(KC):
            h = kc % 2
            tr.op("act", lambda e, kc=kc, h=h: e.activation(out=sq[:, h * 512:h * 512 + n], in_=xt[:, kc, 0:n],
                                                            func=AF.Square), R=[xb], W=[wb[2 + h]])
            tr.op("pe", lambda e, kc=kc, h=h: e.matmul(self.ps[:, bank, 0:n], lhsT=self.ones[:],
                                                       rhs=sq[:, h * 512:h * 512 + n], start=(kc == 0),
                                                       stop=(kc == KC - 1)), R=[wb[2 + h], self.cst], W=[self.pb[bank]])
        rstd = work[:, 0, 0:n]
        tr.op("act", lambda e: e.activation(out=rstd, in_=self.ps[:, bank, 0:n], func=AF.Sqrt,
                                            scale=1.0 / D, bias=self.epsb[:, 0:1]), R=[self.pb[bank], self.cst], W=[wb[0]])
        tr.op("dve", lambda e: e.reciprocal(out=rstd, in_=rstd), R=[wb[0]], W=[wb[0]])
        tmp = work[:, 1, 0:n]
        for kc in range(KC):
            tr.op("dve", lambda e, kc=kc: e.tensor_tensor(out=tmp, in0=xt[:, kc, 0:n], in1=rstd, op=ALU.mult),
                  R=[xb, wb[0]], W=[wb[1]])
            sh = shift_fn(kc) if shift_fn is not None else 0.0
            tr.op("act", lambda e, kc=kc, sh=sh: e.activation(out=out_t[:, kc, 0:n], in_=tmp, func=AF.Identity,
                                                              scale=A_ap(kc), bias=sh),
                  R=[wb[1], self.modb, self.cst], W=[ob])

    def ffn_tile(self, l, f, xt, xb, n, hT, hb, hid, hidb, gate_ap, work, wb):
        tr, di = self.tr, self.din
        win = di["w_ffn_in"][l, f].rearrange("(kc p) n -> p kc n", p=128)
        wout = di["w_ffn_out"][l, f].rearrange("(kc p) n -> p kc n", p=128)
        bi = 0
        for h2 in range(HC // 2):
            sa = self.wslot(win[:, :, h2 * 256:(h2 + 1) * 256], 16, 256)
            su = self.wslot(win[:, :, DFF + h2 * 256:DFF + (h2 + 1) * 256], 16, 256)
            for jj in range(2):
                hc = h2 * 2 + jj
                ba, bu = (bi % 2) * 2, (bi % 2) * 2 + 1
                bi += 1
                for (s, bk) in ((sa, ba), (su, bu)):
                    ems = [lambda e, kc=kc, s=s, bk=bk, jj=jj: e.matmul(
                        self.ps[:, bk, 0:n], lhsT=self.wap(s, kc, 256, jj * 128, jj * 128 + 128),
                        rhs=hT[:, kc, 0:n], start=(kc == 0), stop=(kc == KC - 1)) for kc in range(KC)]
                    tr.grp("pe", ems, R=[self.rbuf[s], hb], W=[self.pb[bk]])
                sl = work[:, bi % 2, 0:n]
                tr.op("act", lambda e, ba=ba, sl=sl: e.activation(out=sl, in_=self.ps[:, ba, 0:n], func=AF.Silu),
                      R=[self.pb[ba]], W=[wb[bi % 2]])
                tr.op("dve", lambda e, bu=bu, sl=sl, hc=hc: e.tensor_tensor(out=hid[:, hc, 0:n], in0=self.ps[:, bu, 0:n],
                                                                            in1=sl, op=ALU.mult),
                      R=[self.pb[bu], wb[bi % 2]], W=[hidb])
        for o2 in range(8):
            slots = [self.wslot(wout[:, q * 11:(q + 1) * 11, o2 * 256:(o2 + 1) * 256], 11, 256) for q in range(4)]
            for jj in range(2):
                oc = o2 * 2 + jj
                bk = 4 + (oc % 2)
                ems = []
                for q in range(4):
                    for k in range(11):
                        kk = q * 11 + k
                        ems.append(lambda e, q=q, k=k, kk=kk, jj=jj, bk=bk: e.matmul(
                            self.ps[:, bk, 0:n], lhsT=self.wap(slots[q], k, 256, jj * 128, jj * 128 + 128),
                            rhs=hid[:, kk, 0:n], start=(kk == 0), stop=(kk == HC - 1)))
                tr.grp("pe", ems, R=[self.rbuf[s] for s in slots] + [hidb], W=[self.pb[bk]])
                tr.op("dve", lambda e, oc=oc, bk=bk: e.scalar_tensor_tensor(
                    out=xt[:, oc, 0:n], in0=self.ps[:, bk, 0:n], scalar=gate_ap(oc), in1=xt[:, oc, 0:n],
                    op0=ALU.mult, op1=ALU.add), R=[self.pb[bk], self.modb, xb], W=[xb])

    def proj_tok(self, slots, hT, hb, t0, bank):
        tr = self.tr
        ems = [lambda e, kc=kc: e.matmul(self.ps[:, bank, :], lhsT=hT[:, kc, t0:t0 + 128],
                                         rhs=self.wap(slots[kc // 8], kc % 8, 512, 0, 512),
                                         start=(kc == 0), stop=(kc == KC - 1)) for kc in range(KC)]
        tr.grp("pe", ems, R=[self.rbuf[s] for s in slots] + [hb], W=[self.pb[bank]])

    def w_in_slots(self, l, c0):
        wv = self.din["w_in"][l].rearrange("(kc p) n -> p kc n", p=128)
        return [self.wslot(wv[:, h * 8:(h + 1) * 8, c0:c0 + 512], 8, 512) for h in range(2)]

    def transpose_to(self, src_tok, sb, ncol_blocks, dst_fn, db, bank, rev=False):
        tr = self.tr
        X = self.Jm if rev else self.ident
        ems = [lambda e, i=i: e.matmul(self.ps[:, bank, i * 128:(i + 1) * 128], lhsT=src_tok[:, i * 128:(i + 1) * 128],
                                       rhs=X[:], start=True, stop=True) for i in range(ncol_blocks)]
        tr.grp("pe", ems, R=[sb, self.cst], W=[self.pb[bank]])
        for i in range(ncol_blocks):
            tr.op("act", lambda e, i=i: e.activation(out=dst_fn(i), in_=self.ps[:, bank, i * 128:(i + 1) * 128],
                                                     func=AF.Copy), R=[self.pb[bank]], W=[db])

    def load_norm(self, xsrc, t0, n, xt, xb, hT, hb, sub, c, work, wb):
        self.tr.dma("sp", xt[:, :, 0:n], xsrc.rearrange("(kc p) t -> p kc t", p=128)[:, :, t0:t0 + n], R=[self.xsb[t0 // 512]], W=[xb])
        self.norm_tile(xt, xb, n, hT, hb, lambda kc: self.Amod[:, sub, c, kc:kc + 1],
                       lambda kc: self.shift_ap(sub, c, kc), work, wb)

    def s5_prep(self, l, d, S):
        tr, di = self.tr, self.din
        pb = S["pbuf"]
        abd = S["abd"]
        tr.dma("sp", abd[:], di["ssm_abd"][l, d], W=[pb])
        tr.wait_fence("pool")
        tr.dma("pool", S["BT"][:].rearrange("p (c n) -> p c n", c=2), di["ssm_BT"][l, d].rearrange("c p n -> p c n"), W=[pb])
        sm = S["sm"]
        V = lambda i: sm[:, i, :]
        a_re, a_im, ldt = abd[:, 0, :], abd[:, 1, :], abd[:, 2, :]
        dt, lr, th, r, cth, sth, kre, kim = V(0), V(1), V(2), V(3), V(4), V(5), V(6), V(7)
        t1, t2, t3, den = V(8), V(9), V(10), V(11)
        c128, s128, kcl, ksl = V(12), V(13), V(14), V(15)
        thr = V(16)
        O = lambda f, **kw: tr.op("dve", f, R=[pb], W=[pb], **kw)
        Aop = lambda f: tr.op("act", f, R=[pb, self.cst], W=[pb])
        TWO_PI = 2.0 * np.pi
        MAGIC = 12582912.0

        def sincos(th_ap, s_out, c_out, shape_tmp1, shape_tmp2):
            O(lambda e: e.tensor_scalar(out=shape_tmp1, in0=th_ap, scalar1=1.0 / TWO_PI, scalar2=MAGIC, op0=ALU.mult, op1=ALU.add))
            O(lambda e: e.tensor_scalar(out=shape_tmp1, in0=shape_tmp1, scalar1=MAGIC, scalar2=-TWO_PI, op0=ALU.subtract, op1=ALU.mult))
            O(lambda e: e.tensor_tensor(out=shape_tmp1, in0=shape_tmp1, in1=th_ap, op=ALU.add))
            O(lambda e: e.tensor_scalar(out=shape_tmp1, in0=shape_tmp1, scalar1=PI_SAFE, scalar2=-PI_SAFE, op0=ALU.min, op1=ALU.max))
            Aop(lambda e: e.activation(out=s_out, in_=shape_tmp1, func=AF.Sin))
            O(lambda e: e.tensor_scalar(out=shape_tmp2, in0=shape_tmp1, scalar1=np.pi / 2, scalar2=None, op0=ALU.add))
            O(lambda e: e.tensor_scalar(out=shape_tmp1, in0=shape_tmp2, scalar1=np.pi, scalar2=-TWO_PI, op0=ALU.is_gt, op1=ALU.mult))
            O(lambda e: e.tensor_tensor(out=shape_tmp2, in0=shape_tmp2, in1=shape_tmp1, op=ALU.add))
            O(lambda e: e.tensor_scalar(out=shape_tmp2, in0=shape_tmp2, scalar1=PI_SAFE, scalar2=-PI_SAFE, op0=ALU.min, op1=ALU.max))
            Aop(lambda e: e.activation(out=c_out, in_=shape_tmp2, func=AF.Sin))

        Aop(lambda e: e.activation(out=dt, in_=ldt, func=AF.Exp))
        O(lambda e: e.tensor_tensor(out=lr, in0=a_re, in1=dt, op=ALU.mult))
        O(lambda e: e.tensor_tensor(out=th, in0=a_im, in1=dt, op=ALU.mult))
        Aop(lambda e: e.activation(out=r, in_=lr, func=AF.Exp))
        sincos(th, sth, cth, t1, t2)
        O(lambda e: e.tensor_tensor(out=t1, in0=r, in1=cth, op=ALU.mult))
        O(lambda e: e.tensor_scalar(out=t1, in0=t1, scalar1=-1.0, scalar2=None, op0=ALU.add))
        O(lambda e: e.tensor_tensor(out=t2, in0=r, in1=sth, op=ALU.mult))
        O(lambda e: e.tensor_tensor(out=den, in0=a_re, in1=a_re, op=ALU.mult))
        O(lambda e: e.tensor_tensor(out=t3, in0=a_im, in1=a_im, op=ALU.mult))
        O(lambda e: e.tensor_tensor(out=den, in0=den, in1=t3, op=ALU.add))
        O(lambda e: e.reciprocal(out=den, in_=den))
        O(lambda e: e.tensor_tensor(out=kre, in0=t1, in1=a_re, op=ALU.mult))
        O(lambda e: e.tensor_tensor(out=t3, in0=t2, in1=a_im, op=ALU.mult))
        O(lambda e: e.tensor_tensor(out=kre, in0=kre, in1=t3, op=ALU.add))
        O(lambda e: e.tensor_tensor(out=kre, in0=kre, in1=den, op=ALU.mult))
        O(lambda e: e.tensor_tensor(out=kim, in0=t2, in1=a_re, op=ALU.mult))
        O(lambda e: e.tensor_tensor(out=t3, in0=t1, in1=a_im, op=ALU.mult))
        O(lambda e: e.tensor_tensor(out=kim, in0=kim, in1=t3, op=ALU.subtract))
        O(lambda e: e.tensor_tensor(out=kim, in0=kim, in1=den, op=ALU.mult))
        cosT, sinT, rtab = S["cosT"], S["sinT"], S["rtab"]
        tA, tB = S["tA"], S["tB"]
        for j in range(16):
            O(lambda e, j=j: e.tensor_scalar(out=tB[:, j * 128:(j + 1) * 128], in0=self.tpos[:], scalar1=th[:, j:j + 1],
                                             scalar2=None, op0=ALU.mult))
            O(lambda e, j=j: e.tensor_scalar(out=rtab[:, j * 128:(j + 1) * 128], in0=self.lmask[:, 0:128], scalar1=0.0,
                                             scalar2=r[:, j:j + 1], op0=ALU.mult, op1=ALU.add))
            O(lambda e, j=j: e.memset(rtab[:, j * 128:j * 128 + 1], 0.0))
        sincos(tB[:], sinT[:], cosT[:], tA[:], tB[:])
        O(lambda e: e.tensor_scalar(out=thr, in0=th, scalar1=128.0, scalar2=None, op0=ALU.mult))
        sincos(thr, s128, c128, t1, t2)
        O(lambda e: e.tensor_copy(out=kcl, in_=cosT[:].rearrange("p (j t) -> p j t", t=128)[:, :, 127]))
        O(lambda e: e.tensor_copy(out=ksl, in_=sinT[:].rearrange("p (j t) -> p j t", t=128)[:, :, 127]))
        O(lambda e: e.tensor_tensor(out=t1, in0=kre, in1=kcl, op=ALU.mult))
        O(lambda e: e.tensor_tensor(out=t2, in0=kim, in1=ksl, op=ALU.mult))
        O(lambda e: e.tensor_tensor(out=t3, in0=kre, in1=ksl, op=ALU.mult))
        O(lambda e: e.tensor_tensor(out=ksl, in0=kim, in1=kcl, op=ALU.mult))
        O(lambda e: e.tensor_tensor(out=kcl, in0=t1, in1=t2, op=ALU.subtract))
        O(lambda e: e.tensor_tensor(out=ksl, in0=ksl, in1=t3, op=ALU.add))
        nkim = V(17)
        O(lambda e: e.tensor_scalar(out=nkim, in0=kim, scalar1=-1.0, scalar2=None, op0=ALU.mult))
        CT32, CTb = S["CT32"], S["CTb"]
        tr.dma("sp", CT32[:].rearrange("p (c n) -> p c n", c=2), di["ssm_CT"][l, d].rearrange("c p n -> p c n"), R=[pb], W=[pb])
        for j in range(16):
            cre = CT32[:, j * 128:(j + 1) * 128]
            cim = CT32[:, 2048 + j * 128:2048 + (j + 1) * 128]
            u1, u2 = S["u12"][:, 0:128], S["u12"][:, 128:256]
            O(lambda e, j=j, cim=cim: e.tensor_scalar(out=u1, in0=cim, scalar1=kim[:, j:j + 1], scalar2=None, op0=ALU.mult))
            O(lambda e, j=j, cre=cre: e.scalar_tensor_tensor(out=CTb[:, j * 128:(j + 1) * 128], in0=cre, scalar=kre[:, j:j + 1],
                                                             in1=u1, op0=ALU.mult, op1=ALU.subtract))
            O(lambda e, j=j, cim=cim: e.tensor_scalar(out=u2, in0=cim, scalar1=kre[:, j:j + 1], scalar2=-1.0, op0=ALU.mult, op1=ALU.mult))
            O(lambda e, j=j, cre=cre: e.scalar_tensor_tensor(out=CTb[:, 2048 + j * 128:2048 + (j + 1) * 128], in0=cre,
                                                             scalar=nkim[:, j:j + 1], in1=u2, op0=ALU.mult, op1=ALU.add))
            O(lambda e, j=j: e.tensor_scalar(out=CTb[:, 4096 + j * 128:4096 + (j + 1) * 128], in0=CTb[:, j * 128:(j + 1) * 128],
                                             scalar1=-1.0, scalar2=None, op0=ALU.mult))

    def uniq(self):
        self._uid = getattr(self, "_uid", 0) + 1
        return self._uid

    def alloc(self, es, name, shape, dt):
        return es.enter_context(self.nc.sbuf_tensor(f"{name}_{self.uniq()}", list(shape), dt))

    def proj_phase(self, l, G, cols, consume, pre=None, cache="write"):
        nc, tr = self.nc, self.tr
        T = G["nseq"] * G["L"]
        hsv = self.hscr.rearrange("(kc p) t -> p kc t", p=128)
        with ExitStack() as es:
            hT = self.alloc(es, "hTm", [128, KC, 512], BF16)
            hb = Buf()
            if cache == "write":
                xt = self.alloc(es, "xtm", [128, KC, 512], F32)
                work = self.alloc(es, "wkm", [128, 3, 512], F32)
                xb = Buf()
                wb = [Buf() for _ in range(4)]
            if pre is not None:
                pre()
            bi = 0
            for ti in range(T // 512):
                g0 = G["t0"] + ti * 512
                if cache == "write":
                    self.load_norm(self.xs, g0, 512, xt, xb, hT, hb, 1, G["c"], work, wb)
                    tr.dma("sp", hsv[:, :, g0:g0 + 512], hT[:], R=[hb], W=[self.hsb[g0 // 512]])
                else:
                    tr.dma("sp", hT[:], hsv[:, :, g0:g0 + 512], R=[self.hsb[g0 // 512]], W=[hb])
                for cb, c0 in enumerate(cols):
                    slots = self.w_in_slots(l, c0)
                    for tt in range(4):
                        bank = bi % 4
                        bi += 1
                        self.proj_tok(slots, hT, hb, tt * 128, bank)
                        consume(cb, ti * 4 + tt, bank)
            tr.barrier()

    def rope_tok(self, x_ap, xbuf, nh, ttg, tmp_ap, tbuf):
        tr = self.tr
        cs = self.ropec[:, ttg, :]
        sn = self.ropes[:, ttg, :]
        for h in range(nh):
            xh = x_ap[:, h * 128:(h + 1) * 128]
            xv = xh.rearrange("p (a b f) -> p a b f", a=2, b=2)
            tv = tmp_ap[:, 0:128].rearrange("p (a b f) -> p a b f", a=2, b=2)
            sv = sn.rearrange("p (a b f) -> p a b f", a=2, b=2)
            tr.op("dve", lambda e, xv=xv, tv=tv, sv=sv: e.tensor_tensor(out=tv[:, :, 0, :], in0=xv[:, :, 1, :], in1=sv[:, :, 0, :], op=ALU.mult),
                  R=[xbuf, self.cst], W=[tbuf])
            tr.op("dve", lambda e, xv=xv, tv=tv, sv=sv: e.tensor_tensor(out=tv[:, :, 1, :], in0=xv[:, :, 0, :], in1=sv[:, :, 1, :], op=ALU.mult),
                  R=[xbuf, self.cst], W=[tbuf])
            tr.op("dve", lambda e, xh=xh: e.tensor_tensor(out=xh, in0=xh, in1=cs, op=ALU.mult), R=[self.cst, tbuf], W=[xbuf])
            tr.op("dve", lambda e, xh=xh: e.tensor_tensor(out=xh, in0=xh, in1=tmp_ap[:, 0:128], op=ALU.add), R=[tbuf], W=[xbuf])

    def mix_s5(self, l, G, mixT, mb):
        nc, tr, di = self.nc, self.tr, self.din
        nseq, L = G["nseq"], G["L"]
        T = nseq * L
        NTT, nch = T // 128, L // 128
        with ExitStack() as es0:
            u_tok = self.alloc(es0, "utok", [128, NTT, 512], BF16)
            ub = Buf()
            self.proj_phase(l, G, [1536], lambda cb, ttg, bank: tr.op(
                "act", lambda e: e.activation(out=u_tok[:, ttg, :], in_=self.ps[:, bank, :], func=AF.Copy),
                R=[self.pb[bank]], W=[ub]))
            with ExitStack() as es:
                A = lambda n, sh, dt: self.alloc(es, n, sh, dt)
                cosT, sinT, rtab = A("cosT", [128, 2048], F32), A("sinT", [128, 2048], F32), A("rtab", [128, 2048], F32)
                W5 = A("W5", [128, 5120], F32)
                BT, CTb = A("BT", [128, 4096], BF16), A("CTb", [128, 6144], BF16)
                _xa, _xb, _xc, _xd = A("xpa", [128, 1024], BF16), A("xpb", [128, 1024], BF16), A("xpc", [128, 1024], BF16), A("xpd", [128, 1024], BF16)
                xre2, xim2, xre_b, xim_b = [_xa, _xa], [_xb, _xb], [_xc, _xc], [_xd, _xd]
                _xbuf = Buf()
                xbf2 = [_xbuf, _xbuf]
                uT2 = [A("uT", [128, 512], BF16), A("uT", [128, 512], BF16)]
                uTb2 = [Buf(), Buf()]
                ybf = Buf()
                ytokf = A("ytokf", [128, NTT, 512], BF16)
                ytb = A("ytb", [128, 512], BF16)
                gT = A("gT", [128, 4, T], BF16)
                gtok = A("gtok", [128, 512], BF16)
                abd, sm = A("abd", [128, 3, 16], F32), A("sm", [128, 24, 16], F32)
                car = A("car", [128, 4, 16], F32)
                hfin = A("hfin", [128, 16, 2], F32)
                drep = A("drep", [128, 512], F32)
                h0t = A("h0t", [128, 2, 16], F32)
                pbuf, wbf, xbf, uTb, yfb, ytbb, gTb, gtb, carb, hfb, drb = [Buf() for _ in range(11)]
                tr.dma("sp", drep[:], di["ssm_d_rep"][l], W=[drb])
                S = dict(pbuf=pbuf, abd=abd, sm=sm, BT=BT, CTb=CTb, cosT=cosT, sinT=sinT, rtab=rtab,
                         tA=W5[:, 0:2048], tB=W5[:, 2048:4096], CT32=W5[:, 0:4096], u12=W5[:, 4096:4352])
                V = lambda i: sm[:, i, :]
                r_, cth, sth, kre, kim = V(3), V(4), V(5), V(6), V(7)
                c128, s128, krl_re, krl_im = V(12), V(13), V(14), V(15)
                D_re, D_im, W_re, W_im, T1 = [W5[:, i * 1024:(i + 1) * 1024] for i in range(5)]
                rcr, rci = car[:, 0, :], car[:, 1, :]
                YB = T1
                O = lambda f, R, W: tr.op("dve", f, R=R, W=W)
                TT = lambda e, o, a, b, op: e.tensor_tensor(out=o, in0=a, in1=b, op=op)
                for d in range(2):
                    tr.barrier()
                    self.s5_prep(l, d, S)
                    for s in range(nseq):
                        if G["sample"]:
                            tr.dma("sp", h0t[:], di["h0s"][l, d], W=[carb])
                            q1, q2, q3, q4 = V(18), V(19), V(20), V(21)
                            h_re, h_im = h0t[:, 0, :], h0t[:, 1, :]
                            R_, W_ = [pbuf, carb], [pbuf, carb]
                            O(lambda e: TT(e, q1, kre, kre, ALU.mult), R_, W_)
                            O(lambda e: TT(e, q2, kim, kim, ALU.mult), R_, W_)
                            O(lambda e: TT(e, q1, q1, q2, ALU.add), R_, W_)
                            O(lambda e: e.reciprocal(out=q1, in_=q1), R_, W_)
                            O(lambda e: TT(e, q2, h_re, kre, ALU.mult), R_, W_)
                            O(lambda e: TT(e, q3, h_im, kim, ALU.mult), R_, W_)
                            O(lambda e: TT(e, q2, q2, q3, ALU.add), R_, W_)
                            O(lambda e: TT(e, q2, q2, q1, ALU.mult), R_, W_)
                            O(lambda e: TT(e, q3, h_im, kre, ALU.mult), R_, W_)
                            O(lambda e: TT(e, q4, h_re, kim, ALU.mult), R_, W_)
                            O(lambda e: TT(e, q3, q3, q4, ALU.subtract), R_, W_)
                            O(lambda e: TT(e, q3, q3, q1, ALU.mult), R_, W_)
                            O(lambda e: TT(e, q1, r_, cth, ALU.mult), R_, W_)
                            O(lambda e: TT(e, q4, r_, sth, ALU.mult), R_, W_)
                            O(lambda e: TT(e, rcr, q1, q2, ALU.mult), R_, W_)
                            O(lambda e: TT(e, rci, q4, q3, ALU.mult), R_, W_)
                            O(lambda e: TT(e, rcr, rcr, rci, ALU.subtract), R_, W_)
                            O(lambda e: TT(e, rci, q1, q3, ALU.mult), R_, W_)
                            O(lambda e: TT(e, q1, q4, q2, ALU.mult), R_, W_)
                            O(lambda e: TT(e, rci, rci, q1, ALU.add), R_, W_)
                        else:
                            O(lambda e: e.memset(car[:, 0:2, :], 0.0), [], [carb])
                        order = list(range(nch)) if d == 0 else list(range(nch - 1, -1, -1))
                        items = [(oi, ci, hh) for oi, ci in enumerate(order) for hh in range(2)]
                        v3 = lambda ap: ap.rearrange("p (a b) -> p a b", b=512)
                        j3 = lambda ap: ap.rearrange("p (j t) -> p j t", t=128)

                        def stage_bu(item):
                            oi, ci, hh = item
                            ttg = s * nch + ci
                            uTc = uT2[oi % 2]
                            if hh == 0:
                                self.transpose_to(u_tok[:, ttg, :], ub, 4, lambda i: uTc[:, i * 128:(i + 1) * 128], uTb2[oi % 2], 0, rev=(d == 1))
                            ems = []
                            for c2 in range(2):
                                for jj in range(8):
                                    j = 8 * hh + jj
                                    ems.append(lambda e, c2=c2, jj=jj, j=j: e.matmul(
                                        self.ps[:, 1 + 2 * c2 + jj // 4, (jj % 4) * 128:(jj % 4 + 1) * 128],
                                        lhsT=BT[:, c2 * 2048 + j * 128:c2 * 2048 + (j + 1) * 128],
                                        rhs=uTc[:, (j // 4) * 128:(j // 4 + 1) * 128], start=True, stop=True))
                            tr.grp("pe", ems, R=[pbuf, uTb2[oi % 2]], W=[self.pb[1], self.pb[2], self.pb[3], self.pb[4]])

                        def stage_derot(item):
                            oi, ci, hh = item
                            bre, bim = self.ps[:, 1:3, :], self.ps[:, 3:5, :]
                            cs = v3(cosT[:, hh * 1024:(hh + 1) * 1024])
                            sn = v3(sinT[:, hh * 1024:(hh + 1) * 1024])
                            Rp = [self.pb[1], self.pb[2], self.pb[3], self.pb[4], pbuf, wbf]
                            O(lambda e: TT(e, v3(T1), bre, cs, ALU.mult), Rp, [wbf])
                            O(lambda e: TT(e, v3(D_re), bim, sn, ALU.mult), Rp, [wbf])
                            O(lambda e: TT(e, v3(D_im), bre, sn, ALU.mult), Rp, [wbf])
                            O(lambda e: TT(e, v3(W_im), bim, cs, ALU.mult), Rp, [wbf])
                            O(lambda e: TT(e, D_re, D_re, T1, ALU.add), [wbf], [wbf])
                            O(lambda e: TT(e, D_im, W_im, D_im, ALU.subtract), [wbf], [wbf])
                            O(lambda e: TT(e, j3(D_re)[:, :, 0], j3(D_re)[:, :, 0], rcr[:, 8 * hh:8 * hh + 8], ALU.add), [wbf, carb], [wbf])
                            O(lambda e: TT(e, j3(D_im)[:, :, 0], j3(D_im)[:, :, 0], rci[:, 8 * hh:8 * hh + 8], ALU.add), [wbf, carb], [wbf])

                        def stage_rest(item):
                            oi, ci, hh = item
                            xre, xim, xbf = xre2[hh], xim2[hh], xbf2[hh]
                            rt = rtab[:, hh * 1024:(hh + 1) * 1024]
                            O(lambda e: e.tensor_tensor_scan(out=W_re, data0=rt, data1=D_re, initial=0.0, op0=ALU.mult, op1=ALU.add), [wbf, pbuf], [wbf])
                            O(lambda e: e.tensor_tensor_scan(out=W_im, data0=rt, data1=D_im, initial=0.0, op0=ALU.mult, op1=ALU.add), [wbf, pbuf], [wbf])
                            lr_, li_ = j3(W_re)[:, :, 127], j3(W_im)[:, :, 127]
                            hs_ = slice(8 * hh, 8 * hh + 8)
                            a1, a2 = car[:, 2, hs_], car[:, 3, hs_]
                            Rc, Wc = [wbf, carb, pbuf], [carb]
                            if oi == nch - 1 and not G["sample"]:
                                hv = hfin[:, hs_, :]
                                O(lambda e: TT(e, a1, lr_, krl_re[:, hs_], ALU.mult), Rc, Wc)
                                O(lambda e: TT(e, a2, li_, krl_im[:, hs_], ALU.mult), Rc, Wc)
                                O(lambda e: TT(e, hv[:, :, 0], a1, a2, ALU.subtract), Rc, [hfb])
                                O(lambda e: TT(e, a1, li_, krl_re[:, hs_], ALU.mult), Rc, Wc)
                                O(lambda e: TT(e, a2, lr_, krl_im[:, hs_], ALU.mult), Rc, Wc)
                                O(lambda e: TT(e, hv[:, :, 1], a1, a2, ALU.add), Rc, [hfb])
                            if oi < nch - 1:
                                O(lambda e: TT(e, a1, lr_, c128[:, hs_], ALU.mult), Rc, Wc)
                                O(lambda e: TT(e, a2, li_, s128[:, hs_], ALU.mult), Rc, Wc)
                                O(lambda e: TT(e, rcr[:, hs_], a1, a2, ALU.subtract), Rc, Wc)
                                O(lambda e: TT(e, a1, lr_, s128[:, hs_], ALU.mult), Rc, Wc)
                                O(lambda e: TT(e, a2, li_, c128[:, hs_], ALU.mult), Rc, Wc)
                                O(lambda e: TT(e, rci[:, hs_], a1, a2, ALU.add), Rc, Wc)
                                O(lambda e: TT(e, rcr[:, hs_], rcr[:, hs_], r_[:, hs_], ALU.mult), Rc, Wc)
                                O(lambda e: TT(e, rci[:, hs_], rci[:, hs_], r_[:, hs_], ALU.mult), Rc, Wc)
                            csf, snf = cosT[:, hh * 1024:(hh + 1) * 1024], sinT[:, hh * 1024:(hh + 1) * 1024]
                            pA, pB = xre[:], xim[:]
                            pC, pD = xre_b[hh][:], xim_b[hh][:]
                            O(lambda e: TT(e, pA, W_re, csf, ALU.mult), [wbf, pbuf], [xbf])
                            O(lambda e: TT(e, pB, W_im, snf, ALU.mult), [wbf, pbuf], [xbf])
                            O(lambda e: TT(e, pC, W_re, snf, ALU.mult), [wbf, pbuf], [xbf])
                            O(lambda e: TT(e, pD, W_im, csf, ALU.mult), [wbf, pbuf], [xbf])
                            ems = []
                            for cc in range(2):
                                c = 2 * hh + cc
                                for q in range(4):
                                    jj = 4 * cc + q
                                    j = 8 * hh + jj
                                    for n_, (tb, xs_) in enumerate(((0, xre), (2, xim), (1, xre_b[hh]), (1, xim_b[hh]))):
                                        ems.append(lambda e, c=c, q=q, jj=jj, j=j, tb=tb, xs_=xs_, n_=n_: e.matmul(
                                            self.ps[:, 5, c * 128:(c + 1) * 128], lhsT=xs_[:, jj * 128:(jj + 1) * 128],
                                            rhs=CTb[:, tb * 2048 + j * 128:tb * 2048 + (j + 1) * 128],
                                            start=(q == 0 and n_ == 0), stop=(q == 3 and n_ == 3)))
                            tr.grp("pe", ems, R=[xbf, pbuf], W=[self.pb[5]])
                            if hh == 1:
                                ttg = s * nch + ci
                                if d == 0:
                                    tr.op("act", lambda e: e.activation(out=ytokf[:, ttg, :], in_=self.ps[:, 5, :], func=AF.Copy),
                                          R=[self.pb[5]], W=[yfb])
                                else:
                                    tr.op("act", lambda e: e.activation(out=ytb[:], in_=self.ps[:, 5, :], func=AF.Copy), R=[self.pb[5]], W=[ytbb])
                                    tr.grp("pe", [lambda e: e.matmul(self.ps[:, 6, :], lhsT=self.ident[:], rhs=ytokf[:, ttg, :], start=True, stop=False),
                                                  lambda e: e.matmul(self.ps[:, 6, :], lhsT=self.Jm[:], rhs=ytb[:], start=False, stop=True)],
                                           R=[yfb, ytbb, self.cst], W=[self.pb[6]])

                        def stage_final(item):
                            oi, ci, hh = item
                            if hh != 1 or d == 0:
                                return
                            ttg = s * nch + ci
                            Y, Y2 = YB[:, 0:512], YB[:, 512:1024]
                            O(lambda e: TT(e, Y, u_tok[:, ttg, :], drep[:], ALU.mult), [ub, drb, ybf], [ybf])
                            O(lambda e: TT(e, Y, Y, self.ps[:, 6, :], ALU.add), [ybf, self.pb[6]], [ybf])
                            O(lambda e: TT(e, Y2, Y, Y, ALU.mult), [ybf], [ybf])
                            O(lambda e: e.tensor_scalar(out=Y2, in0=Y2, scalar1=0.044715, scalar2=1.0, op0=ALU.mult, op1=ALU.add), [ybf], [ybf])
                            O(lambda e: TT(e, Y2, Y2, Y, ALU.mult), [ybf], [ybf])
                            tr.op("act", lambda e: e.activation(out=Y2, in_=Y2, func=AF.Sigmoid, scale=1.5957691216057308), R=[ybf], W=[ybf])
                            O(lambda e: TT(e, gtok[:], Y, Y2, ALU.mult), [ybf], [gtb])
                            self.transpose_to(gtok[:], gtb, 4, lambda i: gT[:, i, ttg * 128:(ttg + 1) * 128], gTb, 7)

                        stage_bu(items[0])
                        pend = None
                        for k, item in enumerate(items):
                            stage_derot(item)
                            if k + 1 < len(items):
                                stage_bu(items[k + 1])
                            if pend is not None:
                                stage_final(pend)
                                pend = None
                            stage_rest(item)
                            pend = item
                        stage_final(pend)
                        if not G["sample"]:
                            tr.dma("sp", self.dout["hs"][l, s, d], hfin[:].rearrange("p j c -> p (j c)"), R=[hfb], is_out=True)
                wg = di["w_glu"][l].rearrange("(kc p) n -> p kc n", p=128)
                gs = [self.wslot(wg[:, :, h * 512:(h + 1) * 512], 4, 512) for h in range(2)]
                for t0 in range(0, T, 512):
                    for oc in range(4):
                        for half, bank in ((0, oc % 2), (1, 2 + oc % 2)):
                            ems = [lambda e, kc=kc, half=half, bank=bank, oc=oc: e.matmul(
                                self.ps[:, bank, :], lhsT=self.wap(gs[half], kc, 512, oc * 128, (oc + 1) * 128),
                                rhs=gT[:, kc, t0:t0 + 512], start=(kc == 0), stop=(kc == 3)) for kc in range(4)]
                            tr.grp("pe", ems, R=[self.rbuf[gs[half]], gTb], W=[self.pb[bank]])
                        sg = W5[:, 0:512]
                        tr.op("act", lambda e, oc=oc: e.activation(out=sg, in_=self.ps[:, 2 + oc % 2, :], func=AF.Sigmoid), R=[self.pb[2 + oc % 2], wbf], W=[wbf])
                        O(lambda e, oc=oc: TT(e, mixT[:, 8 + oc, t0:t0 + 512], self.ps[:, oc % 2, :], sg, ALU.mult), [self.pb[oc % 2], wbf], [mb])
                tr.barrier()

    def headnorm(self, bank, nh, gtab, out_ap, obuf, sqt, ssm_, sbuf_):
        tr = self.tr
        tr.op("act", lambda e: e.activation(out=sqt[:, 0:nh * 128], in_=self.ps[:, bank, 0:nh * 128], func=AF.Square),
              R=[self.pb[bank]], W=[sbuf_])
        tr.op("dve", lambda e: e.tensor_reduce(out=ssm_[:, 0:nh], in_=sqt[:, 0:nh * 128].rearrange("p (h d) -> p h d", d=128),
                                               axis=AX.X, op=ALU.add), R=[sbuf_], W=[sbuf_])
        tr.op("act", lambda e: e.activation(out=ssm_[:, 0:nh], in_=ssm_[:, 0:nh], func=AF.Sqrt, scale=1.0 / 128, bias=self.epsb[:, 0:1]),
              R=[sbuf_, self.cst], W=[sbuf_])
        tr.op("dve", lambda e: e.reciprocal(out=ssm_[:, 0:nh], in_=ssm_[:, 0:nh]), R=[sbuf_], W=[sbuf_])
        for h in range(nh):
            tr.op("dve", lambda e, h=h: e.scalar_tensor_tensor(out=out_ap[:, h * 128:(h + 1) * 128], in0=self.ps[:, bank, h * 128:(h + 1) * 128],
                                                               scalar=ssm_[:, h:h + 1], in1=gtab[:], op0=ALU.mult, op1=ALU.mult),
                  R=[self.pb[bank], sbuf_, self.modb], W=[obuf])

    def mix_attn(self, l, G, mixT, mb):
        nc, tr, di = self.nc, self.tr, self.din
        nseq, L, smp = G["nseq"], G["L"], G["sample"]
        T = nseq * L
        NTT = T // 128
        Lk = L + (PAST if smp else 0)
        koff = PAST if smp else 0
        with ExitStack() as es0:
            A0 = lambda n, sh, dt: self.alloc(es0, n, sh, dt)
            qT = A0("qT", [128, 8, T], BF16)
            kT = A0("kT", [128, 2, nseq * Lk], BF16)
            v_tok = A0("vtok", [128, nseq * Lk // 128, 256], BF16)
            qTb, kTb, vb = Buf(), Buf(), Buf()
            with ExitStack() as es:
                A = lambda n, sh, dt: self.alloc(es, n, sh, dt)
                SETS = []
                for _ in range(2):
                    SETS.append(dict(q_tok=A("qtok", [128, 512], BF16), kst=A("kst", [128, 512], F32), kbf=A("kbf", [128, 256], BF16),
                                     sqt=A("sqt", [128, 512], F32), ssm_=A("ssms", [128, 8], F32), rtmp=A("rtmp", [128, 128], F32),
                                     qtb=Buf(), kstb=Buf(), kbb=Buf(), sqb=Buf(), rtb=Buf()))
                kst, kbf, kstb, kbb = SETS[0]["kst"], SETS[0]["kbf"], SETS[0]["kstb"], SETS[0]["kbb"]
                cnt = [0]

                def pre():
                    if not smp:
                        return
                    for i in range(PAST // 128):
                        tr.dma("sp", kst[:, 0:256], di["cache_k"][l, i * 128:(i + 1) * 128, :], W=[kstb])
                        tr.dma("sp", kst[:, 256:512], di["cache_v"][l, i * 128:(i + 1) * 128, :], W=[kstb])
                        tr.op("act", lambda e: e.activation(out=kbf[:], in_=kst[:, 0:256], func=AF.Copy), R=[kstb], W=[kbb])
                        tr.op("dve", lambda e, i=i: e.tensor_copy(out=v_tok[:, i, :], in_=kst[:, 256:512]), R=[kstb], W=[vb])
                        self.transpose_to(kbf[:], kbb, 2, lambda h, i=i: kT[:, h, i * 128:(i + 1) * 128], kTb, 5)

                def consume(cb, ttg, bank):
                    s_, tl = divmod(ttg * 128, L)
                    par = cnt[0] % 2
                    cnt[0] += 1
                    Z = SETS[par]
                    q_tok, kst, kbf, sqt, ssm_, rtmp = Z["q_tok"], Z["kst"], Z["kbf"], Z["sqt"], Z["ssm_"], Z["rtmp"]
                    qtb, kstb, kbb, sqb, rtb = Z["qtb"], Z["kstb"], Z["kbb"], Z["sqb"], Z["rtb"]
                    if cb < 2:
                        self.headnorm(bank, 4, self.qg, q_tok[:], qtb, sqt, ssm_, sqb)
                        if smp:
                            self.rope_tok(q_tok[:], qtb, 4, ttg, rtmp[:], rtb)
                        self.transpose_to(q_tok[:], qtb, 4, lambda h: qT[:, cb * 4 + h, ttg * 128:(ttg + 1) * 128], qTb, 4 + par)
                    else:
                        self.headnorm(bank, 2, self.kg, kst[:, 0:256], kstb, sqt, ssm_, sqb)
                        tr.op("act", lambda e: e.activation(out=kst[:, 256:512], in_=self.ps[:, bank, 256:512], func=AF.Copy),
                              R=[self.pb[bank]], W=[kstb])
                        if smp:
                            self.rope_tok(kst[:, 0:256], kstb, 2, ttg, rtmp[:], rtb)
                        else:
                            tr.dma("sp", self.dout["ck"][l, ttg * 128:(ttg + 1) * 128, :], kst[:, 0:256], R=[kstb], is_out=True)
                            tr.dma("sp", self.dout["cv"][l, ttg * 128:(ttg + 1) * 128, :], kst[:, 256:512], R=[kstb], is_out=True)
                        tr.op("act", lambda e: e.activation(out=kbf[:], in_=kst[:, 0:256], func=AF.Copy), R=[kstb], W=[kbb])
                        kc0 = s_ * Lk + koff + tl
                        tr.op("dve", lambda e: e.tensor_copy(out=v_tok[:, kc0 // 128, :], in_=kst[:, 256:512]), R=[kstb], W=[vb])
                        self.transpose_to(kbf[:], kbb, 2, lambda h: kT[:, h, kc0:kc0 + 128], kTb, 4 + par)

                self.proj_phase(l, G, [0, 512, 1024], consume, pre=pre, cache="read")
            with ExitStack() as es:
                A = lambda n, sh, dt: self.alloc(es, n, sh, dt)
                PT = A("PT", [128, 2, 512], BF16)
                rs = A("rs", [128, 512], F32)
                ptb, rsb = [Buf(), Buf()], Buf()
                NQ = min(512, L)
                it = 0
                for s in range(nseq):
                    for h in range(8):
                        kvh = h // 4
                        for q0 in range(0, L, NQ):
                            qa = qT[:, h, s * L + q0:s * L + q0 + NQ]
                            bo, bs = 2 + 2 * (it % 2), 3 + 2 * (it % 2)
                            it += 1
                            nsc = Lk // 128
                            for sc in range(nsc):
                                k0 = s * Lk + sc * 128
                                bS = sc % 2
                                tr.op("pe", lambda e, k0=k0, bS=bS: e.matmul(self.ps[:, bS, 0:NQ], lhsT=kT[:, kvh, k0:k0 + 128], rhs=qa,
                                                                             start=True, stop=True), R=[kTb, qTb], W=[self.pb[bS]])
                                tr.op("act", lambda e, bS=bS: e.activation(out=PT[:, bS, 0:NQ], in_=self.ps[:, bS, 0:NQ], func=AF.Exp,
                                                                           scale=128.0 ** -0.5), R=[self.pb[bS]], W=[ptb[bS]])
                                tr.op("pe", lambda e, k0=k0, bS=bS, sc=sc: e.matmul(self.ps[:, bo, 0:NQ], lhsT=v_tok[:, k0 // 128, kvh * 128:(kvh + 1) * 128],
                                                                                    rhs=PT[:, bS, 0:NQ], start=(sc == 0), stop=(sc == nsc - 1)),
                                      R=[vb, ptb[bS]], W=[self.pb[bo]])
                                tr.op("pe", lambda e, bS=bS, sc=sc: e.matmul(self.ps[:, bs, 0:NQ], lhsT=self.ones[:], rhs=PT[:, bS, 0:NQ],
                                                                             start=(sc == 0), stop=(sc == nsc - 1)),
                                      R=[self.cst, ptb[bS]], W=[self.pb[bs]])
                            tr.op("dve", lambda e, bs=bs: e.reciprocal(out=rs[:, 0:NQ], in_=self.ps[:, bs, 0:NQ]), R=[self.pb[bs]], W=[rsb])
                            tr.op("dve", lambda e, bo=bo, s=s, q0=q0, h=h: e.tensor_tensor(
                                out=mixT[:, h, s * L + q0:s * L + q0 + NQ], in0=self.ps[:, bo, 0:NQ], in1=rs[:, 0:NQ], op=ALU.mult),
                                  R=[self.pb[bo], rsb], W=[mb])
                tr.barrier()

    def mix_ret(self, l, G, mixT, mb):
        nc, tr, di = self.nc, self.tr, self.din
        nseq, L, smp = G["nseq"], G["L"], G["sample"]
        T = nseq * L
        NTT, nch = T // 128, L // 128
        with ExitStack() as es0:
            A0 = lambda n, sh, dt: self.alloc(es0, n, sh, dt)
            qrT, krT = A0("qrT", [128, 4, T], BF16), A0("krT", [128, 4, T], BF16)
            kr_tok, vr_tok, gs_tok = A0("krtok", [128, NTT, 512], BF16), A0("vrtok", [128, NTT, 512], BF16), A0("gstok", [128, NTT, 512], BF16)
            qrb, krb, ktb, vtb, gsb = [Buf() for _ in range(5)]
            with ExitStack() as es:
                A = lambda n, sh, dt: self.alloc(es, n, sh, dt)
                RS = []
                for _ in range(2):
                    RS.append(dict(st=A("rst", [128, 512], F32), stb_=A("rstb", [128, 512], BF16), rtmp=A("rrtmp", [128, 128], F32),
                                   s1=Buf(), s2=Buf(), rtb=Buf()))
                cnt = [0]

                def consume(cb, ttg, bank):
                    par = cnt[0] % 2
                    cnt[0] += 1
                    Z = RS[par]
                    st, stb_, rtmp, s1, s2, rtb = Z["st"], Z["stb_"], Z["rtmp"], Z["s1"], Z["s2"], Z["rtb"]
                    if cb in (0, 1):
                        sc = 1.0 if cb == 0 else 128.0 ** -0.5
                        tr.op("act", lambda e: e.activation(out=st[:], in_=self.ps[:, bank, :], func=AF.Copy, scale=sc), R=[self.pb[bank]], W=[s1])
                        if smp:
                            self.rope_tok(st[:], s1, 4, ttg, rtmp[:], rtb)
                        if cb == 0:
                            tr.op("dve", lambda e: e.tensor_copy(out=stb_[:], in_=st[:]), R=[s1], W=[s2])
                            self.transpose_to(stb_[:], s2, 4, lambda h: qrT[:, h, ttg * 128:(ttg + 1) * 128], qrb, 4 + par)
                        else:
                            tr.op("dve", lambda e: e.tensor_copy(out=kr_tok[:, ttg, :], in_=st[:]), R=[s1], W=[ktb])
                            self.transpose_to(kr_tok[:, ttg, :], ktb, 4, lambda h: krT[:, h, ttg * 128:(ttg + 1) * 128], krb, 4 + par)
                    elif cb == 2:
                        tr.op("act", lambda e: e.activation(out=vr_tok[:, ttg, :], in_=self.ps[:, bank, :], func=AF.Copy), R=[self.pb[bank]], W=[vtb])
                    else:
                        tr.op("act", lambda e: e.activation(out=gs_tok[:, ttg, :], in_=self.ps[:, bank, :], func=AF.Silu), R=[self.pb[bank]], W=[gsb])

                self.proj_phase(l, G, [2048, 2560, 3072, 3584], consume, cache="read")
            with ExitStack() as es:
                A = lambda n, sh, dt: self.alloc(es, n, sh, dt)
                dl = A("dl", [128, 8], F32)
                lg = A("lg", [128, 8], F32)
                lg128 = A("lg128", [128, 8], F32)
                Mt = A("Mt", [128, 4, 128], F32)
                qd = A("qd", [128, 8, 128], F32)
                kd = A("kd", [128, 8], F32)
                cd = A("cd", [128, 8], F32)
                tmpa, tmpb = A("tmpa", [128, 128], F32), A("tmpb", [128, 128], F32)
                KV = A("KV", [128, 2, nch, 128], F32)
                Sst = A("Sst", [128, 2, nch, 128], BF16)
                Srun = A("Srun", [128, 2, 128], F32)
                attM = A("attM", [128, 128], BF16)
                kdk = A("kdk", [128, 2, 128], BF16)
                qdq = A("qdq", [128, 2, 128], BF16)
                stat = A("stat", [128, 8], F32)
                rtk = A("rtk", [128, 128], BF16)
                on = A("on", [128, 128], F32)
                tb_, kvb, ssb, srb, amb, kdb_, qdb_, stb2, rtkb, onb = [Buf() for _ in range(10)]
                O = lambda f, R, W: tr.op("dve", f, R=R, W=W)
                Aop = lambda f, R, W: tr.op("act", f, R=R, W=W)
                TT = lambda e, o, a, b, op: e.tensor_tensor(out=o, in0=a, in1=b, op=op)
                tr.dma("sp", dl[:], di["dlog_rep"][l], W=[tb_])
                Aop(lambda e: e.activation(out=lg[:], in_=dl[:], func=AF.Exp, scale=-1.0), [tb_], [tb_])
                O(lambda e: e.tensor_scalar(out=lg[:], in0=lg[:], scalar1=1.0, scalar2=None, op0=ALU.add), [tb_], [tb_])
                Aop(lambda e: e.activation(out=lg[:], in_=lg[:], func=AF.Ln), [tb_], [tb_])
                O(lambda e: e.tensor_scalar(out=lg[:], in0=lg[:], scalar1=-1.0, scalar2=None, op0=ALU.mult), [tb_], [tb_])
                O(lambda e: e.tensor_scalar(out=lg128[:], in0=lg[:], scalar1=128.0, scalar2=None, op0=ALU.mult), [tb_], [tb_])
                Aop(lambda e: e.activation(out=cd[:], in_=lg128[:], func=AF.Exp), [tb_], [tb_])
                for h in range(4):
                    f_, b_ = h, 4 + h
                    O(lambda e: e.tensor_scalar(out=tmpa[:], in0=self.diff[:], scalar1=0.0, scalar2=None, op0=ALU.max), [self.cst, tb_], [tb_])
                    Aop(lambda e, f_=f_: e.activation(out=tmpa[:], in_=tmpa[:], func=AF.Exp, scale=lg[:, f_:f_ + 1]), [tb_], [tb_])
                    O(lambda e: TT(e, tmpa[:], tmpa[:], self.lmask[:], ALU.mult), [tb_, self.cst], [tb_])
                    O(lambda e: e.tensor_scalar(out=tmpb[:], in0=self.diff[:], scalar1=-1.0, scalar2=0.0, op0=ALU.mult, op1=ALU.max), [self.cst, tb_], [tb_])
                    Aop(lambda e, b_=b_: e.activation(out=tmpb[:], in_=tmpb[:], func=AF.Exp, scale=lg[:, b_:b_ + 1]), [tb_], [tb_])
                    O(lambda e: TT(e, tmpb[:], tmpb[:], self.umask[:], ALU.mult), [tb_, self.cst], [tb_])
                    O(lambda e, h=h: TT(e, Mt[:, h, :], tmpa[:], tmpb[:], ALU.add), [tb_], [tb_])
                    O(lambda e: e.tensor_scalar(out=tmpa[:], in0=self.tpos[:], scalar1=1.0, scalar2=None, op0=ALU.add), [self.cst, tb_], [tb_])
                    Aop(lambda e, f_=f_: e.activation(out=qd[:, f_, :], in_=tmpa[:], func=AF.Exp, scale=lg[:, f_:f_ + 1]), [tb_], [tb_])
                    O(lambda e: e.tensor_scalar(out=tmpb[:], in0=self.tpos[:], scalar1=-1.0, scalar2=128.0, op0=ALU.mult, op1=ALU.add), [self.cst, tb_], [tb_])
                    Aop(lambda e, b_=b_: e.activation(out=qd[:, b_, :], in_=tmpb[:], func=AF.Exp, scale=lg[:, b_:b_ + 1]), [tb_], [tb_])
                    O(lambda e: e.tensor_scalar(out=tmpa[:, 0:1], in0=self.ipos[:], scalar1=-1.0, scalar2=127.0, op0=ALU.mult, op1=ALU.add), [self.cst, tb_], [tb_])
                    Aop(lambda e, f_=f_: e.activation(out=kd[:, f_:f_ + 1], in_=tmpa[:, 0:1], func=AF.Exp, scale=lg[:, f_:f_ + 1]), [tb_], [tb_])
                    Aop(lambda e, b_=b_: e.activation(out=kd[:, b_:b_ + 1], in_=self.ipos[:], func=AF.Exp, scale=lg[:, b_:b_ + 1]), [tb_, self.cst], [tb_])
                for s in range(nseq):
                    for h in range(4):
                        hc = slice(h * 128, (h + 1) * 128)
                        for ci in range(nch):
                            ttg = s * nch + ci
                            for d_ in range(2):
                                O(lambda e, d_=d_, ttg=ttg: e.tensor_scalar(out=kdk[:, d_, :], in0=kr_tok[:, ttg, hc], scalar1=kd[:, d_ * 4 + h:d_ * 4 + h + 1],
                                                                            scalar2=None, op0=ALU.mult), [ktb, tb_], [kdb_])
                            tr.grp("pe", [lambda e, d_=d_, ttg=ttg: e.matmul(self.ps[:, 0, d_ * 128:(d_ + 1) * 128], lhsT=kdk[:, d_, :], rhs=vr_tok[:, ttg, hc],
                                                                             start=True, stop=True) for d_ in range(2)], R=[kdb_, vtb], W=[self.pb[0]])
                            Aop(lambda e, ci=ci: e.activation(out=KV[:, :, ci, :], in_=self.ps[:, 0, 0:256].rearrange("p (a b) -> p a b", b=128), func=AF.Copy),
                                [self.pb[0]], [kvb])
                        for d_ in range(2):
                            cdc = cd[:, d_ * 4 + h:d_ * 4 + h + 1]
                            if smp:
                                tr.dma("sp", Srun[:, d_, :], di["s0s"][l, d_, h], W=[srb])
                            else:
                                O(lambda e, d_=d_: e.memset(Srun[:, d_, :], 0.0), [], [srb])
                            order = list(range(nch)) if d_ == 0 else list(range(nch - 1, -1, -1))
                            for ci in order:
                                O(lambda e, d_=d_, ci=ci: e.tensor_copy(out=Sst[:, d_, ci, :], in_=Srun[:, d_, :]), [srb], [ssb])
                                O(lambda e, d_=d_, ci=ci, cdc=cdc: e.scalar_tensor_tensor(out=Srun[:, d_, :], in0=Srun[:, d_, :], scalar=cdc, in1=KV[:, d_, ci, :],
                                                                                          op0=ALU.mult, op1=ALU.add), [srb, kvb, tb_], [srb])
                            if not smp:
                                tr.dma("sp", self.dout["ss"][l, s, d_, h], Srun[:, d_, :], R=[srb], is_out=True)
                        for ci in range(nch):
                            ttg = s * nch + ci
                            tk = slice(ttg * 128, (ttg + 1) * 128)
                            tr.op("pe", lambda e, tk=tk: e.matmul(self.ps[:, 1, 0:128], lhsT=krT[:, h, tk], rhs=qrT[:, h, tk], start=True, stop=True),
                                  R=[krb, qrb], W=[self.pb[1]])
                            O(lambda e: TT(e, attM[:], self.ps[:, 1, 0:128], Mt[:, h, :], ALU.mult), [self.pb[1], tb_], [amb])
                            for d_ in range(2):
                                O(lambda e, d_=d_, tk=tk: TT(e, qdq[:, d_, :], qrT[:, h, tk], qd[:, d_ * 4 + h, :], ALU.mult), [qrb, tb_], [qdb_])
                            tr.grp("pe", [lambda e, ttg=ttg: e.matmul(self.ps[:, 2, 0:128], lhsT=attM[:], rhs=vr_tok[:, ttg, hc], start=True, stop=False),
                                          lambda e, ci=ci: e.matmul(self.ps[:, 2, 0:128], lhsT=qdq[:, 0, :], rhs=Sst[:, 0, ci, :], start=False, stop=False),
                                          lambda e, ci=ci: e.matmul(self.ps[:, 2, 0:128], lhsT=qdq[:, 1, :], rhs=Sst[:, 1, ci, :], start=False, stop=True)],
                                   R=[amb, vtb, qdb_, ssb], W=[self.pb[2]])
                            O(lambda e: e.bn_stats(out=stat[:, 0:6], in_=self.ps[:, 2, 0:128]), [self.pb[2]], [stb2])
                            O(lambda e: e.bn_aggr(out=stat[:, 6:8], in_=stat[:, 0:6]), [stb2], [stb2])
                            Aop(lambda e: e.activation(out=stat[:, 7:8], in_=stat[:, 7:8], func=AF.Sqrt, bias=self.epsb[:, 0:1]), [stb2, self.cst], [stb2])
                            O(lambda e: e.reciprocal(out=stat[:, 7:8], in_=stat[:, 7:8]), [stb2], [stb2])
                            O(lambda e: e.tensor_scalar(out=on[:], in0=self.ps[:, 2, 0:128], scalar1=stat[:, 6:7], scalar2=stat[:, 7:8],
                                                        op0=ALU.subtract, op1=ALU.mult), [self.pb[2], stb2], [onb])
                            O(lambda e, ttg=ttg: TT(e, rtk[:], on[:], gs_tok[:, ttg, hc], ALU.mult), [onb, gsb], [rtkb])
                            self.transpose_to(rtk[:], rtkb, 1, lambda i, tk=tk: mixT[:, 12 + h, tk], mb, 3)
                tr.barrier()

    def mix_out(self, l, G, mixT, mb):
        nc, tr, di = self.nc, self.tr, self.din
        T = G["nseq"] * G["L"]
        wv = di["w_out"][l].rearrange("(kc p) n -> p kc n", p=128)
        xsv = self.xs.rearrange("(kc p) t -> p kc t", p=128)
        with ExitStack() as es:
            xt = self.alloc(es, "xto", [128, KC, 512], F32)
            xb = Buf()
            for ti in range(T // 512):
                g0 = G["t0"] + ti * 512
                tr.dma("sp", xt[:], xsv[:, :, g0:g0 + 512], R=[self.xsb[g0 // 512]], W=[xb])
                for o2 in range(8):
                    s = self.wslot(wv[:, :, o2 * 256:(o2 + 1) * 256], 16, 256)
                    for jj in range(2):
                        oc = o2 * 2 + jj
                        bk = oc % 2
                        ems = [lambda e, kc=kc, jj=jj, bk=bk, s=s: e.matmul(
                            self.ps[:, bk, :], lhsT=self.wap(s, kc, 256, jj * 128, jj * 128 + 128),
                            rhs=mixT[:, kc, ti * 512:(ti + 1) * 512], start=(kc == 0), stop=(kc == KC - 1)) for kc in range(KC)]
                        tr.grp("pe", ems, R=[self.rbuf[s], mb], W=[self.pb[bk]])
                        tr.op("dve", lambda e, oc=oc, bk=bk: e.scalar_tensor_tensor(
                            out=xt[:, oc, :], in0=self.ps[:, bk, :], scalar=self.gate[:, 1, G["c"], oc:oc + 1], in1=xt[:, oc, :],
                            op0=ALU.mult, op1=ALU.add), R=[self.pb[bk], self.modb, xb], W=[xb])
                tr.dma("sp", xsv[:, :, g0:g0 + 512], xt[:], R=[xb], W=[self.xsb[g0 // 512]])
            tr.barrier()

    def mixer(self, l, G):
        nc, tr = self.nc, self.tr
        T = G["nseq"] * G["L"]
        with ExitStack() as es:
            mixT = self.alloc(es, "mixT", [128, 16, T], BF16)
            mb = Buf()
            self.mix_s5(l, G, mixT, mb)
            self.mix_attn(l, G, mixT, mb)
            self.mix_ret(l, G, mixT, mb)
            self.mix_out(l, G, mixT, mb)
            tr.barrier()

    def ffn_stage(self, l, f, supers, last):
        nc, tr, di = self.nc, self.tr, self.din
        sub = 0 if f == 0 else 2
        xsv = self.xs.rearrange("(kc p) t -> p kc t", p=128)
        win = di["w_ffn_in"][l, f].rearrange("(kc p) n -> p kc n", p=128)
        wout = di["w_ffn_out"][l, f].rearrange("(kc p) n -> p kc n", p=128)
        BLK = [(0, 12), (12, 12), (24, 12), (36, 8)]
        with ExitStack() as es:
            xt = self.alloc(es, "xtf", [128, 2, KC, 512], F32)
            hT = self.alloc(es, "hTf", [128, 2, KC, 512], BF16)
            work = self.alloc(es, "wkf", [128, 3, 512], F32)
            hid = self.alloc(es, "hid", [128, 2, 12, 512], BF16)
            xb, hb, hidb = [Buf(), Buf()], [Buf(), Buf()], [Buf(), Buf()]
            wb = [Buf() for _ in range(4)]
            for sup in supers:
                NTs = len(sup)
                for i, (t0, c) in enumerate(sup):
                    src = di["xin"] if (l == 0 and f == 0) else self.xs
                    self.load_norm(src, t0, 512, xt[:, i], xb[i], hT[:, i], hb[i], sub, c, work, wb)
                for (k0, HB) in BLK:
                    for h2 in range(HB // 2):
                        c0 = (k0 + 2 * h2) * 128
                        sa = self.wslot(win[:, :, c0:c0 + 256], 16, 256)
                        su = self.wslot(win[:, :, DFF + c0:DFF + c0 + 256], 16, 256)
                        for jj in range(2):
                            for i in range(NTs):
                                ba, bu = 2 * i, 2 * i + 1
                                for (s, bk) in ((sa, ba), (su, bu)):
                                    ems = [lambda e, kc=kc, s=s, bk=bk, jj=jj, i=i: e.matmul(
                                        self.ps[:, bk, :], lhsT=self.wap(s, kc, 256, jj * 128, jj * 128 + 128),
                                        rhs=hT[:, i, kc, :], start=(kc == 0), stop=(kc == KC - 1)) for kc in range(KC)]
                                    tr.grp("pe", ems, R=[self.rbuf[s], hb[i]], W=[self.pb[bk]])
                                sl = work[:, i, :]
                                tr.op("act", lambda e, ba=ba, sl=sl: e.activation(out=sl, in_=self.ps[:, ba, :], func=AF.Silu),
                                      R=[self.pb[ba]], W=[wb[i]])
                                tr.op("dve", lambda e, bu=bu, sl=sl, i=i, kk=2 * h2 + jj: e.tensor_tensor(
                                    out=hid[:, i, kk, :], in0=self.ps[:, bu, :], in1=sl, op=ALU.mult),
                                      R=[self.pb[bu], wb[i]], W=[hidb[i]])
                    for o2 in range(8):
                        s = self.wslot(wout[:, k0:k0 + HB, o2 * 256:(o2 + 1) * 256], HB, 256)
                        for jj in range(2):
                            oc = o2 * 2 + jj
                            for i in range(NTs):
                                bk = 4 + 2 * (oc % 2) + i
                                ems = [lambda e, k=k, jj=jj, bk=bk, i=i, s=s: e.matmul(
                                    self.ps[:, bk, :], lhsT=self.wap(s, k, 256, jj * 128, jj * 128 + 128),
                                    rhs=hid[:, i, k, :], start=(k == 0), stop=(k == HB - 1)) for k in range(HB)]
                                tr.grp("pe", ems, R=[self.rbuf[s], hidb[i]], W=[self.pb[bk]])
                                tr.op("dve", lambda e, oc=oc, bk=bk, i=i, c=sup[i][1]: e.scalar_tensor_tensor(
                                    out=xt[:, i, oc, :], in0=self.ps[:, bk, :], scalar=self.gate[:, sub, c, oc:oc + 1],
                                    in1=xt[:, i, oc, :], op0=ALU.mult, op1=ALU.add), R=[self.pb[bk], self.modb, xb[i]], W=[xb[i]])
                for i, (t0, c) in enumerate(sup):
                    if last:
                        self.final_norm_store(xt[:, i], xb[i], t0, work, wb)
                    else:
                        tr.dma("sp", xsv[:, :, t0:t0 + 512], xt[:, i], R=[xb[i]], W=[self.xsb[t0 // 512]])
            tr.barrier()

    def build(self):
        tr, nc, di, do = self.tr, self.nc, self.din, self.dout
        self.epsb = nc.alloc_sbuf_tensor("epsb", [128, 1], F32)
        tr.op("dve", lambda e: e.memset(self.epsb[:], EPS), W=[self.cst])
        self.setup_consts()
        groups = [dict(t0=0, nseq=self.nseq_p, L=self.L_p, c=0, sample=False)]
        tiles = [(0, 0)]
        supers = [[(0, 0)]]
        if self.with_sample:
            groups.append(dict(t0=512, nseq=1, L=1024, c=1, sample=True))
            tiles += [(512, 1), (1024, 1)]
            supers.append([(512, 1), (1024, 1)])
        if not self.do_ffn:
            xsv0 = self.xs.rearrange("(kc p) t -> p kc t", p=128)
            xiv0 = self.din["xin"].rearrange("(kc p) t -> p kc t", p=128)
            with ExitStack() as es:
                xt0 = self.alloc(es, "xt0", [128, KC, 512], F32)
                xb0 = Buf()
                for (t0, c) in tiles:
                    tr.dma("sp", xt0[:], xiv0[:, :, t0:t0 + 512], W=[xb0])
                    tr.dma("sp", xsv0[:, :, t0:t0 + 512], xt0[:], R=[xb0], W=[self.xsb[t0 // 512]])
                tr.barrier()
        for l in self.layers:
            self.adaln(l)
            if self.do_ffn:
                self.ffn_stage(l, 0, supers, False)
            for G in groups:
                self.mixer(l, G)
            if self.do_ffn:
                self.ffn_stage(l, 1, supers, l == self.layers[-1])
        if not self.do_ffn:
            xsv = self.xs.rearrange("(kc p) t -> p kc t", p=128)
            yv = self.dout["y"].rearrange("(kc p) t -> p kc t", p=128)
            with ExitStack() as es:
                xt = self.alloc(es, "xtd", [128, KC, 512], F32)
                xb = Buf()
                for (t0, c) in tiles:
                    tr.dma("sp", xt[:], xsv[:, :, t0:t0 + 512], R=[self.xsb[t0 // 512]], W=[xb])
                    tr.dma("sp", yv[:, :, t0:t0 + 512], xt[:], R=[xb], is_out=True)
        tr.finish()

    def final_norm_store(self, xt, xb, t0, work, wb):
        tr, nc = self.tr, self.nc
        yv = self.dout["y"].rearrange("(kc p) t -> p kc t", p=128)
        bank = 6
        sq = work[:, 2, :].bitcast(BF16)
        for kc in range(KC):
            h = kc % 2
            tr.op("act", lambda e, kc=kc, h=h: e.activation(out=sq[:, h * 512:(h + 1) * 512], in_=xt[:, kc, :], func=AF.Square),
                  R=[xb], W=[wb[2 + h]])
            tr.op("pe", lambda e, kc=kc, h=h: e.matmul(self.ps[:, bank, :], lhsT=self.ones[:], rhs=sq[:, h * 512:(h + 1) * 512],
                                                       start=(kc == 0), stop=(kc == KC - 1)), R=[wb[2 + h], self.cst], W=[self.pb[bank]])
        rstd = work[:, 0, :]
        tr.op("act", lambda e: e.activation(out=rstd, in_=self.ps[:, bank, :], func=AF.Sqrt, scale=1.0 / D, bias=self.epsb[:, 0:1]),
              R=[self.pb[bank], self.cst], W=[wb[0]])
        tr.op("dve", lambda e: e.reciprocal(out=rstd, in_=rstd), R=[wb[0]], W=[wb[0]])
        for kc in range(KC):
            tr.op("dve", lambda e, kc=kc: e.scalar_tensor_tensor(out=xt[:, kc, :], in0=xt[:, kc, :], scalar=self.fgT[:, kc:kc + 1],
                                                                 in1=rstd, op0=ALU.mult, op1=ALU.mult),
                  R=[xb, wb[0], self.cst], W=[xb])
        tr.dma("sp", yv[:, :, t0:t0 + 512], xt[:], R=[xb], is_out=True)


def _prep_shared(inp):
    f = lambda a: np.ascontiguousarray(np.asarray(a, dtype=np.float32))
    sh = {}
    for k in ("w_mod", "w_ffn_in", "w_ffn_out", "w_in", "w_out"):
        sh[k] = f(inp[k])
    sh["w_glu"] = f(inp["w_ssm_glu"])
    sh["b_modT"] = f(np.asarray(inp["b_mod"]).reshape(DEPTH, 144, 128).transpose(0, 2, 1))
    sh["norm_gT"] = f(np.asarray(inp["norm_g"]).reshape(DEPTH, 3, KC, 128).transpose(0, 3, 1, 2).reshape(DEPTH, 128, 48))
    sh["final_gT"] = f(np.asarray(inp["final_norm_g"]).reshape(KC, 128).T)
    sh["qg_rep"] = f(np.broadcast_to(np.asarray(inp["q_norm_g"])[:, None, :], (DEPTH, 128, 128)))
    sh["kg_rep"] = f(np.broadcast_to(np.asarray(inp["k_norm_g"])[:, None, :], (DEPTH, 128, 128)))

    def chmaj(a):
        a = np.asarray(a).reshape(DEPTH, 2, 16, 2, 64)
        return a.transpose(0, 1, 3, 4, 2).reshape(DEPTH, 2, 128, 16)
    ldt = np.broadcast_to(np.asarray(inp["ssm_log_dt"])[..., None], (DEPTH, 2, 32, 64))
    sh["ssm_abd"] = f(np.stack([chmaj(inp["ssm_a_re"]), chmaj(inp["ssm_a_im"]), chmaj(ldt)], axis=3))
    BT = np.zeros((DEPTH, 2, 2, 128, 16, 128), np.float32)
    CT = np.zeros((DEPTH, 2, 2, 128, 16, 128), np.float32)
    Bs = [np.asarray(inp["ssm_b_re"]), np.asarray(inp["ssm_b_im"])]
    Cs = [np.asarray(inp["ssm_c_re"]), np.asarray(inp["ssm_c_im"])]
    for j in range(16):
        for gl in range(2):
            g = 2 * j + gl
            r0 = 32 * (j % 4) + 16 * gl
            for c in range(2):
                BT[:, :, c, r0:r0 + 16, j, gl * 64:(gl + 1) * 64] = Bs[c][:, :, g].transpose(0, 1, 3, 2)
                CT[:, :, c, gl * 64:(gl + 1) * 64, j, r0:r0 + 16] = Cs[c][:, :, g].transpose(0, 1, 3, 2)
    sh["ssm_BT"] = BT.reshape(DEPTH, 2, 2, 128, 2048)
    sh["ssm_CT"] = CT.reshape(DEPTH, 2, 2, 128, 2048)
    sh["ssm_d_rep"] = f(np.broadcast_to(np.asarray(inp["ssm_d"])[:, None, :], (DEPTH, 128, 512)))
    sh["dlog_rep"] = f(np.broadcast_to(np.asarray(inp["ret_decay_logit"]).reshape(DEPTH, 1, 8), (DEPTH, 128, 8)))
    i = np.arange(128, dtype=np.float32)
    sh["c_ident"] = np.eye(128, dtype=np.float32)
    sh["c_J"] = np.ascontiguousarray(np.eye(128, dtype=np.float32)[::-1])
    sh["c_tpos"] = f(np.broadcast_to(i[None, :], (128, 128)))
    sh["c_ipos"] = f(i[:, None])
    sh["c_diff"] = f(i[None, :] - i[:, None])
    sh["c_lmask"] = f((i[None, :] >= i[:, None]))
    sh["c_umask"] = f((i[None, :] <= i[:, None]))
    t = np.arange(1024)
    pos = np.stack([t // 64, t % 64], axis=-1).astype(np.float32)
    inv = (np.float32(10000.0) ** (-np.arange(32, dtype=np.float32) / np.float32(32))).astype(np.float32)
    ang = (pos[:, :, None] * inv[None, None, :]).astype(np.float32)
    co, si = np.cos(ang).astype(np.float32), np.sin(ang).astype(np.float32)
    cfull = np.stack([co, co], axis=2).reshape(1024, 128)
    sfull = np.stack([-si, si], axis=2).reshape(1024, 128)
    sh["rope_c"] = f(cfull.reshape(8, 128, 128).transpose(1, 0, 2))
    sh["rope_s"] = f(sfull.reshape(8, 128, 128).transpose(1, 0, 2))
    return sh


def _in_maps(inp, P):
    sh = _prep_shared(inp)
    xp = np.asarray(inp["x_prompt"], np.float32)
    xsm = np.asarray(inp["x_sample"], np.float32)
    cs = np.asarray(inp["c"], np.float32)
    cctx = np.asarray(inp["c_ctx"], np.float32)
    ck = np.asarray(inp["cache_k"], np.float32)
    cv_ = np.asarray(inp["cache_v"], np.float32)
    sts = np.asarray(inp["state_ssm"], np.float32)
    str_ = np.asarray(inp["state_ret"], np.float32)
    maps = []
    for c in range(NCORE):
        b = c % 2
        m = dict(sh)
        xs_ = [xp[2 * c:2 * c + 2].reshape(512, D)]
        if P.with_sample:
            xs_.append(xsm[b])
        m["xin"] = np.ascontiguousarray(np.concatenate(xs_, axis=0).T)
        cvv = np.stack([cctx, cs[b]], axis=-1)
        m["cvec"] = np.ascontiguousarray(cvv.reshape(KC, 128, 2).transpose(1, 0, 2))
        m["cache_k"] = np.ascontiguousarray(ck[b].reshape(DEPTH, PAST, 256))
        m["cache_v"] = np.ascontiguousarray(cv_[b].reshape(DEPTH, PAST, 256))
        h0 = sts[b].reshape(DEPTH, 2, 16, 2, 64, 2).transpose(0, 1, 3, 4, 5, 2).reshape(DEPTH, 2, 128, 2, 16)
        m["h0s"] = np.ascontiguousarray(h0)
        m["s0s"] = np.ascontiguousarray(str_[b])
        maps.append({k: v for k, v in m.items() if k in P.din})
    return maps


_PROG = None


def kernel(**inp):
    global _PROG
    if _PROG is None:
        _PROG = Prog()
    P = _PROG
    res = run_bass_kernel_spmd(P.nc, _in_maps(inp, P), core_ids=list(range(NCORE)))
    R = res.results
    f32 = np.float32
    y_prompt = np.stack([R[c]["y"][:, 0:512].T.reshape(2, 256, D) for c in range(NCORE)]).reshape(16, 256, D).astype(f32)
    y_sample = np.stack([R[b]["y"][:, 512:1536].T for b in range(2)]).astype(f32)
    def cache(name):
        a = np.stack([R[c][name].reshape(DEPTH, 2, 256, 2, 128).transpose(1, 0, 2, 3, 4) for c in range(NCORE)])
        return np.ascontiguousarray(a.reshape(16, DEPTH, 256, 2, 128).astype(f32))
    def hstate():
        out = []
        for c in range(NCORE):
            a = R[c]["hs"].reshape(DEPTH, 2, 2, 2, 64, 16, 2)
            out.append(a.transpose(1, 0, 2, 5, 3, 4, 6).reshape(2, DEPTH, 2, 32, 64, 2))
        return np.ascontiguousarray(np.concatenate(out, axis=0).astype(f32))
    def sstate():
        out = [R[c]["ss"].transpose(1, 0, 2, 3, 4, 5) for c in range(NCORE)]
        return np.ascontiguousarray(np.concatenate(out, axis=0).astype(f32))
    return (y_prompt, y_sample, cache("ck"), cache("cv"), hstate(), sstate())
```

```python
import numpy as np
import concourse.bass as bass
import concourse.mybir as mybir
from concourse.bass_utils import run_bass_kernel_spmd
from contextlib import ExitStack

F32 = mybir.dt.float32
BF16 = mybir.dt.bfloat16
AF = mybir.ActivationFunctionType
ALU = mybir.AluOpType
AX = mybir.AxisListType

D = 2048
KC = 16
DFF = 5632
HC = 44
NCORE = 8
DEPTH = 2
EPS = 1e-6
PAST = 512
PI_SAFE = 3.1415925


class Buf:
    __slots__ = ("w", "r", "name")

    def __init__(self, name=""):
        self.w = None
        self.r = {}
        self.name = name


class Eng:
    EPOCH = 12000

    def __init__(self, nc, name, h, ndma=0):
        self.nc, self.name, self.h = nc, name, h
        self.sems = [nc.alloc_semaphore(f"s_{name}_0")]
        self.ep, self.cnt = 0, 0
        self.waited = {}
        self.dsem = [nc.alloc_semaphore(f"d_{name}_{i}") for i in range(ndma)]
        self.dval = [0] * ndma
        self.di = 0

    def bump(self, inst):
        if self.cnt >= self.EPOCH:
            self.ep += 1
            self.cnt = 0
            self.sems.append(self.nc.alloc_semaphore(f"s_{self.name}_{self.ep}"))
        self.cnt += 1
        inst.then_inc(self.sems[self.ep], 1)
        return ("c", self.name, self.ep, self.cnt, self.sems[self.ep])


class TR:
    def __init__(self, nc):
        self.nc = nc
        self.E = {
            "pe": Eng(nc, "pe", nc.tensor),
            "act": Eng(nc, "act", nc.scalar),
            "dve": Eng(nc, "dve", nc.vector),
            "sp": Eng(nc, "sp", nc.sync, ndma=12),
            "pool": Eng(nc, "pool", nc.gpsimd, ndma=12),
        }
        self.out_toks = []

    def _wait(self, E, deps):
        for t in deps:
            if t is None:
                continue
            if t[0] == "c":
                _, en, ep, cnt, sem = t
                if en == "pe" and E.name == "pe":
                    continue
                key = ("c", en)
                cur = E.waited.get(key, (-1, -1))
                if (ep, cnt) <= cur:
                    continue
                E.h.wait_ge(sem, cnt)
                E.waited[key] = (ep, cnt)
            else:
                _, sid, val, sem = t
                key = ("d", sid)
                if E.waited.get(key, 0) >= val:
                    continue
                E.h.wait_ge(sem, val)
                E.waited[key] = val

    def _deps(self, R, W):
        deps = []
        for b in R:
            deps.append(b.w)
        for b in W:
            deps.append(b.w)
            deps.extend(b.r.values())
        return deps

    def _mark(self, tok, R, W):
        key = (tok[0], tok[1])
        for b in R:
            b.r[key] = tok
        for b in W:
            b.w = tok
            b.r = {}

    def op(self, en, emit, R=(), W=()):
        E = self.E[en]
        self._wait(E, self._deps(R, W))
        inst = emit(E.h)
        tok = E.bump(inst)
        self._mark(tok, R, W)
        return tok

    def grp(self, en, emits, R=(), W=()):
        E = self.E[en]
        self._wait(E, self._deps(R, W))
        inst = None
        for f in emits:
            inst = f(E.h)
        tok = E.bump(inst)
        self._mark(tok, R, W)
        return tok

    def dma(self, q, out, in_, R=(), W=(), is_out=False):
        E = self.E[q]
        self._wait(E, self._deps(R, W))
        i = E.di
        E.di = (E.di + 1) % len(E.dsem)
        sem, prev = E.dsem[i], E.dval[i]
        key = ("d", id(sem))
        if prev > 0 and E.waited.get(key, 0) < prev:
            E.h.wait_ge(sem, prev)
            E.waited[key] = prev
        E.h.dma_start(out=out, in_=in_).then_inc(sem, 16)
        E.dval[i] = prev + 16
        tok = ("d", id(sem), prev + 16, sem)
        self._mark(tok, R, W)
        if is_out:
            self.out_toks.append(tok)
        return tok

    def barrier(self):
        toks = []
        for en in ("pe", "act", "dve"):
            E = self.E[en]
            if E.cnt > 0 or E.ep > 0:
                toks.append(("c", en, E.ep, E.cnt, E.sems[E.ep]))
        E = self.E["sp"]
        for i, s in enumerate(E.dsem):
            if E.dval[i] > 0:
                toks.append(("d", id(s), E.dval[i], s))
        for en in ("pe", "act", "dve", "sp"):
            X = self.E[en]
            for t in toks:
                if t[0] == "c" and t[1] == en:
                    continue
                self._wait(X, [t])
        self.fence = toks

    def wait_fence(self, en):
        self._wait(self.E[en], getattr(self, "fence", []))

    def finish(self):
        E = self.E["sp"]
        self._wait(E, self.out_toks)
        for q in ("sp", "pool"):
            Q = self.E[q]
            for i, s in enumerate(Q.dsem):
                if Q.dval[i] > 0:
                    self._wait(E, [("d", id(s), Q.dval[i], s)])


class Prog:
    def __init__(self, nseq_p=2, L_p=256, with_sample=True, layers=(0, 1), do_ffn=True):
        self.nc = nc = bass.Bass("TRN2", target_bir_lowering=False)
        self.tr = TR(nc)
        self.with_sample = with_sample
        self.layers, self.do_ffn = list(layers), do_ffn
        self.TP = nseq_p * L_p
        self.TT = self.TP + (1024 if with_sample else 0)
        self.nseq_p, self.L_p = nseq_p, L_p
        di = self.din = {}

        def inp(name, shape, dt=F32):
            di[name] = nc.dram_tensor(name, list(shape), dt, kind="ExternalInput").ap()
            return di[name]

        inp("xin", [D, self.TT])
        inp("cvec", [128, KC, 2])
        inp("w_mod", [DEPTH, D, 9 * D])
        inp("b_modT", [DEPTH, 128, 144])
        inp("norm_gT", [DEPTH, 128, 48])
        inp("final_gT", [128, KC])
        if do_ffn:
            inp("w_ffn_in", [DEPTH, 2, D, 2 * DFF])
            inp("w_ffn_out", [DEPTH, 2, DFF, D])
        inp("w_in", [DEPTH, D, 4096])
        inp("w_out", [DEPTH, D, D])
        inp("qg_rep", [DEPTH, 128, 128])
        inp("kg_rep", [DEPTH, 128, 128])
        inp("ssm_abd", [DEPTH, 2, 128, 3, 16])
        inp("ssm_BT", [DEPTH, 2, 2, 128, 16 * 128])
        inp("ssm_CT", [DEPTH, 2, 2, 128, 16 * 128])
        inp("ssm_d_rep", [DEPTH, 128, 512])
        inp("w_glu", [DEPTH, 512, 1024])
        inp("dlog_rep", [DEPTH, 128, 8])
        for nm in ("c_ident", "c_J", "c_tpos", "c_diff", "c_lmask", "c_umask"):
            inp(nm, [128, 128])
        inp("c_ipos", [128, 1])
        if with_sample:
            inp("cache_k", [DEPTH, PAST, 256])
            inp("cache_v", [DEPTH, PAST, 256])
            inp("h0s", [DEPTH, 2, 128, 2, 16])
            inp("s0s", [DEPTH, 2, 4, 128, 128])
            inp("rope_c", [128, 8, 128])
            inp("rope_s", [128, 8, 128])
        do = self.dout = {}

        def outp(name, shape):
            do[name] = nc.dram_tensor(name, list(shape), F32, kind="ExternalOutput").ap()

        outp("y", [D, self.TT])
        outp("ck", [DEPTH, self.TP, 256])
        outp("cv", [DEPTH, self.TP, 256])
        outp("hs", [DEPTH, nseq_p, 2, 128, 32])
        outp("ss", [DEPTH, nseq_p, 2, 4, 128, 128])
        self.xs = nc.dram_tensor("xs_scr", [D, self.TT], F32, kind="Internal").ap()
        self.xsb = [Buf(f"xs{i}") for i in range(self.TT // 512)]

        A = nc.alloc_sbuf_tensor
        self.ident = A("ident", [128, 128], BF16)
        self.Jm = A("Jm", [128, 128], BF16)
        self.ones = A("ones", [128, 128], BF16)
        self.tpos = A("tpos", [128, 128], F32)
        self.ipos = A("ipos", [128, 1], F32)
        self.diff = A("diff", [128, 128], F32)
        self.lmask = A("lmask", [128, 128], F32)
        self.umask = A("umask", [128, 128], F32)
        self.cst = Buf("cst")
        self.NSLOT = 6
        self.ring = A("ring", [128, self.NSLOT, 4096], BF16)
        self.rbuf = [Buf(f"ring{i}") for i in range(self.NSLOT)]
        self.ri = 0
        self.ps = nc.alloc_psum_tensor("ps", [128, 8, 512], F32)
        self.pb = [Buf(f"psb{i}") for i in range(8)]
        self.modT = A("modT", [128, 144, 2], F32)
        self.bmod = A("bmod", [128, 144], F32)
        self.ngT = A("ngT", [128, 48], F32)
        self.fgT = A("fgT", [128, KC], F32)
        self.Amod = A("Amod", [128, 3, 2, KC], F32)
        self.gate = A("gate", [128, 3, 2, KC], F32)
        self.scb = A("scb", [128, KC, 2], BF16)
        self.cv32 = A("cv32", [128, KC, 2], F32)
        self.modb = Buf("mod")
        self.qg = A("qg", [128, 128], F32)
        if with_sample:
            self.ropec = A("ropec", [128, 8, 128], F32)
            self.ropes = A("ropes", [128, 8, 128], F32)
        self.kg = A("kg", [128, 128], F32)
        self.build()

    def load_f32_via_sp(self, dst_ap, src_ap, buf):
        return self.tr.dma("sp", dst_ap, src_ap, W=[buf])

    def wslot(self, src_ap, nk, ncols):
        s = self.ri
        self.ri = (self.ri + 1) % self.NSLOT
        dst = self.ring[:, s, 0:nk * ncols].rearrange("p (k n) -> p k n", n=ncols)
        self.tr.dma("pool", dst, src_ap, W=[self.rbuf[s]])
        return s

    def wap(self, s, k, ncols, c0, c1):
        return self.ring[:, s, k * ncols + c0:k * ncols + c1]

    def setup_consts(self):
        tr, nc, di = self.tr, self.nc, self.din
        with nc.sbuf_tensor("ctmp", [128, 2, 128], F32) as ctmp:
            tb = Buf("ctmp")
            tr.dma("sp", ctmp[:, 0, :], di["c_ident"], W=[tb])
            tr.dma("sp", ctmp[:, 1, :], di["c_J"], W=[tb])
            tr.op("dve", lambda e: e.tensor_copy(out=self.ident[:], in_=ctmp[:, 0, :]), R=[tb], W=[self.cst])
            tr.op("dve", lambda e: e.tensor_copy(out=self.Jm[:], in_=ctmp[:, 1, :]), R=[tb], W=[self.cst])
            tr.op("dve", lambda e: e.memset(self.ones[:], 1.0), W=[self.cst])
            tr.dma("sp", self.tpos[:], di["c_tpos"], W=[self.cst])
            tr.dma("sp", self.ipos[:], di["c_ipos"], W=[self.cst])
            tr.dma("sp", self.diff[:], di["c_diff"], W=[self.cst])
            tr.dma("sp", self.lmask[:], di["c_lmask"], W=[self.cst])
            tr.dma("sp", self.umask[:], di["c_umask"], W=[self.cst])
            tr.dma("sp", self.fgT[:], di["final_gT"], W=[self.cst])
            tr.dma("sp", self.cv32[:], di["cvec"], W=[self.cst])
            if self.with_sample:
                tr.dma("sp", self.ropec[:], di["rope_c"], W=[self.cst])
                tr.dma("sp", self.ropes[:], di["rope_s"], W=[self.cst])
            tr.op("act", lambda e: e.activation(out=self.scb[:], in_=self.cv32[:], func=AF.Silu),
                  R=[self.cst], W=[self.cst])
            tr.barrier()

    def adaln(self, l):
        tr, nc, di = self.tr, self.nc, self.din
        wm = di["w_mod"][l].rearrange("(kc p) n -> p kc n", p=128)
        tr.dma("sp", self.bmod[:], di["b_modT"][l], W=[self.modb])
        tr.dma("sp", self.ngT[:], di["norm_gT"][l], W=[self.modb])
        tr.dma("sp", self.qg[:], di["qg_rep"][l], W=[self.modb])
        tr.dma("sp", self.kg[:], di["kg_rep"][l], W=[self.modb])
        bank = 7
        pv = self.ps[:, bank, 0:288].rearrange("p (j c) -> p j c", c=2)
        for u in range(72):
            s = self.wslot(wm[:, :, u * 256:(u + 1) * 256], 16, 256)
            for jj in range(2):
                j = u * 2 + jj
                ems = []
                for kc in range(KC):
                    ems.append(lambda e, kc=kc, jj=jj, j=j, s=s: e.matmul(
                        pv[:, j, :], lhsT=self.wap(s, kc, 256, jj * 128, jj * 128 + 128),
                        rhs=self.scb[:, kc, :], start=(kc == 0), stop=(kc == KC - 1)))
                tr.grp("pe", ems, R=[self.rbuf[s], self.cst], W=[self.pb[bank]])
        for c in range(2):
            tr.op("dve", lambda e, c=c: e.tensor_tensor(out=self.modT[:, :, c], in0=pv[:, :, c],
                                                        in1=self.bmod[:], op=ALU.add),
                  R=[self.pb[bank], self.modb], W=[self.modb])
        for sub in range(3):
            for c in range(2):
                sc = self.modT[:, (sub * 3 + 1) * 16:(sub * 3 + 2) * 16, c]
                gt = self.modT[:, (sub * 3 + 2) * 16:(sub * 3 + 3) * 16, c]
                tr.op("dve", lambda e, sub=sub, c=c, sc=sc: e.scalar_tensor_tensor(
                    out=self.Amod[:, sub, c, :], in0=sc, scalar=1.0, in1=self.ngT[:, sub * 16:(sub + 1) * 16],
                    op0=ALU.add, op1=ALU.mult), R=[self.modb], W=[self.modb])
                tr.op("dve", lambda e, sub=sub, c=c, gt=gt: e.tensor_scalar(
                    out=self.gate[:, sub, c, :], in0=gt, scalar1=(1.0 if sub == 1 else 0.5), scalar2=None,
                    op0=ALU.mult), R=[self.modb], W=[self.modb])

    def shift_ap(self, sub, c, kc):
        j = (sub * 3 + 0) * 16 + kc
        return self.modT[:, j, c:c + 1]

    def norm_tile(self, xt, xb, n, out_t, ob, A_ap, shift_fn, work, wb):
        tr = self.tr
        bank = 6
        sq = work[:, 2, :].bitcast(BF16)
        for kc in range(KC):
            h = kc % 2
            tr.op("act", lambda e, kc=kc, h=h: e.activation(out=sq[:, h * 512:h * 512 + n], in_=xt[:, kc, 0:n],
                                                            func=AF.Square), R=[xb], W=[wb[2 + h]])
            tr.op("pe", lambda e, kc=kc, h=h: e.matmul(self.ps[:, bank, 0:n], lhsT=self.ones[:],
                                                       rhs=sq[:, h * 512:h * 512 + n], start=(kc == 0),
                                                       stop=(kc == KC - 1)), R=[wb[2 + h], self.cst], W=[self.pb[bank]])
        rstd = work[:, 0, 0:n]
        tr.op("act", lambda e: e.activation(out=rstd, in_=self.ps[:, bank, 0:n], func=AF.Sqrt,
                                            scale=1.0 / D, bias=self.epsb[:, 0:1]), R=[self.pb[bank], self.cst], W=[wb[0]])
        tr.op("dve", lambda e: e.reciprocal(out=rstd, in_=rstd), R=[wb[0]], W=[wb[0]])
        tmp = work[:, 1, 0:n]
        for kc in range(KC):
            tr.op("dve", lambda e, kc=kc: e.tensor_tensor(out=tmp, in0=xt[:, kc, 0:n], in1=rstd, op=ALU.mult),
                  R=[xb, wb[0]], W=[wb[1]])
            sh = shift_fn(kc) if shift_fn is not None else 0.0
            tr.op("act", lambda e, kc=kc, sh=sh: e.activation(out=out_t[:, kc, 0:n], in_=tmp, func=AF.Identity,
                                                              scale=A_ap(kc), bias=sh),
                  R=[wb[1], self.modb, self.cst], W=[ob])

    def ffn_tile(self, l, f, xt, xb, n, hT, hb, hid, hidb, gate_ap, work, wb):
        tr, di = self.tr, self.din
        win = di["w_ffn_in"][l, f].rearrange("(kc p) n -> p kc n", p=128)
        wout = di["w_ffn_out"][l, f].rearrange("(kc p) n -> p kc n", p=128)
        bi = 0
        for h2 in range(HC // 2):
            sa = self.wslot(win[:, :, h2 * 256:(h2 + 1) * 256], 16, 256)
            su = self.wslot(win[:, :, DFF + h2 * 256:DFF + (h2 + 1) * 256], 16, 256)
            for jj in range(2):
                hc = h2 * 2 + jj
                ba, bu = (bi % 2) * 2, (bi % 2) * 2 + 1
                bi += 1
                for (s, bk) in ((sa, ba), (su, bu)):
                    ems = [lambda e, kc=kc, s=s, bk=bk, jj=jj: e.matmul(
                        self.ps[:, bk, 0:n], lhsT=self.wap(s, kc, 256, jj * 128, jj * 128 + 128),
                        rhs=hT[:, kc, 0:n], start=(kc == 0), stop=(kc == KC - 1)) for kc in range(KC)]
                    tr.grp("pe", ems, R=[self.rbuf[s], hb], W=[self.pb[bk]])
                sl = work[:, bi % 2, 0:n]
                tr.op("act", lambda e, ba=ba, sl=sl: e.activation(out=sl, in_=self.ps[:, ba, 0:n], func=AF.Silu),
                      R=[self.pb[ba]], W=[wb[bi % 2]])
                tr.op("dve", lambda e, bu=bu, sl=sl, hc=hc: e.tensor_tensor(out=hid[:, hc, 0:n], in0=self.ps[:, bu, 0:n],
                                                                            in1=sl, op=ALU.mult),
                      R=[self.pb[bu], wb[bi % 2]], W=[hidb])
        for o2 in range(8):
            slots = [self.wslot(wout[:, q * 11:(q + 1) * 11, o2 * 256:(o2 + 1) * 256], 11, 256) for q in range(4)]
            for jj in range(2):
                oc = o2 * 2 + jj
                bk = 4 + (oc % 2)
                ems = []
                for q in range(4):
                    for k in range(11):
                        kk = q * 11 + k
                        ems.append(lambda e, q=q, k=k, kk=kk, jj=jj, bk=bk: e.matmul(
                            self.ps[:, bk, 0:n], lhsT=self.wap(slots[q], k, 256, jj * 128, jj * 128 + 128),
                            rhs=hid[:, kk, 0:n], start=(kk == 0), stop=(kk == HC - 1)))
                tr.grp("pe", ems, R=[self.rbuf[s] for s in slots] + [hidb], W=[self.pb[bk]])
                tr.op("dve", lambda e, oc=oc, bk=bk: e.scalar_tensor_tensor(
                    out=xt[:, oc, 0:n], in0=self.ps[:, bk, 0:n], scalar=gate_ap(oc), in1=xt[:, oc, 0:n],
                    op0=ALU.mult, op1=ALU.add), R=[self.pb[bk], self.modb, xb], W=[xb])

    def proj_tok(self, slots, hT, hb, t0, bank):
        tr = self.tr
        ems = [lambda e, kc=kc: e.matmul(self.ps[:, bank, :], lhsT=hT[:, kc, t0:t0 + 128],
                                         rhs=self.wap(slots[kc // 8], kc % 8, 512, 0, 512),
                                         start=(kc == 0), stop=(kc == KC - 1)) for kc in range(KC)]
        tr.grp("pe", ems, R=[self.rbuf[s] for s in slots] + [hb], W=[self.pb[bank]])

    def w_in_slots(self, l, c0):
        wv = self.din["w_in"][l].rearrange("(kc p) n -> p kc n", p=128)
        return [self.wslot(wv[:, h * 8:(h + 1) * 8, c0:c0 + 512], 8, 512) for h in range(2)]

    def transpose_to(self, src_tok, sb, ncol_blocks, dst_fn, db, bank, rev=False):
        tr = self.tr
        X = self.Jm if rev else self.ident
        ems = [lambda e, i=i: e.matmul(self.ps[:, bank, i * 128:(i + 1) * 128], lhsT=src_tok[:, i * 128:(i + 1) * 128],
                                       rhs=X[:], start=True, stop=True) for i in range(ncol_blocks)]
        tr.grp("pe", ems, R=[sb, self.cst], W=[self.pb[bank]])
        for i in range(ncol_blocks):
            tr.op("act", lambda e, i=i: e.activation(out=dst_fn(i), in_=self.ps[:, bank, i * 128:(i + 1) * 128],
                                                     func=AF.Copy), R=[self.pb[bank]], W=[db])

    def load_norm(self, xsrc, t0, n, xt, xb, hT, hb, sub, c, work, wb):
        self.tr.dma("sp", xt[:, :, 0:n], xsrc.rearrange("(kc p) t -> p kc t", p=128)[:, :, t0:t0 + n], R=[self.xsb[t0 // 512]], W=[xb])
        self.norm_tile(xt, xb, n, hT, hb, lambda kc: self.Amod[:, sub, c, kc:kc + 1],
                       lambda kc: self.shift_ap(sub, c, kc), work, wb)

    def s5_prep(self, l, d, S):
        tr, di = self.tr, self.din
        pb = S["pbuf"]
        abd = S["abd"]
        tr.dma("sp", abd[:], di["ssm_abd"][l, d], W=[pb])
        tr.wait_fence("pool")
        tr.dma("pool", S["BT"][:].rearrange("p (c n) -> p c n", c=2), di["ssm_BT"][l, d].rearrange("c p n -> p c n"), W=[pb])
        sm = S["sm"]
        V = lambda i: sm[:, i, :]
        a_re, a_im, ldt = abd[:, 0, :], abd[:, 1, :], abd[:, 2, :]
        dt, lr, th, r, cth, sth, kre, kim = V(0), V(1), V(2), V(3), V(4), V(5), V(6), V(7)
        t1, t2, t3, den = V(8), V(9), V(10), V(11)
        c128, s128, kcl, ksl = V(12), V(13), V(14), V(15)
        thr = V(16)
        O = lambda f, **kw: tr.op("dve", f, R=[pb], W=[pb], **kw)
        Aop = lambda f: tr.op("act", f, R=[pb, self.cst], W=[pb])
        TWO_PI = 2.0 * np.pi
        MAGIC = 12582912.0

        def sincos(th_ap, s_out, c_out, shape_tmp1, shape_tmp2):
            O(lambda e: e.tensor_scalar(out=shape_tmp1, in0=th_ap, scalar1=1.0 / TWO_PI, scalar2=MAGIC, op0=ALU.mult, op1=ALU.add))
            O(lambda e: e.tensor_scalar(out=shape_tmp1, in0=shape_tmp1, scalar1=MAGIC, scalar2=-TWO_PI, op0=ALU.subtract, op1=ALU.mult))
            O(lambda e: e.tensor_tensor(out=shape_tmp1, in0=shape_tmp1, in1=th_ap, op=ALU.add))
            O(lambda e: e.tensor_scalar(out=shape_tmp1, in0=shape_tmp1, scalar1=PI_SAFE, scalar2=-PI_SAFE, op0=ALU.min, op1=ALU.max))
            Aop(lambda e: e.activation(out=s_out, in_=shape_tmp1, func=AF.Sin))
            O(lambda e: e.tensor_scalar(out=shape_tmp2, in0=shape_tmp1, scalar1=np.pi / 2, scalar2=None, op0=ALU.add))
            O(lambda e: e.tensor_scalar(out=shape_tmp1, in0=shape_tmp2, scalar1=np.pi, scalar2=-TWO_PI, op0=ALU.is_gt, op1=ALU.mult))
            O(lambda e: e.tensor_tensor(out=shape_tmp2, in0=shape_tmp2, in1=shape_tmp1, op=ALU.add))
            O(lambda e: e.tensor_scalar(out=shape_tmp2, in0=shape_tmp2, scalar1=PI_SAFE, scalar2=-PI_SAFE, op0=ALU.min, op1=ALU.max))
            Aop(lambda e: e.activation(out=c_out, in_=shape_tmp2, func=AF.Sin))

        Aop(lambda e: e.activation(out=dt, in_=ldt, func=AF.Exp))
        O(lambda e: e.tensor_tensor(out=lr, in0=a_re, in1=dt, op=ALU.mult))
        O(lambda e: e.tensor_tensor(out=th, in0=a_im, in1=dt, op=ALU.mult))
        Aop(lambda e: e.activation(out=r, in_=lr, func=AF.Exp))
        sincos(th, sth, cth, t1, t2)
        O(lambda e: e.tensor_tensor(out=t1, in0=r, in1=cth, op=ALU.mult))
        O(lambda e: e.tensor_scalar(out=t1, in0=t1, scalar1=-1.0, scalar2=None, op0=ALU.add))
        O(lambda e: e.tensor_tensor(out=t2, in0=r, in1=sth, op=ALU.mult))
        O(lambda e: e.tensor_tensor(out=den, in0=a_re, in1=a_re, op=ALU.mult))
        O(lambda e: e.tensor_tensor(out=t3, in0=a_im, in1=a_im, op=ALU.mult))
        O(lambda e: e.tensor_tensor(out=den, in0=den, in1=t3, op=ALU.add))
        O(lambda e: e.reciprocal(out=den, in_=den))
        O(lambda e: e.tensor_tensor(out=kre, in0=t1, in1=a_re, op=ALU.mult))
        O(lambda e: e.tensor_tensor(out=t3, in0=t2, in1=a_im, op=ALU.mult))
        O(lambda e: e.tensor_tensor(out=kre, in0=kre, in1=t3, op=ALU.add))
        O(lambda e: e.tensor_tensor(out=kre, in0=kre, in1=den, op=ALU.mult))
        O(lambda e: e.tensor_tensor(out=kim, in0=t2, in1=a_re, op=ALU.mult))
        O(lambda e: e.tensor_tensor(out=t3, in0=t1, in1=a_im, op=ALU.mult))
        O(lambda e: e.tensor_tensor(out=kim, in0=kim, in1=t3, op=ALU.subtract))
        O(lambda e: e.tensor_tensor(out=kim, in0=kim, in1=den, op=ALU.mult))
        cosT, sinT, rtab = S["cosT"], S["sinT"], S["rtab"]
        tA, tB = S["tA"], S["tB"]
        for j in range(16):
            O(lambda e, j=j: e.tensor_scalar(out=tB[:, j * 128:(j + 1) * 128], in0=self.tpos[:], scalar1=th[:, j:j + 1],
                                             scalar2=None, op0=ALU.mult))
            O(lambda e, j=j: e.tensor_scalar(out=rtab[:, j * 128:(j + 1) * 128], in0=self.lmask[:, 0:128], scalar1=0.0,
                                             scalar2=r[:, j:j + 1], op0=ALU.mult, op1=ALU.add))
            O(lambda e, j=j: e.memset(rtab[:, j * 128:j * 128 + 1], 0.0))
        sincos(tB[:], sinT[:], cosT[:], tA[:], tB[:])
        O(lambda e: e.tensor_scalar(out=thr, in0=th, scalar1=128.0, scalar2=None, op0=ALU.mult))
        sincos(thr, s128, c128, t1, t2)
        O(lambda e: e.tensor_copy(out=kcl, in_=cosT[:].rearrange("p (j t) -> p j t", t=128)[:, :, 127]))
        O(lambda e: e.tensor_copy(out=ksl, in_=sinT[:].rearrange("p (j t) -> p j t", t=128)[:, :, 127]))
        O(lambda e: e.tensor_tensor(out=t1, in0=kre, in1=kcl, op=ALU.mult))
        O(lambda e: e.tensor_tensor(out=t2, in0=kim, in1=ksl, op=ALU.mult))
        O(lambda e: e.tensor_tensor(out=t3, in0=kre, in1=ksl, op=ALU.mult))
        O(lambda e: e.tensor_tensor(out=ksl, in0=kim, in1=kcl, op=ALU.mult))
        O(lambda e: e.tensor_tensor(out=kcl, in0=t1, in1=t2, op=ALU.subtract))
        O(lambda e: e.tensor_tensor(out=ksl, in0=ksl, in1=t3, op=ALU.add))
        nkim = V(17)
        O(lambda e: e.tensor_scalar(out=nkim, in0=kim, scalar1=-1.0, scalar2=None, op0=ALU.mult))
        CT32, CTb = S["CT32"], S["CTb"]
        tr.dma("sp", CT32[:].rearrange("p (c n) -> p c n", c=2), di["ssm_CT"][l, d].rearrange("c p n -> p c n"), R=[pb], W=[pb])
        for j in range(16):
            cre = CT32[:, j * 128:(j + 1) * 128]
            cim = CT32[:, 2048 + j * 128:2048 + (j + 1) * 128]
            u1, u2 = S["u12"][:, 0:128], S["u12"][:, 128:256]
            O(lambda e, j=j, cim=cim: e.tensor_scalar(out=u1, in0=cim, scalar1=kim[:, j:j + 1], scalar2=None, op0=ALU.mult))
            O(lambda e, j=j, cre=cre: e.scalar_tensor_tensor(out=CTb[:, j * 128:(j + 1) * 128], in0=cre, scalar=kre[:, j:j + 1],
                                                             in1=u1, op0=ALU.mult, op1=ALU.subtract))
            O(lambda e, j=j, cim=cim: e.tensor_scalar(out=u2, in0=cim, scalar1=kre[:, j:j + 1], scalar2=-1.0, op0=ALU.mult, op1=ALU.mult))
            O(lambda e, j=j, cre=cre: e.scalar_tensor_tensor(out=CTb[:, 2048 + j * 128:2048 + (j + 1) * 128], in0=cre,
                                                             scalar=nkim[:, j:j + 1], in1=u2, op0=ALU.mult, op1=ALU.add))
            O(lambda e, j=j: e.tensor_scalar(out=CTb[:, 4096 + j * 128:4096 + (j + 1) * 128], in0=CTb[:, j * 128:(j + 1) * 128],
                                             scalar1=-1.0, scalar2=None, op0=ALU.mult))

    def uniq(self):
        self._uid = getattr(self, "_uid", 0) + 1
        return self._uid

    def alloc(self, es, name, shape, dt):
        return es.enter_context(self.nc.sbuf_tensor(f"{name}_{self.uniq()}", list(shape), dt))

    def proj_phase(self, l, G, cols, consume, pre=None):
        nc, tr = self.nc, self.tr
        T = G["nseq"] * G["L"]
        with ExitStack() as es:
            xt = self.alloc(es, "xtm", [128, KC, 512], F32)
            hT = self.alloc(es, "hTm", [128, KC, 512], BF16)
            work = self.alloc(es, "wkm", [128, 3, 512], F32)
            xb, hb = Buf(), Buf()
            wb = [Buf() for _ in range(4)]
            if pre is not None:
                pre()
            bi = 0
            for ti in range(T // 512):
                self.load_norm(self.xs, G["t0"] + ti * 512, 512, xt, xb, hT, hb, 1, G["c"], work, wb)
                for cb, c0 in enumerate(cols):
                    slots = self.w_in_slots(l, c0)
                    for tt in range(4):
                        self.proj_tok(slots, hT, hb, tt * 128, tt)
                    for tt in range(4):
                        consume(cb, ti * 4 + tt, tt)
            tr.barrier()

    def rope_tok(self, x_ap, xbuf, nh, ttg, tmp_ap, tbuf):
        tr = self.tr
        cs = self.ropec[:, ttg, :]
        sn = self.ropes[:, ttg, :]
        for h in range(nh):
            xh = x_ap[:, h * 128:(h + 1) * 128]
            xv = xh.rearrange("p (a b f) -> p a b f", a=2, b=2)
            tv = tmp_ap[:, 0:128].rearrange("p (a b f) -> p a b f", a=2, b=2)
            sv = sn.rearrange("p (a b f) -> p a b f", a=2, b=2)
            tr.op("dve", lambda e, xv=xv, tv=tv, sv=sv: e.tensor_tensor(out=tv[:, :, 0, :], in0=xv[:, :, 1, :], in1=sv[:, :, 0, :], op=ALU.mult),
                  R=[xbuf, self.cst], W=[tbuf])
            tr.op("dve", lambda e, xv=xv, tv=tv, sv=sv: e.tensor_tensor(out=tv[:, :, 1, :], in0=xv[:, :, 0, :], in1=sv[:, :, 1, :], op=ALU.mult),
                  R=[xbuf, self.cst], W=[tbuf])
            tr.op("dve", lambda e, xh=xh: e.tensor_tensor(out=xh, in0=xh, in1=cs, op=ALU.mult), R=[self.cst, tbuf], W=[xbuf])
            tr.op("dve", lambda e, xh=xh: e.tensor_tensor(out=xh, in0=xh, in1=tmp_ap[:, 0:128], op=ALU.add), R=[tbuf], W=[xbuf])

    def mix_s5(self, l, G, mixT, mb):
        nc, tr, di = self.nc, self.tr, self.din
        nseq, L = G["nseq"], G["L"]
        T = nseq * L
        NTT, nch = T // 128, L // 128
        with ExitStack() as es0:
            u_tok = self.alloc(es0, "utok", [128, NTT, 512], BF16)
            ub = Buf()
            self.proj_phase(l, G, [1536], lambda cb, ttg, bank: tr.op(
                "act", lambda e: e.activation(out=u_tok[:, ttg, :], in_=self.ps[:, bank, :], func=AF.Copy),
                R=[self.pb[bank]], W=[ub]))
            with ExitStack() as es:
                A = lambda n, sh, dt: self.alloc(es, n, sh, dt)
                cosT, sinT, rtab = A("cosT", [128, 2048], F32), A("sinT", [128, 2048], F32), A("rtab", [128, 2048], F32)
                W5 = A("W5", [128, 5120], F32)
                BT, CTb = A("BT", [128, 4096], BF16), A("CTb", [128, 6144], BF16)
                _xa, _xb, _xc, _xd = A("xpa", [128, 1024], BF16), A("xpb", [128, 1024], BF16), A("xpc", [128, 1024], BF16), A("xpd", [128, 1024], BF16)
                xre2, xim2, xre_b, xim_b = [_xa, _xa], [_xb, _xb], [_xc, _xc], [_xd, _xd]
                _xbuf = Buf()
                xbf2 = [_xbuf, _xbuf]
                uT2 = [A("uT", [128, 512], BF16), A("uT", [128, 512], BF16)]
                uTb2 = [Buf(), Buf()]
                ybf = Buf()
                ytokf = A("ytokf", [128, NTT, 512], BF16)
                ytb = A("ytb", [128, 512], BF16)
                gT = A("gT", [128, 4, T], BF16)
                gtok = A("gtok", [128, 512], BF16)
                abd, sm = A("abd", [128, 3, 16], F32), A("sm", [128, 24, 16], F32)
                car = A("car", [128, 4, 16], F32)
                hfin = A("hfin", [128, 16, 2], F32)
                drep = A("drep", [128, 512], F32)
                h0t = A("h0t", [128, 2, 16], F32)
                pbuf, wbf, xbf, uTb, yfb, ytbb, gTb, gtb, carb, hfb, drb = [Buf() for _ in range(11)]
                tr.dma("sp", drep[:], di["ssm_d_rep"][l], W=[drb])
                S = dict(pbuf=pbuf, abd=abd, sm=sm, BT=BT, CTb=CTb, cosT=cosT, sinT=sinT, rtab=rtab,
                         tA=W5[:, 0:2048], tB=W5[:, 2048:4096], CT32=W5[:, 0:4096], u12=W5[:, 4096:4352])
                V = lambda i: sm[:, i, :]
                r_, cth, sth, kre, kim = V(3), V(4), V(5), V(6), V(7)
                c128, s128, krl_re, krl_im = V(12), V(13), V(14), V(15)
                D_re, D_im, W_re, W_im, T1 = [W5[:, i * 1024:(i + 1) * 1024] for i in range(5)]
                rcr, rci = car[:, 0, :], car[:, 1, :]
                YB = T1
                O = lambda f, R, W: tr.op("dve", f, R=R, W=W)
                TT = lambda e, o, a, b, op: e.tensor_tensor(out=o, in0=a, in1=b, op=op)
                for d in range(2):
                    tr.barrier()
                    self.s5_prep(l, d, S)
                    for s in range(nseq):
                        if G["sample"]:
                            tr.dma("sp", h0t[:], di["h0s"][l, d], W=[carb])
                            q1, q2, q3, q4 = V(18), V(19), V(20), V(21)
                            h_re, h_im = h0t[:, 0, :], h0t[:, 1, :]
                            R_, W_ = [pbuf, carb], [pbuf, carb]
                            O(lambda e: TT(e, q1, kre, kre, ALU.mult), R_, W_)
                            O(lambda e: TT(e, q2, kim, kim, ALU.mult), R_, W_)
                            O(lambda e: TT(e, q1, q1, q2, ALU.add), R_, W_)
                            O(lambda e: e.reciprocal(out=q1, in_=q1), R_, W_)
                            O(lambda e: TT(e, q2, h_re, kre, ALU.mult), R_, W_)
                            O(lambda e: TT(e, q3, h_im, kim, ALU.mult), R_, W_)
                            O(lambda e: TT(e, q2, q2, q3, ALU.add), R_, W_)
                            O(lambda e: TT(e, q2, q2, q1, ALU.mult), R_, W_)
                            O(lambda e: TT(e, q3, h_im, kre, ALU.mult), R_, W_)
                            O(lambda e: TT(e, q4, h_re, kim, ALU.mult), R_, W_)
                            O(lambda e: TT(e, q3, q3, q4, ALU.subtract), R_, W_)
                            O(lambda e: TT(e, q3, q3, q1, ALU.mult), R_, W_)
                            O(lambda e: TT(e, q1, r_, cth, ALU.mult), R_, W_)
                            O(lambda e: TT(e, q4, r_, sth, ALU.mult), R_, W_)
                            O(lambda e: TT(e, rcr, q1, q2, ALU.mult), R_, W_)
                            O(lambda e: TT(e, rci, q4, q3, ALU.mult), R_, W_)
                            O(lambda e: TT(e, rcr, rcr, rci, ALU.subtract), R_, W_)
                            O(lambda e: TT(e, rci, q1, q3, ALU.mult), R_, W_)
                            O(lambda e: TT(e, q1, q4, q2, ALU.mult), R_, W_)
                            O(lambda e: TT(e, rci, rci, q1, ALU.add), R_, W_)
                        else:
                            O(lambda e: e.memset(car[:, 0:2, :], 0.0), [], [carb])
                        order = list(range(nch)) if d == 0 else list(range(nch - 1, -1, -1))
                        items = [(oi, ci, hh) for oi, ci in enumerate(order) for hh in range(2)]
                        v3 = lambda ap: ap.rearrange("p (a b) -> p a b", b=512)
                        j3 = lambda ap: ap.rearrange("p (j t) -> p j t", t=128)

                        def stage_bu(item):
                            oi, ci, hh = item
                            ttg = s * nch + ci
                            uTc = uT2[oi % 2]
                            if hh == 0:
                                self.transpose_to(u_tok[:, ttg, :], ub, 4, lambda i: uTc[:, i * 128:(i + 1) * 128], uTb2[oi % 2], 0, rev=(d == 1))
                            ems = []
                            for c2 in range(2):
                                for jj in range(8):
                                    j = 8 * hh + jj
                                    ems.append(lambda e, c2=c2, jj=jj, j=j: e.matmul(
                                        self.ps[:, 1 + 2 * c2 + jj // 4, (jj % 4) * 128:(jj % 4 + 1) * 128],
                                        lhsT=BT[:, c2 * 2048 + j * 128:c2 * 2048 + (j + 1) * 128],
                                        rhs=uTc[:, (j // 4) * 128:(j // 4 + 1) * 128], start=True, stop=True))
                            tr.grp("pe", ems, R=[pbuf, uTb2[oi % 2]], W=[self.pb[1], self.pb[2], self.pb[3], self.pb[4]])

                        def stage_derot(item):
                            oi, ci, hh = item
                            bre, bim = self.ps[:, 1:3, :], self.ps[:, 3:5, :]
                            cs = v3(cosT[:, hh * 1024:(hh + 1) * 1024])
                            sn = v3(sinT[:, hh * 1024:(hh + 1) * 1024])
                            Rp = [self.pb[1], self.pb[2], self.pb[3], self.pb[4], pbuf, wbf]
                            O(lambda e: TT(e, v3(T1), bre, cs, ALU.mult), Rp, [wbf])
                            O(lambda e: TT(e, v3(D_re), bim, sn, ALU.mult), Rp, [wbf])
                            O(lambda e: TT(e, v3(D_im), bre, sn, ALU.mult), Rp, [wbf])
                            O(lambda e: TT(e, v3(W_im), bim, cs, ALU.mult), Rp, [wbf])
                            O(lambda e: TT(e, D_re, D_re, T1, ALU.add), [wbf], [wbf])
                            O(lambda e: TT(e, D_im, W_im, D_im, ALU.subtract), [wbf], [wbf])
                            O(lambda e: TT(e, j3(D_re)[:, :, 0], j3(D_re)[:, :, 0], rcr[:, 8 * hh:8 * hh + 8], ALU.add), [wbf, carb], [wbf])
                            O(lambda e: TT(e, j3(D_im)[:, :, 0], j3(D_im)[:, :, 0], rci[:, 8 * hh:8 * hh + 8], ALU.add), [wbf, carb], [wbf])

                        def stage_rest(item):
                            oi, ci, hh = item
                            xre, xim, xbf = xre2[hh], xim2[hh], xbf2[hh]
                            rt = rtab[:, hh * 1024:(hh + 1) * 1024]
                            O(lambda e: e.tensor_tensor_scan(out=W_re, data0=rt, data1=D_re, initial=0.0, op0=ALU.mult, op1=ALU.add), [wbf, pbuf], [wbf])
                            O(lambda e: e.tensor_tensor_scan(out=W_im, data0=rt, data1=D_im, initial=0.0, op0=ALU.mult, op1=ALU.add), [wbf, pbuf], [wbf])
                            lr_, li_ = j3(W_re)[:, :, 127], j3(W_im)[:, :, 127]
                            hs_ = slice(8 * hh, 8 * hh + 8)
                            a1, a2 = car[:, 2, hs_], car[:, 3, hs_]
                            Rc, Wc = [wbf, carb, pbuf], [carb]
                            if oi == nch - 1 and not G["sample"]:
                                hv = hfin[:, hs_, :]
                                O(lambda e: TT(e, a1, lr_, krl_re[:, hs_], ALU.mult), Rc, Wc)
                                O(lambda e: TT(e, a2, li_, krl_im[:, hs_], ALU.mult), Rc, Wc)
                                O(lambda e: TT(e, hv[:, :, 0], a1, a2, ALU.subtract), Rc, [hfb])
                                O(lambda e: TT(e, a1, li_, krl_re[:, hs_], ALU.mult), Rc, Wc)
                                O(lambda e: TT(e, a2, lr_, krl_im[:, hs_], ALU.mult), Rc, Wc)
                                O(lambda e: TT(e, hv[:, :, 1], a1, a2, ALU.add), Rc, [hfb])
                            if oi < nch - 1:
                                O(lambda e: TT(e, a1, lr_, c128[:, hs_], ALU.mult), Rc, Wc)
                                O(lambda e: TT(e, a2, li_, s128[:, hs_], ALU.mult), Rc, Wc)
                                O(lambda e: TT(e, rcr[:, hs_], a1, a2, ALU.subtract), Rc, Wc)
                                O(lambda e: TT(e, a1, lr_, s128[:, hs_], ALU.mult), Rc, Wc)
                                O(lambda e: TT(e, a2, li_, c128[:, hs_], ALU.mult), Rc, Wc)
                                O(lambda e: TT(e, rci[:, hs_], a1, a2, ALU.add), Rc, Wc)
                                O(lambda e: TT(e, rcr[:, hs_], rcr[:, hs_], r_[:, hs_], ALU.mult), Rc, Wc)
                                O(lambda e: TT(e, rci[:, hs_], rci[:, hs_], r_[:, hs_], ALU.mult), Rc, Wc)
                            csf, snf = cosT[:, hh * 1024:(hh + 1) * 1024], sinT[:, hh * 1024:(hh + 1) * 1024]
                            pA, pB = xre[:], xim[:]
                            pC, pD = xre_b[hh][:], xim_b[hh][:]
                            O(lambda e: TT(e, pA, W_re, csf, ALU.mult), [wbf, pbuf], [xbf])
                            O(lambda e: TT(e, pB, W_im, snf, ALU.mult), [wbf, pbuf], [xbf])
                            O(lambda e: TT(e, pC, W_re, snf, ALU.mult), [wbf, pbuf], [xbf])
                            O(lambda e: TT(e, pD, W_im, csf, ALU.mult), [wbf, pbuf], [xbf])
                            ems = []
                            for cc in range(2):
                                c = 2 * hh + cc
                                for q in range(4):
                                    jj = 4 * cc + q
                                    j = 8 * hh + jj
                                    for n_, (tb, xs_) in enumerate(((0, xre), (2, xim), (1, xre_b[hh]), (1, xim_b[hh]))):
                                        ems.append(lambda e, c=c, q=q, jj=jj, j=j, tb=tb, xs_=xs_, n_=n_: e.matmul(
                                            self.ps[:, 5, c * 128:(c + 1) * 128], lhsT=xs_[:, jj * 128:(jj + 1) * 128],
                                            rhs=CTb[:, tb * 2048 + j * 128:tb * 2048 + (j + 1) * 128],
                                            start=(q == 0 and n_ == 0), stop=(q == 3 and n_ == 3)))
                            tr.grp("pe", ems, R=[xbf, pbuf], W=[self.pb[5]])
                            if hh == 1:
                                ttg = s * nch + ci
                                if d == 0:
                                    tr.op("act", lambda e: e.activation(out=ytokf[:, ttg, :], in_=self.ps[:, 5, :], func=AF.Copy),
                                          R=[self.pb[5]], W=[yfb])
                                else:
                                    tr.op("act", lambda e: e.activation(out=ytb[:], in_=self.ps[:, 5, :], func=AF.Copy), R=[self.pb[5]], W=[ytbb])
                                    tr.grp("pe", [lambda e: e.matmul(self.ps[:, 6, :], lhsT=self.ident[:], rhs=ytokf[:, ttg, :], start=True, stop=False),
                                                  lambda e: e.matmul(self.ps[:, 6, :], lhsT=self.Jm[:], rhs=ytb[:], start=False, stop=True)],
                                           R=[yfb, ytbb, self.cst], W=[self.pb[6]])

                        def stage_final(item):
                            oi, ci, hh = item
                            if hh != 1 or d == 0:
                                return
                            ttg = s * nch + ci
                            Y, Y2 = YB[:, 0:512], YB[:, 512:1024]
                            O(lambda e: TT(e, Y, u_tok[:, ttg, :], drep[:], ALU.mult), [ub, drb, ybf], [ybf])
                            O(lambda e: TT(e, Y, Y, self.ps[:, 6, :], ALU.add), [ybf, self.pb[6]], [ybf])
                            O(lambda e: TT(e, Y2, Y, Y, ALU.mult), [ybf], [ybf])
                            O(lambda e: e.tensor_scalar(out=Y2, in0=Y2, scalar1=0.044715, scalar2=1.0, op0=ALU.mult, op1=ALU.add), [ybf], [ybf])
                            O(lambda e: TT(e, Y2, Y2, Y, ALU.mult), [ybf], [ybf])
                            tr.op("act", lambda e: e.activation(out=Y2, in_=Y2, func=AF.Sigmoid, scale=1.5957691216057308), R=[ybf], W=[ybf])
                            O(lambda e: TT(e, gtok[:], Y, Y2, ALU.mult), [ybf], [gtb])
                            self.transpose_to(gtok[:], gtb, 4, lambda i: gT[:, i, ttg * 128:(ttg + 1) * 128], gTb, 7)

                        stage_bu(items[0])
                        pend = None
                        for k, item in enumerate(items):
                            stage_derot(item)
                            if k + 1 < len(items):
                                stage_bu(items[k + 1])
                            if pend is not None:
                                stage_final(pend)
                                pend = None
                            stage_rest(item)
                            pend = item
                        stage_final(pend)
                        if not G["sample"]:
                            tr.dma("sp", self.dout["hs"][l, s, d], hfin[:].rearrange("p j c -> p (j c)"), R=[hfb], is_out=True)
                wg = di["w_glu"][l].rearrange("(kc p) n -> p kc n", p=128)
                gs = [self.wslot(wg[:, :, h * 512:(h + 1) * 512], 4, 512) for h in range(2)]
                for t0 in range(0, T, 512):
                    for oc in range(4):
                        for half, bank in ((0, oc % 2), (1, 2 + oc % 2)):
                            ems = [lambda e, kc=kc, half=half, bank=bank, oc=oc: e.matmul(
                                self.ps[:, bank, :], lhsT=self.wap(gs[half], kc, 512, oc * 128, (oc + 1) * 128),
                                rhs=gT[:, kc, t0:t0 + 512], start=(kc == 0), stop=(kc == 3)) for kc in range(4)]
                            tr.grp("pe", ems, R=[self.rbuf[gs[half]], gTb], W=[self.pb[bank]])
                        sg = W5[:, 0:512]
                        tr.op("act", lambda e, oc=oc: e.activation(out=sg, in_=self.ps[:, 2 + oc % 2, :], func=AF.Sigmoid), R=[self.pb[2 + oc % 2], wbf], W=[wbf])
                        O(lambda e, oc=oc: TT(e, mixT[:, 8 + oc, t0:t0 + 512], self.ps[:, oc % 2, :], sg, ALU.mult), [self.pb[oc % 2], wbf], [mb])
                tr.barrier()

    def headnorm(self, bank, nh, gtab, out_ap, obuf, sqt, ssm_, sbuf_):
        tr = self.tr
        tr.op("act", lambda e: e.activation(out=sqt[:, 0:nh * 128], in_=self.ps[:, bank, 0:nh * 128], func=AF.Square),
              R=[self.pb[bank]], W=[sbuf_])
        tr.op("dve", lambda e: e.tensor_reduce(out=ssm_[:, 0:nh], in_=sqt[:, 0:nh * 128].rearrange("p (h d) -> p h d", d=128),
                                               axis=AX.X, op=ALU.add), R=[sbuf_], W=[sbuf_])
        tr.op("act", lambda e: e.activation(out=ssm_[:, 0:nh], in_=ssm_[:, 0:nh], func=AF.Sqrt, scale=1.0 / 128, bias=self.epsb[:, 0:1]),
              R=[sbuf_, self.cst], W=[sbuf_])
        tr.op("dve", lambda e: e.reciprocal(out=ssm_[:, 0:nh], in_=ssm_[:, 0:nh]), R=[sbuf_], W=[sbuf_])
        for h in range(nh):
            tr.op("dve", lambda e, h=h: e.scalar_tensor_tensor(out=out_ap[:, h * 128:(h + 1) * 128], in0=self.ps[:, bank, h * 128:(h + 1) * 128],
                                                               scalar=ssm_[:, h:h + 1], in1=gtab[:], op0=ALU.mult, op1=ALU.mult),
                  R=[self.pb[bank], sbuf_, self.modb], W=[obuf])

    def mix_attn(self, l, G, mixT, mb):
        nc, tr, di = self.nc, self.tr, self.din
        nseq, L, smp = G["nseq"], G["L"], G["sample"]
        T = nseq * L
        NTT = T // 128
        Lk = L + (PAST if smp else 0)
        koff = PAST if smp else 0
        with ExitStack() as es0:
            A0 = lambda n, sh, dt: self.alloc(es0, n, sh, dt)
            qT = A0("qT", [128, 8, T], BF16)
            kT = A0("kT", [128, 2, nseq * Lk], BF16)
            v_tok = A0("vtok", [128, nseq * Lk // 128, 256], BF16)
            qTb, kTb, vb = Buf(), Buf(), Buf()
            with ExitStack() as es:
                A = lambda n, sh, dt: self.alloc(es, n, sh, dt)
                SETS = []
                for _ in range(2):
                    SETS.append(dict(q_tok=A("qtok", [128, 512], BF16), kst=A("kst", [128, 512], F32), kbf=A("kbf", [128, 256], BF16),
                                     sqt=A("sqt", [128, 512], F32), ssm_=A("ssms", [128, 8], F32), rtmp=A("rtmp", [128, 128], F32),
                                     qtb=Buf(), kstb=Buf(), kbb=Buf(), sqb=Buf(), rtb=Buf()))
                kst, kbf, kstb, kbb = SETS[0]["kst"], SETS[0]["kbf"], SETS[0]["kstb"], SETS[0]["kbb"]
                cnt = [0]

                def pre():
                    if not smp:
                        return
                    for i in range(PAST // 128):
                        tr.dma("sp", kst[:, 0:256], di["cache_k"][l, i * 128:(i + 1) * 128, :], W=[kstb])
                        tr.dma("sp", kst[:, 256:512], di["cache_v"][l, i * 128:(i + 1) * 128, :], W=[kstb])
                        tr.op("act", lambda e: e.activation(out=kbf[:], in_=kst[:, 0:256], func=AF.Copy), R=[kstb], W=[kbb])
                        tr.op("dve", lambda e, i=i: e.tensor_copy(out=v_tok[:, i, :], in_=kst[:, 256:512]), R=[kstb], W=[vb])
                        self.transpose_to(kbf[:], kbb, 2, lambda h, i=i: kT[:, h, i * 128:(i + 1) * 128], kTb, 5)

                def consume(cb, ttg, bank):
                    s_, tl = divmod(ttg * 128, L)
                    par = cnt[0] % 2
                    cnt[0] += 1
                    Z = SETS[par]
                    q_tok, kst, kbf, sqt, ssm_, rtmp = Z["q_tok"], Z["kst"], Z["kbf"], Z["sqt"], Z["ssm_"], Z["rtmp"]
                    qtb, kstb, kbb, sqb, rtb = Z["qtb"], Z["kstb"], Z["kbb"], Z["sqb"], Z["rtb"]
                    if cb < 2:
                        self.headnorm(bank, 4, self.qg, q_tok[:], qtb, sqt, ssm_, sqb)
                        if smp:
                            self.rope_tok(q_tok[:], qtb, 4, ttg, rtmp[:], rtb)
                        self.transpose_to(q_tok[:], qtb, 4, lambda h: qT[:, cb * 4 + h, ttg * 128:(ttg + 1) * 128], qTb, 4 + par)
                    else:
                        self.headnorm(bank, 2, self.kg, kst[:, 0:256], kstb, sqt, ssm_, sqb)
                        tr.op("act", lambda e: e.activation(out=kst[:, 256:512], in_=self.ps[:, bank, 256:512], func=AF.Copy),
                              R=[self.pb[bank]], W=[kstb])
                        if smp:
                            self.rope_tok(kst[:, 0:256], kstb, 2, ttg, rtmp[:], rtb)
                        else:
                            tr.dma("sp", self.dout["ck"][l, ttg * 128:(ttg + 1) * 128, :], kst[:, 0:256], R=[kstb], is_out=True)
                            tr.dma("sp", self.dout["cv"][l, ttg * 128:(ttg + 1) * 128, :], kst[:, 256:512], R=[kstb], is_out=True)
                        tr.op("act", lambda e: e.activation(out=kbf[:], in_=kst[:, 0:256], func=AF.Copy), R=[kstb], W=[kbb])
                        kc0 = s_ * Lk + koff + tl
                        tr.op("dve", lambda e: e.tensor_copy(out=v_tok[:, kc0 // 128, :], in_=kst[:, 256:512]), R=[kstb], W=[vb])
                        self.transpose_to(kbf[:], kbb, 2, lambda h: kT[:, h, kc0:kc0 + 128], kTb, 4 + par)

                self.proj_phase(l, G, [0, 512, 1024], consume, pre=pre)
            with ExitStack() as es:
                A = lambda n, sh, dt: self.alloc(es, n, sh, dt)
                PT = A("PT", [128, 2, 512], BF16)
                rs = A("rs", [128, 512], F32)
                ptb, rsb = [Buf(), Buf()], Buf()
                NQ = min(512, L)
                it = 0
                for s in range(nseq):
                    for h in range(8):
                        kvh = h // 4
                        for q0 in range(0, L, NQ):
                            qa = qT[:, h, s * L + q0:s * L + q0 + NQ]
                            bo, bs = 2 + 2 * (it % 2), 3 + 2 * (it % 2)
                            it += 1
                            nsc = Lk // 128
                            def score(sc_):
                                k0_ = s * Lk + sc_ * 128
                                tr.op("pe", lambda e: e.matmul(self.ps[:, sc_ % 2, 0:NQ], lhsT=kT[:, kvh, k0_:k0_ + 128], rhs=qa,
                                                               start=True, stop=True), R=[kTb, qTb], W=[self.pb[sc_ % 2]])
                            score(0)
                            for sc in range(nsc):
                                k0 = s * Lk + sc * 128
                                bS = sc % 2
                                if sc + 1 < nsc:
                                    score(sc + 1)
                                tr.op("act", lambda e, bS=bS: e.activation(out=PT[:, bS, 0:NQ], in_=self.ps[:, bS, 0:NQ], func=AF.Exp,
                                                                           scale=128.0 ** -0.5), R=[self.pb[bS]], W=[ptb[bS]])
                                tr.op("pe", lambda e, k0=k0, bS=bS, sc=sc: e.matmul(self.ps[:, bo, 0:NQ], lhsT=v_tok[:, k0 // 128, kvh * 128:(kvh + 1) * 128],
                                                                                    rhs=PT[:, bS, 0:NQ], start=(sc == 0), stop=(sc == nsc - 1)),
                                      R=[vb, ptb[bS]], W=[self.pb[bo]])
                                tr.op("pe", lambda e, bS=bS, sc=sc: e.matmul(self.ps[:, bs, 0:NQ], lhsT=self.ones[:], rhs=PT[:, bS, 0:NQ],
                                                                             start=(sc == 0), stop=(sc == nsc - 1)),
                                      R=[self.cst, ptb[bS]], W=[self.pb[bs]])
                            tr.op("dve", lambda e, bs=bs: e.reciprocal(out=rs[:, 0:NQ], in_=self.ps[:, bs, 0:NQ]), R=[self.pb[bs]], W=[rsb])
                            tr.op("dve", lambda e, bo=bo, s=s, q0=q0, h=h: e.tensor_tensor(
                                out=mixT[:, h, s * L + q0:s * L + q0 + NQ], in0=self.ps[:, bo, 0:NQ], in1=rs[:, 0:NQ], op=ALU.mult),
                                  R=[self.pb[bo], rsb], W=[mb])
                tr.barrier()

    def mix_ret(self, l, G, mixT, mb):
        nc, tr, di = self.nc, self.tr, self.din
        nseq, L, smp = G["nseq"], G["L"], G["sample"]
        T = nseq * L
        NTT, nch = T // 128, L // 128
        with ExitStack() as es0:
            A0 = lambda n, sh, dt: self.alloc(es0, n, sh, dt)
            qrT, krT = A0("qrT", [128, 4, T], BF16), A0("krT", [128, 4, T], BF16)
            kr_tok, vr_tok, gs_tok = A0("krtok", [128, NTT, 512], BF16), A0("vrtok", [128, NTT, 512], BF16), A0("gstok", [128, NTT, 512], BF16)
            qrb, krb, ktb, vtb, gsb = [Buf() for _ in range(5)]
            with ExitStack() as es:
                A = lambda n, sh, dt: self.alloc(es, n, sh, dt)
                RS = []
                for _ in range(2):
                    RS.append(dict(st=A("rst", [128, 512], F32), stb_=A("rstb", [128, 512], BF16), rtmp=A("rrtmp", [128, 128], F32),
                                   s1=Buf(), s2=Buf(), rtb=Buf()))
                cnt = [0]

                def consume(cb, ttg, bank):
                    par = cnt[0] % 2
                    cnt[0] += 1
                    Z = RS[par]
                    st, stb_, rtmp, s1, s2, rtb = Z["st"], Z["stb_"], Z["rtmp"], Z["s1"], Z["s2"], Z["rtb"]
                    if cb in (0, 1):
                        sc = 1.0 if cb == 0 else 128.0 ** -0.5
                        tr.op("act", lambda e: e.activation(out=st[:], in_=self.ps[:, bank, :], func=AF.Copy, scale=sc), R=[self.pb[bank]], W=[s1])
                        if smp:
                            self.rope_tok(st[:], s1, 4, ttg, rtmp[:], rtb)
                        if cb == 0:
                            tr.op("dve", lambda e: e.tensor_copy(out=stb_[:], in_=st[:]), R=[s1], W=[s2])
                            self.transpose_to(stb_[:], s2, 4, lambda h: qrT[:, h, ttg * 128:(ttg + 1) * 128], qrb, 4 + par)
                        else:
                            tr.op("dve", lambda e: e.tensor_copy(out=kr_tok[:, ttg, :], in_=st[:]), R=[s1], W=[ktb])
                            self.transpose_to(kr_tok[:, ttg, :], ktb, 4, lambda h: krT[:, h, ttg * 128:(ttg + 1) * 128], krb, 4 + par)
                    elif cb == 2:
                        tr.op("act", lambda e: e.activation(out=vr_tok[:, ttg, :], in_=self.ps[:, bank, :], func=AF.Copy), R=[self.pb[bank]], W=[vtb])
                    else:
                        tr.op("act", lambda e: e.activation(out=gs_tok[:, ttg, :], in_=self.ps[:, bank, :], func=AF.Silu), R=[self.pb[bank]], W=[gsb])

                self.proj_phase(l, G, [2048, 2560, 3072, 3584], consume)
            with ExitStack() as es:
                A = lambda n, sh, dt: self.alloc(es, n, sh, dt)
                dl = A("dl", [128, 8], F32)
                lg = A("lg", [128, 8], F32)
                lg128 = A("lg128", [128, 8], F32)
                Mt = A("Mt", [128, 4, 128], F32)
                qd = A("qd", [128, 8, 128], F32)
                kd = A("kd", [128, 8], F32)
                cd = A("cd", [128, 8], F32)
                tmpa, tmpb = A("tmpa", [128, 128], F32), A("tmpb", [128, 128], F32)
                KV = A("KV", [128, 2, nch, 128], F32)
                Sst = A("Sst", [128, 2, nch, 128], BF16)
                Srun = A("Srun", [128, 2, 128], F32)
                attM = A("attM", [128, 128], BF16)
                kdk = A("kdk", [128, 2, 128], BF16)
                qdq = A("qdq", [128, 2, 128], BF16)
                stat = A("stat", [128, 8], F32)
                rtk = A("rtk", [128, 128], BF16)
                on = A("on", [128, 128], F32)
                tb_, kvb, ssb, srb, amb, kdb_, qdb_, stb2, rtkb, onb = [Buf() for _ in range(10)]
                O = lambda f, R, W: tr.op("dve", f, R=R, W=W)
                Aop = lambda f, R, W: tr.op("act", f, R=R, W=W)
                TT = lambda e, o, a, b, op: e.tensor_tensor(out=o, in0=a, in1=b, op=op)
                tr.dma("sp", dl[:], di["dlog_rep"][l], W=[tb_])
                Aop(lambda e: e.activation(out=lg[:], in_=dl[:], func=AF.Exp, scale=-1.0), [tb_], [tb_])
                O(lambda e: e.tensor_scalar(out=lg[:], in0=lg[:], scalar1=1.0, scalar2=None, op0=ALU.add), [tb_], [tb_])
                Aop(lambda e: e.activation(out=lg[:], in_=lg[:], func=AF.Ln), [tb_], [tb_])
                O(lambda e: e.tensor_scalar(out=lg[:], in0=lg[:], scalar1=-1.0, scalar2=None, op0=ALU.mult), [tb_], [tb_])
                O(lambda e: e.tensor_scalar(out=lg128[:], in0=lg[:], scalar1=128.0, scalar2=None, op0=ALU.mult), [tb_], [tb_])
                Aop(lambda e: e.activation(out=cd[:], in_=lg128[:], func=AF.Exp), [tb_], [tb_])
                for h in range(4):
                    f_, b_ = h, 4 + h
                    O(lambda e: e.tensor_scalar(out=tmpa[:], in0=self.diff[:], scalar1=0.0, scalar2=None, op0=ALU.max), [self.cst, tb_], [tb_])
                    Aop(lambda e, f_=f_: e.activation(out=tmpa[:], in_=tmpa[:], func=AF.Exp, scale=lg[:, f_:f_ + 1]), [tb_], [tb_])
                    O(lambda e: TT(e, tmpa[:], tmpa[:], self.lmask[:], ALU.mult), [tb_, self.cst], [tb_])
                    O(lambda e: e.tensor_scalar(out=tmpb[:], in0=self.diff[:], scalar1=-1.0, scalar2=0.0, op0=ALU.mult, op1=ALU.max), [self.cst, tb_], [tb_])
                    Aop(lambda e, b_=b_: e.activation(out=tmpb[:], in_=tmpb[:], func=AF.Exp, scale=lg[:, b_:b_ + 1]), [tb_], [tb_])
                    O(lambda e: TT(e, tmpb[:], tmpb[:], self.umask[:], ALU.mult), [tb_, self.cst], [tb_])
                    O(lambda e, h=h: TT(e, Mt[:, h, :], tmpa[:], tmpb[:], ALU.add), [tb_], [tb_])
                    O(lambda e: e.tensor_scalar(out=tmpa[:], in0=self.tpos[:], scalar1=1.0, scalar2=None, op0=ALU.add), [self.cst, tb_], [tb_])
                    Aop(lambda e, f_=f_: e.activation(out=qd[:, f_, :], in_=tmpa[:], func=AF.Exp, scale=lg[:, f_:f_ + 1]), [tb_], [tb_])
                    O(lambda e: e.tensor_scalar(out=tmpb[:], in0=self.tpos[:], scalar1=-1.0, scalar2=128.0, op0=ALU.mult, op1=ALU.add), [self.cst, tb_], [tb_])
                    Aop(lambda e, b_=b_: e.activation(out=qd[:, b_, :], in_=tmpb[:], func=AF.Exp, scale=lg[:, b_:b_ + 1]), [tb_], [tb_])
                    O(lambda e: e.tensor_scalar(out=tmpa[:, 0:1], in0=self.ipos[:], scalar1=-1.0, scalar2=127.0, op0=ALU.mult, op1=ALU.add), [self.cst, tb_], [tb_])
                    Aop(lambda e, f_=f_: e.activation(out=kd[:, f_:f_ + 1], in_=tmpa[:, 0:1], func=AF.Exp, scale=lg[:, f_:f_ + 1]), [tb_], [tb_])
                    Aop(lambda e, b_=b_: e.activation(out=kd[:, b_:b_ + 1], in_=self.ipos[:], func=AF.Exp, scale=lg[:, b_:b_ + 1]), [tb_, self.cst], [tb_])
                for s in range(nseq):
                    for h in range(4):
                        hc = slice(h * 128, (h + 1) * 128)
                        for ci in range(nch):
                            ttg = s * nch + ci
                            for d_ in range(2):
                                O(lambda e, d_=d_, ttg=ttg: e.tensor_scalar(out=kdk[:, d_, :], in0=kr_tok[:, ttg, hc], scalar1=kd[:, d_ * 4 + h:d_ * 4 + h + 1],
                                                                            scalar2=None, op0=ALU.mult), [ktb, tb_], [kdb_])
                            tr.grp("pe", [lambda e, d_=d_, ttg=ttg: e.matmul(self.ps[:, 0, d_ * 128:(d_ + 1) * 128], lhsT=kdk[:, d_, :], rhs=vr_tok[:, ttg, hc],
                                                                             start=True, stop=True) for d_ in range(2)], R=[kdb_, vtb], W=[self.pb[0]])
                            Aop(lambda e, ci=ci: e.activation(out=KV[:, :, ci, :], in_=self.ps[:, 0, 0:256].rearrange("p (a b) -> p a b", b=128), func=AF.Copy),
                                [self.pb[0]], [kvb])
                        for d_ in range(2):
                            cdc = cd[:, d_ * 4 + h:d_ * 4 + h + 1]
                            if smp:
                                tr.dma("sp", Srun[:, d_, :], di["s0s"][l, d_, h], W=[srb])
                            else:
                                O(lambda e, d_=d_: e.memset(Srun[:, d_, :], 0.0), [], [srb])
                            order = list(range(nch)) if d_ == 0 else list(range(nch - 1, -1, -1))
                            for ci in order:
                                O(lambda e, d_=d_, ci=ci: e.tensor_copy(out=Sst[:, d_, ci, :], in_=Srun[:, d_, :]), [srb], [ssb])
                                O(lambda e, d_=d_, ci=ci, cdc=cdc: e.scalar_tensor_tensor(out=Srun[:, d_, :], in0=Srun[:, d_, :], scalar=cdc, in1=KV[:, d_, ci, :],
                                                                                          op0=ALU.mult, op1=ALU.add), [srb, kvb, tb_], [srb])
                            if not smp:
                                tr.dma("sp", self.dout["ss"][l, s, d_, h], Srun[:, d_, :], R=[srb], is_out=True)
                        for ci in range(nch):
                            ttg = s * nch + ci
                            tk = slice(ttg * 128, (ttg + 1) * 128)
                            tr.op("pe", lambda e, tk=tk: e.matmul(self.ps[:, 1, 0:128], lhsT=krT[:, h, tk], rhs=qrT[:, h, tk], start=True, stop=True),
                                  R=[krb, qrb], W=[self.pb[1]])
                            O(lambda e: TT(e, attM[:], self.ps[:, 1, 0:128], Mt[:, h, :], ALU.mult), [self.pb[1], tb_], [amb])
                            for d_ in range(2):
                                O(lambda e, d_=d_, tk=tk: TT(e, qdq[:, d_, :], qrT[:, h, tk], qd[:, d_ * 4 + h, :], ALU.mult), [qrb, tb_], [qdb_])
                            tr.grp("pe", [lambda e, ttg=ttg: e.matmul(self.ps[:, 2, 0:128], lhsT=attM[:], rhs=vr_tok[:, ttg, hc], start=True, stop=False),
                                          lambda e, ci=ci: e.matmul(self.ps[:, 2, 0:128], lhsT=qdq[:, 0, :], rhs=Sst[:, 0, ci, :], start=False, stop=False),
                                          lambda e, ci=ci: e.matmul(self.ps[:, 2, 0:128], lhsT=qdq[:, 1, :], rhs=Sst[:, 1, ci, :], start=False, stop=True)],
                                   R=[amb, vtb, qdb_, ssb], W=[self.pb[2]])
                            O(lambda e: e.bn_stats(out=stat[:, 0:6], in_=self.ps[:, 2, 0:128]), [self.pb[2]], [stb2])
                            O(lambda e: e.bn_aggr(out=stat[:, 6:8], in_=stat[:, 0:6]), [stb2], [stb2])
                            Aop(lambda e: e.activation(out=stat[:, 7:8], in_=stat[:, 7:8], func=AF.Sqrt, bias=self.epsb[:, 0:1]), [stb2, self.cst], [stb2])
                            O(lambda e: e.reciprocal(out=stat[:, 7:8], in_=stat[:, 7:8]), [stb2], [stb2])
                            O(lambda e: e.tensor_scalar(out=on[:], in0=self.ps[:, 2, 0:128], scalar1=stat[:, 6:7], scalar2=stat[:, 7:8],
                                                        op0=ALU.subtract, op1=ALU.mult), [self.pb[2], stb2], [onb])
                            O(lambda e, ttg=ttg: TT(e, rtk[:], on[:], gs_tok[:, ttg, hc], ALU.mult), [onb, gsb], [rtkb])
                            self.transpose_to(rtk[:], rtkb, 1, lambda i, tk=tk: mixT[:, 12 + h, tk], mb, 3)
                tr.barrier()

    def mix_out(self, l, G, mixT, mb):
        nc, tr, di = self.nc, self.tr, self.din
        T = G["nseq"] * G["L"]
        wv = di["w_out"][l].rearrange("(kc p) n -> p kc n", p=128)
        xsv = self.xs.rearrange("(kc p) t -> p kc t", p=128)
        with ExitStack() as es:
            xt = self.alloc(es, "xto", [128, KC, 512], F32)
            xb = Buf()
            for ti in range(T // 512):
                g0 = G["t0"] + ti * 512
                tr.dma("sp", xt[:], xsv[:, :, g0:g0 + 512], R=[self.xsb[g0 // 512]], W=[xb])
                for o2 in range(8):
                    s = self.wslot(wv[:, :, o2 * 256:(o2 + 1) * 256], 16, 256)
                    for jj in range(2):
                        oc = o2 * 2 + jj
                        bk = oc % 2
                        ems = [lambda e, kc=kc, jj=jj, bk=bk, s=s: e.matmul(
                            self.ps[:, bk, :], lhsT=self.wap(s, kc, 256, jj * 128, jj * 128 + 128),
                            rhs=mixT[:, kc, ti * 512:(ti + 1) * 512], start=(kc == 0), stop=(kc == KC - 1)) for kc in range(KC)]
                        tr.grp("pe", ems, R=[self.rbuf[s], mb], W=[self.pb[bk]])
                        tr.op("dve", lambda e, oc=oc, bk=bk: e.scalar_tensor_tensor(
                            out=xt[:, oc, :], in0=self.ps[:, bk, :], scalar=self.gate[:, 1, G["c"], oc:oc + 1], in1=xt[:, oc, :],
                            op0=ALU.mult, op1=ALU.add), R=[self.pb[bk], self.modb, xb], W=[xb])
                tr.dma("sp", xsv[:, :, g0:g0 + 512], xt[:], R=[xb], W=[self.xsb[g0 // 512]])
            tr.barrier()

    def mixer(self, l, G):
        nc, tr = self.nc, self.tr
        T = G["nseq"] * G["L"]
        with ExitStack() as es:
            mixT = self.alloc(es, "mixT", [128, 16, T], BF16)
            mb = Buf()
            self.mix_s5(l, G, mixT, mb)
            self.mix_attn(l, G, mixT, mb)
            self.mix_ret(l, G, mixT, mb)
            self.mix_out(l, G, mixT, mb)
            tr.barrier()

    def ffn_stage(self, l, f, supers, last):
        nc, tr, di = self.nc, self.tr, self.din
        sub = 0 if f == 0 else 2
        xsv = self.xs.rearrange("(kc p) t -> p kc t", p=128)
        win = di["w_ffn_in"][l, f].rearrange("(kc p) n -> p kc n", p=128)
        wout = di["w_ffn_out"][l, f].rearrange("(kc p) n -> p kc n", p=128)
        BLK = [(0, 12), (12, 12), (24, 12), (36, 8)]
        with ExitStack() as es:
            xt = self.alloc(es, "xtf", [128, 2, KC, 512], F32)
            hT = self.alloc(es, "hTf", [128, 2, KC, 512], BF16)
            work = self.alloc(es, "wkf", [128, 3, 512], F32)
            hid = self.alloc(es, "hid", [128, 2, 12, 512], BF16)
            xb, hb, hidb = [Buf(), Buf()], [Buf(), Buf()], [Buf(), Buf()]
            wb = [Buf() for _ in range(4)]
            for sup in supers:
                NTs = len(sup)
                for i, (t0, c) in enumerate(sup):
                    src = di["xin"] if (l == 0 and f == 0) else self.xs
                    self.load_norm(src, t0, 512, xt[:, i], xb[i], hT[:, i], hb[i], sub, c, work, wb)
                for (k0, HB) in BLK:
                    for h2 in range(HB // 2):
                        c0 = (k0 + 2 * h2) * 128
                        sa = self.wslot(win[:, :, c0:c0 + 256], 16, 256)
                        su = self.wslot(win[:, :, DFF + c0:DFF + c0 + 256], 16, 256)
                        for jj in range(2):
                            for i in range(NTs):
                                ba, bu = 2 * i, 2 * i + 1
                                for (s, bk) in ((sa, ba), (su, bu)):
                                    ems = [lambda e, kc=kc, s=s, bk=bk, jj=jj, i=i: e.matmul(
                                        self.ps[:, bk, :], lhsT=self.wap(s, kc, 256, jj * 128, jj * 128 + 128),
                                        rhs=hT[:, i, kc, :], start=(kc == 0), stop=(kc == KC - 1)) for kc in range(KC)]
                                    tr.grp("pe", ems, R=[self.rbuf[s], hb[i]], W=[self.pb[bk]])
                                sl = work[:, i, :]
                                tr.op("act", lambda e, ba=ba, sl=sl: e.activation(out=sl, in_=self.ps[:, ba, :], func=AF.Silu),
                                      R=[self.pb[ba]], W=[wb[i]])
                                tr.op("dve", lambda e, bu=bu, sl=sl, i=i, kk=2 * h2 + jj: e.tensor_tensor(
                                    out=hid[:, i, kk, :], in0=self.ps[:, bu, :], in1=sl, op=ALU.mult),
                                      R=[self.pb[bu], wb[i]], W=[hidb[i]])
                    for o2 in range(8):
                        s = self.wslot(wout[:, k0:k0 + HB, o2 * 256:(o2 + 1) * 256], HB, 256)
                        for jj in range(2):
                            oc = o2 * 2 + jj
                            for i in range(NTs):
                                bk = 4 + 2 * (oc % 2) + i
                                ems = [lambda e, k=k, jj=jj, bk=bk, i=i, s=s: e.matmul(
                                    self.ps[:, bk, :], lhsT=self.wap(s, k, 256, jj * 128, jj * 128 + 128),
                                    rhs=hid[:, i, k, :], start=(k == 0), stop=(k == HB - 1)) for k in range(HB)]
                                tr.grp("pe", ems, R=[self.rbuf[s], hidb[i]], W=[self.pb[bk]])
                                tr.op("dve", lambda e, oc=oc, bk=bk, i=i, c=sup[i][1]: e.scalar_tensor_tensor(
                                    out=xt[:, i, oc, :], in0=self.ps[:, bk, :], scalar=self.gate[:, sub, c, oc:oc + 1],
                                    in1=xt[:, i, oc, :], op0=ALU.mult, op1=ALU.add), R=[self.pb[bk], self.modb, xb[i]], W=[xb[i]])
                for i, (t0, c) in enumerate(sup):
                    if last:
                        self.final_norm_store(xt[:, i], xb[i], t0, work, wb)
                    else:
                        tr.dma("sp", xsv[:, :, t0:t0 + 512], xt[:, i], R=[xb[i]], W=[self.xsb[t0 // 512]])
            tr.barrier()

    def build(self):
        tr, nc, di, do = self.tr, self.nc, self.din, self.dout
        self.epsb = nc.alloc_sbuf_tensor("epsb", [128, 1], F32)
        tr.op("dve", lambda e: e.memset(self.epsb[:], EPS), W=[self.cst])
        self.setup_consts()
        groups = [dict(t0=0, nseq=self.nseq_p, L=self.L_p, c=0, sample=False)]
        tiles = [(0, 0)]
        supers = [[(0, 0)]]
        if self.with_sample:
            groups.append(dict(t0=512, nseq=1, L=1024, c=1, sample=True))
            tiles += [(512, 1), (1024, 1)]
            supers.append([(512, 1), (1024, 1)])
        if not self.do_ffn:
            xsv0 = self.xs.rearrange("(kc p) t -> p kc t", p=128)
            xiv0 = self.din["xin"].rearrange("(kc p) t -> p kc t", p=128)
            with ExitStack() as es:
                xt0 = self.alloc(es, "xt0", [128, KC, 512], F32)
                xb0 = Buf()
                for (t0, c) in tiles:
                    tr.dma("sp", xt0[:], xiv0[:, :, t0:t0 + 512], W=[xb0])
                    tr.dma("sp", xsv0[:, :, t0:t0 + 512], xt0[:], R=[xb0], W=[self.xsb[t0 // 512]])
                tr.barrier()
        for l in self.layers:
            self.adaln(l)
            if self.do_ffn:
                self.ffn_stage(l, 0, supers, False)
            for G in groups:
                self.mixer(l, G)
            if self.do_ffn:
                self.ffn_stage(l, 1, supers, l == self.layers[-1])
        if not self.do_ffn:
            xsv = self.xs.rearrange("(kc p) t -> p kc t", p=128)
            yv = self.dout["y"].rearrange("(kc p) t -> p kc t", p=128)
            with ExitStack() as es:
                xt = self.alloc(es, "xtd", [128, KC, 512], F32)
                xb = Buf()
                for (t0, c) in tiles:
                    tr.dma("sp", xt[:], xsv[:, :, t0:t0 + 512], R=[self.xsb[t0 // 512]], W=[xb])
                    tr.dma("sp", yv[:, :, t0:t0 + 512], xt[:], R=[xb], is_out=True)
        tr.finish()

    def final_norm_store(self, xt, xb, t0, work, wb):
        tr, nc = self.tr, self.nc
        yv = self.dout["y"].rearrange("(kc p) t -> p kc t", p=128)
        bank = 6
        sq = work[:, 2, :].bitcast(BF16)
        for kc in range(KC):
            h = kc % 2
            tr.op("act", lambda e, kc=kc, h=h: e.activation(out=sq[:, h * 512:(h + 1) * 512], in_=xt[:, kc, :], func=AF.Square),
                  R=[xb], W=[wb[2 + h]])
            tr.op("pe", lambda e, kc=kc, h=h: e.matmul(self.ps[:, bank, :], lhsT=self.ones[:], rhs=sq[:, h * 512:(h + 1) * 512],
                                                       start=(kc == 0), stop=(kc == KC - 1)), R=[wb[2 + h], self.cst], W=[self.pb[bank]])
        rstd = work[:, 0, :]
        tr.op("act", lambda e: e.activation(out=rstd, in_=self.ps[:, bank, :], func=AF.Sqrt, scale=1.0 / D, bias=self.epsb[:, 0:1]),
              R=[self.pb[bank], self.cst], W=[wb[0]])
        tr.op("dve", lambda e: e.reciprocal(out=rstd, in_=rstd), R=[wb[0]], W=[wb[0]])
        for kc in range(KC):
            tr.op("dve", lambda e, kc=kc: e.scalar_tensor_tensor(out=xt[:, kc, :], in0=xt[:, kc, :], scalar=self.fgT[:, kc:kc + 1],
                                                                 in1=rstd, op0=ALU.mult, op1=ALU.mult),
                  R=[xb, wb[0], self.cst], W=[xb])
        tr.dma("sp", yv[:, :, t0:t0 + 512], xt[:], R=[xb], is_out=True)


def _prep_shared(inp):
    f = lambda a: np.ascontiguousarray(np.asarray(a, dtype=np.float32))
    sh = {}
    for k in ("w_mod", "w_ffn_in", "w_ffn_out", "w_in", "w_out"):
        sh[k] = f(inp[k])
    sh["w_glu"] = f(inp["w_ssm_glu"])
    sh["b_modT"] = f(np.asarray(inp["b_mod"]).reshape(DEPTH, 144, 128).transpose(0, 2, 1))
    sh["norm_gT"] = f(np.asarray(inp["norm_g"]).reshape(DEPTH, 3, KC, 128).transpose(0, 3, 1, 2).reshape(DEPTH, 128, 48))
    sh["final_gT"] = f(np.asarray(inp["final_norm_g"]).reshape(KC, 128).T)
    sh["qg_rep"] = f(np.broadcast_to(np.asarray(inp["q_norm_g"])[:, None, :], (DEPTH, 128, 128)))
    sh["kg_rep"] = f(np.broadcast_to(np.asarray(inp["k_norm_g"])[:, None, :], (DEPTH, 128, 128)))

    def chmaj(a):
        a = np.asarray(a).reshape(DEPTH, 2, 16, 2, 64)
        return a.transpose(0, 1, 3, 4, 2).reshape(DEPTH, 2, 128, 16)
    ldt = np.broadcast_to(np.asarray(inp["ssm_log_dt"])[..., None], (DEPTH, 2, 32, 64))
    sh["ssm_abd"] = f(np.stack([chmaj(inp["ssm_a_re"]), chmaj(inp["ssm_a_im"]), chmaj(ldt)], axis=3))
    BT = np.zeros((DEPTH, 2, 2, 128, 16, 128), np.float32)
    CT = np.zeros((DEPTH, 2, 2, 128, 16, 128), np.float32)
    Bs = [np.asarray(inp["ssm_b_re"]), np.asarray(inp["ssm_b_im"])]
    Cs = [np.asarray(inp["ssm_c_re"]), np.asarray(inp["ssm_c_im"])]
    for j in range(16):
        for gl in range(2):
            g = 2 * j + gl
            r0 = 32 * (j % 4) + 16 * gl
            for c in range(2):
                BT[:, :, c, r0:r0 + 16, j, gl * 64:(gl + 1) * 64] = Bs[c][:, :, g].transpose(0, 1, 3, 2)
                CT[:, :, c, gl * 64:(gl + 1) * 64, j, r0:r0 + 16] = Cs[c][:, :, g].transpose(0, 1, 3, 2)
    sh["ssm_BT"] = BT.reshape(DEPTH, 2, 2, 128, 2048)
    sh["ssm_CT"] = CT.reshape(DEPTH, 2, 2, 128, 2048)
    sh["ssm_d_rep"] = f(np.broadcast_to(np.asarray(inp["ssm_d"])[:, None, :], (DEPTH, 128, 512)))
    sh["dlog_rep"] = f(np.broadcast_to(np.asarray(inp["ret_decay_logit"]).reshape(DEPTH, 1, 8), (DEPTH, 128, 8)))
    i = np.arange(128, dtype=np.float32)
    sh["c_ident"] = np.eye(128, dtype=np.float32)
    sh["c_J"] = np.ascontiguousarray(np.eye(128, dtype=np.float32)[::-1])
    sh["c_tpos"] = f(np.broadcast_to(i[None, :], (128, 128)))
    sh["c_ipos"] = f(i[:, None])
    sh["c_diff"] = f(i[None, :] - i[:, None])
    sh["c_lmask"] = f((i[None, :] >= i[:, None]))
    sh["c_umask"] = f((i[None, :] <= i[:, None]))
    t = np.arange(1024)
    pos = np.stack([t // 64, t % 64], axis=-1).astype(np.float32)
    inv = (np.float32(10000.0) ** (-np.arange(32, dtype=np.float32) / np.float32(32))).astype(np.float32)
    ang = (pos[:, :, None] * inv[None, None, :]).astype(np.float32)
    co, si = np.cos(ang).astype(np.float32), np.sin(ang).astype(np.float32)
    cfull = np.stack([co, co], axis=2).reshape(1024, 128)
    sfull = np.stack([-si, si], axis=2).reshape(1024, 128)
    sh["rope_c"] = f(cfull.reshape(8, 128, 128).transpose(1, 0, 2))
    sh["rope_s"] = f(sfull.reshape(8, 128, 128).transpose(1, 0, 2))
    return sh


def _in_maps(inp, P):
    sh = _prep_shared(inp)
    xp = np.asarray(inp["x_prompt"], np.float32)
    xsm = np.asarray(inp["x_sample"], np.float32)
    cs = np.asarray(inp["c"], np.float32)
    cctx = np.asarray(inp["c_ctx"], np.float32)
    ck = np.asarray(inp["cache_k"], np.float32)
    cv_ = np.asarray(inp["cache_v"], np.float32)
    sts = np.asarray(inp["state_ssm"], np.float32)
    str_ = np.asarray(inp["state_ret"], np.float32)
    maps = []
    for c in range(NCORE):
        b = c % 2
        m = dict(sh)
        xs_ = [xp[2 * c:2 * c + 2].reshape(512, D)]
        if P.with_sample:
            xs_.append(xsm[b])
        m["xin"] = np.ascontiguousarray(np.concatenate(xs_, axis=0).T)
        cvv = np.stack([cctx, cs[b]], axis=-1)
        m["cvec"] = np.ascontiguousarray(cvv.reshape(KC, 128, 2).transpose(1, 0, 2))
        m["cache_k"] = np.ascontiguousarray(ck[b].reshape(DEPTH, PAST, 256))
        m["cache_v"] = np.ascontiguousarray(cv_[b].reshape(DEPTH, PAST, 256))
        h0 = sts[b].reshape(DEPTH, 2, 16, 2, 64, 2).transpose(0, 1, 3, 4, 5, 2).reshape(DEPTH, 2, 128, 2, 16)
        m["h0s"] = np.ascontiguousarray(h0)
        m["s0s"] = np.ascontiguousarray(str_[b])
        maps.append({k: v for k, v in m.items() if k in P.din})
    return maps


_PROG = None


def kernel(**inp):
    global _PROG
    if _PROG is None:
        _PROG = Prog()
    P = _PROG
    res = run_bass_kernel_spmd(P.nc, _in_maps(inp, P), core_ids=list(range(NCORE)))
    R = res.results
    f32 = np.float32
    y_prompt = np.stack([R[c]["y"][:, 0:512].T.reshape(2, 256, D) for c in range(NCORE)]).reshape(16, 256, D).astype(f32)
    y_sample = np.stack([R[b]["y"][:, 512:1536].T for b in range(2)]).astype(f32)
    def cache(name):
        a = np.stack([R[c][name].reshape(DEPTH, 2, 256, 2, 128).transpose(1, 0, 2, 3, 4) for c in range(NCORE)])
        return np.ascontiguousarray(a.reshape(16, DEPTH, 256, 2, 128).astype(f32))
    def hstate():
        out = []
        for c in range(NCORE):
            a = R[c]["hs"].reshape(DEPTH, 2, 2, 2, 64, 16, 2)
            out.append(a.transpose(1, 0, 2, 5, 3, 4, 6).reshape(2, DEPTH, 2, 32, 64, 2))
        return np.ascontiguousarray(np.concatenate(out, axis=0).astype(f32))
    def sstate():
        out = [R[c]["ss"].transpose(1, 0, 2, 3, 4, 5) for c in range(NCORE)]
        return np.ascontiguousarray(np.concatenate(out, axis=0).astype(f32))
    return (y_prompt, y_sample, cache("ck"), cache("cv"), hstate(), sstate())
```
